# Optimizing a Trainium2 kernel written in Bass

```python
import jax, jax.numpy as jnp
from jax import lax

D_MODEL = 2048
BATCH = 4
SEQ = 2048
DEPTH = 1
DEC_BATCH = 128
DEC_SEQ = 1
PAST_LEN = 16384
PAGE_SIZE = 128

D_MIX = D_MODEL
D_GDN = D_MIX // 2
D_POOL = D_MIX - D_GDN
GDN_HEAD_DIM = 128
GDN_HEADS = D_GDN // GDN_HEAD_DIM
D_QKV = 3 * D_GDN
CONV_W = 4
CHUNK = 64
POOL_WINDOWS = (2, 4, 8, 16)
POOL_GROUPS = len(POOL_WINDOWS)
POOL_GROUP_DIM = D_POOL // POOL_GROUPS
POOL_BUF = max(POOL_WINDOWS) - 1
D_FF = 4 * D_MODEL
N_IN = D_QKV + D_GDN + 2 * GDN_HEADS + D_POOL
EPS = 1e-6

kernel_name = "hymba_gdn_pool_sandwich_decode"


def _rmsnorm(x, w):
    xf = x.astype(jnp.float32)
    y = xf * lax.rsqrt(jnp.mean(xf * xf, axis=-1, keepdims=True) + EPS)
    return (y * w.astype(jnp.float32)).astype(x.dtype)


def _l2norm(x):
    return x * lax.rsqrt(jnp.sum(x * x, axis=-1, keepdims=True) + EPS)


def _gdn_chunked(q, k, v, g, beta, s0):
    B, L, H, DK = q.shape
    DV = v.shape[-1]
    nc = L // CHUNK
    to_c = lambda t: t.reshape(B, nc, CHUNK, H, -1).transpose(0, 3, 1, 2, 4)
    q, k, v = to_c(q), to_c(k), to_c(v)
    g = g.reshape(B, nc, CHUNK, H).transpose(0, 3, 1, 2)
    beta = beta.reshape(B, nc, CHUNK, H).transpose(0, 3, 1, 2)
    gc = jnp.cumsum(g, axis=-1)
    tril = jnp.tril(jnp.ones((CHUNK, CHUNK), bool))
    strict = jnp.tril(jnp.ones((CHUNK, CHUNK), bool), -1)
    decay = jnp.exp(jnp.where(tril, gc[..., :, None] - gc[..., None, :], -jnp.inf))
    kb = k * beta[..., None]
    lmat = jnp.where(strict, jnp.einsum('bhncd,bhnsd->bhncs', kb, k) * decay, 0.0)
    a_mat = lmat + jnp.eye(CHUNK, dtype=jnp.float32)
    rhs = jnp.concatenate([v * beta[..., None], kb * jnp.exp(gc)[..., None]], axis=-1)
    sol = lax.linalg.triangular_solve(a_mat, rhs, left_side=True, lower=True)
    u, w = sol[..., :DV], sol[..., DV:]
    intra = jnp.where(tril, jnp.einsum('bhncd,bhnsd->bhncs', q, k) * decay, 0.0)

    def step(s, inp):
        qi, ki, ui, wi, gci, ai = inp
        v_new = ui - jnp.einsum('bhcd,bhde->bhce', wi, s)
        o = jnp.einsum('bhcd,bhde->bhce', qi * jnp.exp(gci)[..., None], s) + jnp.einsum('bhcs,bhse->bhce', ai, v_new)
        g_last = gci[..., -1]
        s = s * jnp.exp(g_last)[..., None, None] + jnp.einsum(
            'bhcd,bhce->bhde', ki * jnp.exp(g_last[..., None] - gci)[..., None], v_new)
        return s, o

    mv = lambda t: jnp.moveaxis(t, 2, 0)
    s_fin, o = lax.scan(step, s0, (mv(q), mv(k), mv(u), mv(w), mv(gc), mv(intra)))
    o = o.transpose(1, 0, 3, 2, 4).reshape(B, L, H, DV)
    return o, s_fin


def _gdn_recurrent(q, k, v, g, beta, s0):
    def step(s, inp):
        qt, kt, vt, gt, bt = inp
        s = s * jnp.exp(gt)[..., None, None]
        delta = (vt - jnp.einsum('bhd,bhde->bhe', kt, s)) * bt[..., None]
        s = s + kt[..., :, None] * delta[..., None, :]
        return s, jnp.einsum('bhd,bhde->bhe', qt, s)

    mv = lambda t: jnp.moveaxis(t, 1, 0)
    s_fin, o = lax.scan(step, s0, (mv(q), mv(k), mv(v), mv(g), mv(beta)))
    return jnp.moveaxis(o, 0, 1), s_fin


def _layer(x, pos0, conv_buf, pool_buf, s0, chunked, norm_pre_mix, w_in, conv_w, a_log, dt_bias,
           norm_gdn_out, w_pool, pool_scale, w_out, norm_post_mix, norm_pre_mlp, w_up, w_down, norm_post_mlp):
    B, L, _ = x.shape
    h = _rmsnorm(x, norm_pre_mix)
    proj = h @ w_in
    o1 = D_QKV; o2 = o1 + D_GDN; o3 = o2 + GDN_HEADS; o4 = o3 + GDN_HEADS
    qkv_in, gate, a_in, b_in, u = proj[..., :o1], proj[..., o1:o2], proj[..., o2:o3], proj[..., o3:o4], proj[..., o4:]

    xp = jnp.concatenate([conv_buf.astype(qkv_in.dtype), qkv_in], axis=1)
    conv = sum(conv_w[j] * xp[:, j:j + L] for j in range(CONV_W))
    new_conv = xp[:, -(CONV_W - 1):]
    qkv = jax.nn.silu(conv.astype(jnp.float32)).reshape(B, L, 3, GDN_HEADS, GDN_HEAD_DIM)
    q = _l2norm(qkv[:, :, 0]) * (GDN_HEAD_DIM ** -0.5)
    k = _l2norm(qkv[:, :, 1])
    v = qkv[:, :, 2]
    beta = jax.nn.sigmoid(b_in.astype(jnp.float32))
    g = -jnp.exp(a_log.astype(jnp.float32)) * jax.nn.softplus(a_in.astype(jnp.float32) + dt_bias.astype(jnp.float32))
    core = _gdn_chunked if chunked else _gdn_recurrent
    o_a, s_new = core(q, k, v, g, beta, s0.astype(jnp.float32))
    o_a = _rmsnorm(o_a, norm_gdn_out) * jax.nn.silu(gate.astype(jnp.float32)).reshape(B, L, GDN_HEADS, GDN_HEAD_DIM)
    o_a = o_a.reshape(B, L, D_GDN).astype(x.dtype)

    up = jnp.concatenate([pool_buf.astype(u.dtype), u], axis=1)
    upf = up.astype(jnp.float32)
    c0 = jnp.concatenate([jnp.zeros((B, 1, D_POOL), jnp.float32), jnp.cumsum(upf, axis=1)], axis=1)
    pos = pos0 + jnp.arange(L)
    cur = upf[:, POOL_BUF:]
    groups = []
    for i, win in enumerate(POOL_WINDOWS):
        sl = slice(i * POOL_GROUP_DIM, (i + 1) * POOL_GROUP_DIM)
        wsum = c0[:, POOL_BUF + 1:POOL_BUF + 1 + L, sl] - c0[:, POOL_BUF + 1 - win:POOL_BUF + 1 - win + L, sl]
        cnt = jnp.minimum(pos + 1, win).astype(jnp.float32)[None, :, None]
        groups.append(wsum / cnt - cur[..., sl])
    pooled = jnp.stack(groups, axis=2).astype(x.dtype)
    o_b = jnp.einsum('blgc,gcd->blgd', pooled, w_pool).reshape(B, L, D_POOL) * pool_scale
    new_pool = up[:, -POOL_BUF:]

    mix = jnp.concatenate([o_a, o_b.astype(x.dtype)], axis=-1) @ w_out
    x = x + _rmsnorm(mix, norm_post_mix)

    hm = _rmsnorm(x, norm_pre_mlp)
    ff = jnp.square(jax.nn.relu(hm @ w_up)) @ w_down
    x = x + _rmsnorm(ff, norm_post_mlp)
    return x, s_new.astype(x.dtype), new_conv, new_pool


def setup_inputs(seed: int = 0) -> dict:
    key = jax.random.key(seed)
    ks = jax.random.split(key, 20)
    f32 = jnp.float32
    nrm = lambda k, shape, scale: jax.random.normal(k, shape, f32) * scale
    gain = lambda k, n: 1.0 + 0.05 * jax.random.normal(k, (DEPTH, n), f32)
    return {
        "x_prompt": nrm(ks[0], (BATCH, SEQ, D_MODEL), 1.0),
        "x_sample": nrm(ks[1], (DEC_BATCH, DEC_SEQ, D_MODEL), 1.0),
        "state_gdn": nrm(ks[2], (DEPTH, DEC_BATCH, GDN_HEADS, GDN_HEAD_DIM, GDN_HEAD_DIM), 0.1),
        "state_conv": nrm(ks[3], (DEPTH, DEC_BATCH, CONV_W - 1, D_QKV), 1.0),
        "state_pool": nrm(ks[4], (DEPTH, DEC_BATCH, POOL_BUF, D_POOL), 1.0),
        "norm_pre_mix": gain(ks[5], D_MODEL),
        "w_in": nrm(ks[6], (DEPTH, D_MODEL, N_IN), D_MODEL ** -0.5),
        "conv_w": nrm(ks[7], (DEPTH, CONV_W, D_QKV), CONV_W ** -0.5),
        "a_log": jnp.log(jax.random.uniform(ks[8], (DEPTH, GDN_HEADS), f32, 1.0, 16.0)),
        "dt_bias": nrm(ks[9], (DEPTH, GDN_HEADS), 0.1),
        "norm_gdn_out": gain(ks[10], GDN_HEAD_DIM),
        "w_pool": nrm(ks[11], (DEPTH, POOL_GROUPS, POOL_GROUP_DIM, POOL_GROUP_DIM), POOL_GROUP_DIM ** -0.5),
        "pool_scale": 1.0 + 0.1 * jax.random.normal(ks[12], (DEPTH, D_POOL), f32),
        "w_out": nrm(ks[13], (DEPTH, D_MIX, D_MODEL), D_MIX ** -0.5),
        "norm_post_mix": gain(ks[14], D_MODEL),
        "norm_pre_mlp": gain(ks[15], D_MODEL),
        "w_up": nrm(ks[16], (DEPTH, D_MODEL, D_FF), D_MODEL ** -0.5),
        "w_down": nrm(ks[17], (DEPTH, D_FF, D_MODEL), D_FF ** -0.5),
        "norm_post_mlp": gain(ks[18], D_MODEL),
    }


def reference(x_prompt, x_sample, state_gdn, state_conv, state_pool, norm_pre_mix, w_in, conv_w, a_log,
              dt_bias, norm_gdn_out, w_pool, pool_scale, w_out, norm_post_mix, norm_pre_mlp, w_up, w_down,
              norm_post_mlp):
    yp, ys = x_prompt, x_sample
    gdn_p, conv_p, pool_p, gdn_s, conv_s, pool_s = [], [], [], [], [], []
    for l in range(DEPTH):
        params = (norm_pre_mix[l], w_in[l], conv_w[l], a_log[l], dt_bias[l], norm_gdn_out[l], w_pool[l],
                  pool_scale[l], w_out[l], norm_post_mix[l], norm_pre_mlp[l], w_up[l], w_down[l], norm_post_mlp[l])
        B = yp.shape[0]
        yp, s1, c1, p1 = _layer(
            yp, 0,
            jnp.zeros((B, CONV_W - 1, D_QKV), yp.dtype),
            jnp.zeros((B, POOL_BUF, D_POOL), yp.dtype),
            jnp.zeros((B, GDN_HEADS, GDN_HEAD_DIM, GDN_HEAD_DIM), jnp.float32),
            True, *params)
        ys, s2, c2, p2 = _layer(ys, PAST_LEN, state_conv[l], state_pool[l], state_gdn[l], False, *params)
        gdn_p.append(s1); conv_p.append(c1); pool_p.append(p1)
        gdn_s.append(s2); conv_s.append(c2); pool_s.append(p2)
    new_gdn_prompt = jnp.stack(gdn_p)
    new_conv_prompt = jnp.stack(conv_p)
    new_pool_prompt = jnp.stack(pool_p)
    new_gdn_sample = jnp.stack(gdn_s)
    new_conv_sample = jnp.stack(conv_s)
    new_pool_sample = jnp.stack(pool_s)
    return (yp, ys, new_gdn_prompt, new_conv_prompt, new_pool_prompt, new_gdn_sample, new_conv_sample, new_pool_sample)
```

```python
from contextlib import ExitStack
import numpy as np
import concourse.bass as bass
import concourse.mybir as mybir
from concourse.bass_utils import run_bass_kernel_spmd

F32 = mybir.dt.float32
BF16 = mybir.dt.bfloat16
AF = mybir.ActivationFunctionType
ALU = mybir.AluOpType

ENGS = ("tensor", "vector", "scalar", "gpsimd", "sync")
D = 2048
NIN = 5136
TP = 1024
TS = 16
NT = 2 * TP + TS
NM = TP + TS
EPS = 1e-6
NEG = -30000.0
SAME_SYNC = True


class Buf:
    __slots__ = ("name", "w", "r", "excl")

    def __init__(self, name, excl=False):
        self.name = name
        self.w = None
        self.r = []
        self.excl = excl


import sys as _sys


def CALL(name, *a, **kw):
    ln = _sys._getframe(1).f_lineno

    def f(e):
        ins = getattr(e, name)(*a, **kw)
        try:
            ins.annotate("L%d" % ln)
        except Exception:
            pass
        return ins
    return f


def inherit(new_bufs, old_bufs):
    for nb in new_bufs:
        for ob in old_bufs:
            if ob.w is not None:
                nb.r.append(ob.w)
            nb.r.extend(ob.r)


class Op:
    __slots__ = ("eng", "fn", "deps", "dma", "stream", "flag", "cnt")

    def __init__(self, eng, fn, dma, stream):
        self.eng = eng
        self.fn = fn
        self.deps = []
        self.dma = dma
        self.stream = stream
        self.flag = False
        self.cnt = None


class Sched:
    def __init__(self, nc, same_engine_sync=True):
        self.nc = nc
        self.ops = {e: [] for e in ENGS}
        self.same = same_engine_sync

    def op(self, eng, fn, reads=(), writes=(), dma=False, stream=None):
        o = Op(eng, fn, dma, stream if dma else None)
        ex = [b for b in reads if b.excl]
        if ex:
            reads = [b for b in reads if not b.excl]
            writes = list(writes) + [b for b in ex if b not in writes]
        deps = []
        for b in reads:
            if b.w is not None:
                deps.append(b.w)
        for b in writes:
            if b.w is not None:
                deps.append(b.w)
            deps.extend(b.r)
        seen = set()
        for d in deps:
            if id(d) in seen or d is o:
                continue
            seen.add(id(d))
            if (not d.dma) and d.eng == eng and (eng == "tensor" or (not self.same and eng != "gpsimd")):
                continue
            o.deps.append(d)
            d.flag = True
        for b in reads:
            b.r.append(o)
        for b in writes:
            b.w = o
            b.r = []
        self.ops[eng].append(o)
        return o

    def emit(self, stack, final_streams=()):
        nc = self.nc
        pool_sz = {"sync": 16, "gpsimd": 6, "scalar": 4, "vector": 2, "tensor": 2}
        sems = {}
        for e in ENGS:
            sems[("eng", e)] = stack.enter_context(nc.semaphore("s_" + e))
        cnt = {k: 0 for k in sems}
        dma_i = {e: 0 for e in ENGS}
        final_waits = {}
        for e in ENGS:
            for o in self.ops[e]:
                if o.dma:
                    o.flag = True
                if not o.flag:
                    continue
                if o.dma:
                    i = dma_i[e]
                    dma_i[e] += 1
                    R = pool_sz[e]
                    k = ("dma", e, i % R)
                    if k not in sems:
                        sems[k] = stack.enter_context(nc.semaphore("d_%s_%d" % (e, i % R)))
                    o.cnt = (k, 16 * (i // R + 1))
                    if o.stream in final_streams:
                        final_waits[k] = max(final_waits.get(k, 0), o.cnt[1])
                else:
                    k = ("eng", e)
                    cnt[k] += 1
                    o.cnt = (k, cnt[k])
        block = stack.enter_context(nc.Block())

        def run(ename, eng):
            waited = {}
            for o in self.ops[ename]:
                need = {}
                for d in o.deps:
                    k, c = d.cnt
                    if c > need.get(k, 0):
                        need[k] = c
                if o.dma and o.cnt[1] > 16:
                    k, c = o.cnt
                    need[k] = max(need.get(k, 0), c - 16)
                for k, c in need.items():
                    if waited.get(k, 0) >= c:
                        continue
                    eng.wait_ge(sems[k], c)
                    waited[k] = c
                ins = o.fn(eng)
                if o.flag:
                    k, c = o.cnt
                    ins.then_inc(sems[k], 16 if o.dma else 1)
            if ename == "sync":
                for k, c in final_waits.items():
                    if waited.get(k, 0) < c:
                        eng.wait_ge(sems[k], c)

        @block.tensor
        def _(e):
            run("tensor", e)

        @block.vector
        def _(e):
            run("vector", e)

        @block.scalar
        def _(e):
            run("scalar", e)

        @block.gpsimd
        def _(e):
            run("gpsimd", e)

        @block.sync
        def _(e):
            run("sync", e)


RB_GDNOUT = 0
RB_ALOG = 128
RB_DT = 136
RB_INVC = 144
RB = 144 + 64
PV_CONV = 0
PV_PREMIX = 96
PV_PREMLP = 112
PV_PSCALE = 128
PV_GDNOUT = 136
PV = 137


def build_program(debug=False):
    nc = bass.Bass("TRN2", target_bir_lowering=False)
    dt_in = lambda n, s: nc.dram_tensor(n, s, F32, kind="ExternalInput")
    dt_out = lambda n, s: nc.dram_tensor(n, s, F32, kind="ExternalOutput")
    xall = dt_in("xall", [NT, D])
    st_gdn = dt_in("st_gdn", [TS, 8, 128, 128])
    st_conv = dt_in("st_conv", [TS * 3, 3072])
    st_pool = dt_in("st_pool", [TS * 15, 1024])
    w_in = dt_in("w_in", [D, NIN])
    w_pool = dt_in("w_pool", [4, 256, 256])
    w_out = dt_in("w_out", [D, D])
    w_up = dt_in("w_up", [D, 4 * D])
    w_down = dt_in("w_down", [4 * D, D])
    rows_bc = dt_in("rows_bc", [128, RB])
    nrm_post = dt_in("nrm_post", [2, 128, D])
    pvec_d = dt_in("pvec", [128, PV])
    y_main = dt_out("y_main", [NM, D])
    o_gdn_p = dt_out("o_gdn_p", [8, 128, 128])
    o_conv_p = dt_out("o_conv_p", [3, 3072])
    o_pool_p = dt_out("o_pool_p", [15, 1024])
    o_gdn_s = dt_out("o_gdn_s", [TS, 8, 128, 128])
    o_conv_s = dt_out("o_conv_s", [TS, 3, 3072])
    o_pool_s = dt_out("o_pool_s", [TS, 15, 1024])
    sc_qT = nc.dram_tensor("sc_qT", [8, 128, TP], BF16, kind="Internal")
    sc_kT = nc.dram_tensor("sc_kT", [8, 128, 2 * TP], BF16, kind="Internal")
    sc_vT = nc.dram_tensor("sc_vT", [8, 128, 2 * TP], BF16, kind="Internal")
    sc_gate = nc.dram_tensor("sc_gate", [NM, 1024], BF16, kind="Internal")
    sc_mixT = nc.dram_tensor("sc_mixT", [16, 128, NM], BF16, kind="Internal")

    with ExitStack() as st:
        S = Sched(nc, same_engine_sync=SAME_SYNC)
        sb = lambda n, s, d: st.enter_context(nc.sbuf_tensor(n, s, d))
        V = lambda fn, r=(), w=(): S.op("vector", fn, r, w)
        A = lambda fn, r=(), w=(): S.op("scalar", fn, r, w)
        T = lambda fn, r=(), w=(): S.op("tensor", fn, r, w)
        P = lambda fn, r=(), w=(): S.op("gpsimd", fn, r, w)
        DQ = lambda fn, r=(), w=(), stream="io": S.op("sync", fn, r, w, dma=True, stream=stream)
        GQ = lambda fn, r=(), w=(): S.op("gpsimd", fn, r, w, dma=True, stream="w")

        banks = []
        for i in range(8):
            t = st.enter_context(nc.psum_tensor("pb%d" % i, [128, 512], F32))
            banks.append((t, t.bitcast(BF16), Buf("pb%d" % i, True)))
        bank_rr = [0]

        def nb():
            i = bank_rr[0]
            bank_rr[0] = (i + 1) % 8
            return banks[i]

        def nb2():
            i = bank_rr[0]
            if i % 2:
                i = (i + 1) % 8
            bank_rr[0] = (i + 2) % 8
            return banks[i], banks[i + 1]

        identf = sb("identf", [128, 128], F32); b_c = Buf("consts")
        identb = sb("identb", [128, 128], BF16)
        onesb = sb("onesb", [128, 128], BF16)
        onesf = sb("onesf", [128, 128], F32)
        utri = sb("utri", [128, 128], F32)
        lstr = sb("lstr", [128, 128], F32)
        negs = sb("negs", [128, 128], F32)
        negi = sb("negi", [128, 128], F32)
        epst = sb("epst", [128, 1], F32)
        rbc = sb("rbc", [128, RB], F32); b_rbc = Buf("rbc")
        pvec = sb("pvec_sb", [128, PV], F32); b_pv = Buf("pvec")
        nalog = sb("nalog", [128, 8], F32)

        def aff(t, pattern, cm, op, fill):
            P(CALL("affine_select", out=t[:], in_=t[:], pattern=pattern, compare_op=op, fill=fill,
                                        base=0, channel_multiplier=cm), r=[b_c], w=[b_c])

        P(CALL("memset", identf[:], 1.0), w=[b_c])
        aff(identf, [[-1, 128]], 1, ALU.is_equal, 0.0)
        P(CALL("memset", utri[:], 1.0), w=[b_c])
        aff(utri, [[1, 128]], -1, ALU.is_ge, 0.0)
        P(CALL("memset", lstr[:], 1.0), w=[b_c])
        aff(lstr, [[-1, 128]], 1, ALU.is_gt, 0.0)
        P(CALL("memset", negs[:], 0.0), w=[b_c])
        aff(negs, [[-1, 128]], 1, ALU.is_gt, NEG)
        P(CALL("memset", negi[:], 0.0), w=[b_c])
        aff(negi, [[1, 128]], -1, ALU.is_ge, NEG)
        P(CALL("memset", onesf[:], 1.0), w=[b_c])
        P(CALL("memset", onesb[:], 1.0), w=[b_c])
        P(CALL("memset", epst[:], EPS), w=[b_c])
        V(CALL("tensor_copy", out=identb[:], in_=identf[:]), r=[b_c], w=[b_c])
        DQ(CALL("dma_start", out=rbc[:], in_=rows_bc.ap()), w=[b_rbc])
        DQ(CALL("dma_start", out=pvec[:], in_=pvec_d.ap()), w=[b_pv])
        A(CALL("activation", out=nalog[:], in_=rbc[:, RB_ALOG:RB_ALOG + 8], func=AF.Exp), r=[b_rbc], w=[b_c])
        V(CALL("tensor_scalar_mul", out=nalog[:], in0=nalog[:], scalar1=-1.0), r=[b_c], w=[b_c])

        hT = sb("hT", [128, 16, NM], BF16)
        b_hT = [Buf("hT%d" % i) for i in range(9)]
        wbuf = [sb("wbuf%d" % i, [128, 16 * 256], BF16) for i in range(2)]
        b_wbuf = [Buf("wbuf%d" % i) for i in range(2)]
        wrr = [0]
        wab = sb("wab", [128, 16, 16], BF16); b_wab = Buf("wab")
        wpl = sb("wpl", [128, 4, 2, 256], BF16); b_wpl = Buf("wpl")
        b_mixT = {}
        halo_c = sb("halo_c", [128, 24, 3], F32); b_hc = Buf("halo_c")
        halo_u = sb("halo_u", [128, 8, 15], F32); b_hu = Buf("halo_u")
        gb = sb("gb", [128, 17, 24], F32); b_gb = Buf("gb")
        rawtail = sb("rawtail", [128, 32, 32], F32); b_rt = Buf("rawtail")
        Sst = sb("Sst", [128, 8, 128], F32); b_S = Buf("S")
        Sbf = sb("Sbf", [128, 8, 128], BF16); b_Sb = Buf("Sb")
        smp = sb("smp", [128, 24, 16], F32); b_smp = Buf("smp")
        xt = [sb("xt%d" % i, [128, D], F32) for i in range(2)]
        b_xt = [Buf("xt%d" % i) for i in range(2)]
        xb = sb("xb", [128, D], BF16); b_xb = Buf("xb")
        junk = xb; b_junk = b_xb
        st1 = sb("st1", [128, 8], F32); b_st1 = Buf("st1")

        arena1 = sb("arena1", [128, 9 * 2048 + 640], F32)
        arena1b = arena1.bitcast(BF16)
        arena2 = sb("arena2", [128, 2 * 8 * NM], BF16)
        b_a1 = Buf("arena1_phase13")
        b_a2 = Buf("arena2_phase13")

        def rstd_from_ss(ss_ap, n, inv_d, rd, wr):
            V(CALL("tensor_scalar", out=ss_ap, in0=ss_ap, scalar1=inv_d, scalar2=EPS, op0=ALU.mult, op1=ALU.add),
              r=rd, w=wr)
            A(CALL("activation", out=ss_ap, in_=ss_ap, func=AF.Ln), r=wr, w=wr)
            A(CALL("activation", out=ss_ap, in_=ss_ap, func=AF.Exp, scale=-0.5), r=wr, w=wr)

        NT_SETS = []

        def norm_transpose(src_ap, src_bufs, rows, col0, tile_idx, pv_off, si=0):
            xb_, bxb, stc_, bst = NT_SETS[si]
            A(CALL("activation", out=xb_[0:rows, :], in_=src_ap, func=AF.Square, accum_out=stc_[0:rows, 0:1]),
              r=src_bufs, w=[bxb, bst])
            yield
            rstd_from_ss(stc_[0:rows, 0:1], rows, 1.0 / D, [bst], [bst])
            yield
            V(CALL("tensor_scalar_mul", out=xb_[0:rows, :], in0=src_ap, scalar1=stc_[0:rows, 0:1]),
              r=list(src_bufs) + [bst], w=[bxb])
            yield
            for half in range(2):
                pb, pbb, bb = nb()
                for k in range(8):
                    kc = half * 8 + k
                    T(CALL("transpose", out=pbb[:, k * 128:k * 128 + rows], in_=xb_[0:rows, kc * 128:(kc + 1) * 128],
                           identity=identb[0:rows, 0:rows]), r=[bxb, b_c], w=[bb])
                yield
                src = pbb[:, 0:1024].rearrange("p (k t) -> p k t", k=8)[:, :, 0:rows]
                wv = pvec[:, pv_off + half * 8:pv_off + half * 8 + 8].unsqueeze(2).to_broadcast([128, 8, rows])
                V(CALL("tensor_tensor", out=hT[:, half * 8:half * 8 + 8, col0:col0 + rows], in0=src, in1=wv, op=ALU.mult),
                  r=[bb, b_pv], w=[b_hT[tile_idx]])
                yield

        def interleave(gens, lag=0):
            gens = list(gens)
            active = []
            step = 0
            while gens or active:
                if gens and (lag == 0 or step % max(lag, 1) == 0 or not active):
                    if lag == 0:
                        active.extend(gens)
                        gens = []
                    else:
                        active.append(gens.pop(0))
                for g_ in list(active):
                    try:
                        next(g_)
                    except StopIteration:
                        active.remove(g_)
                step += 1

        def load_w(ap_fn):
            i = wrr[0]
            wrr[0] = (i + 1) % 2
            GQ(CALL("dma_start", out=ap_fn(wbuf[i]), in_=ap_fn.src), w=[b_wbuf[i]])
            return wbuf[i], b_wbuf[i]

        def w_in_unit(c0):
            def f(t):
                return t[:, :].rearrange("p (k n) -> p k n", k=16)
            f.src = w_in.ap()[:, c0:c0 + 256].rearrange("(k p) n -> p k n", p=128)
            return f

        a1off = [0]

        def a1f(n):
            o = a1off[0]
            a1off[0] += n
            return arena1[:, o:o + n]

        def a1b(n):
            o = a1off[0]
            a1off[0] += (n + 1) // 2
            return arena1b[:, 2 * o:2 * o + n]

        raw = [a1f(528) for _ in range(2)]; b_raw = [Buf("raw0"), Buf("raw1")]
        acc = a1f(512); b_acc = Buf("acc")
        sl = acc; b_sl = b_acc
        sqb = None; b_sqb = Buf("sqb")
        rs = a1f(512); b_rs = Buf("rs")
        fmo = [a1b(512) for _ in range(2)]; b_fmo = [Buf("fmo0"), Buf("fmo1")]
        tmo = fmo; b_tmo = b_fmo
        _g = a1b(512); _bg = Buf("gto"); gto = [_g, _g]; b_gto = [_bg, _bg]
        uA = a1f(544); uB = a1f(544); b_uA = Buf("uA"); b_uB = Buf("uB")
        uraw = [a1f(544) for _ in range(2)]; b_uraw = [Buf("uraw0"), Buf("uraw1")]
        poolT = arena2[:, 0:8 * NM].rearrange("p (c t) -> p c t", c=8); b_poolT = [Buf("poolT%d" % i) for i in range(8)]
        rr = {"raw": 0, "fmo": 0, "tmo": 0, "gto": 0, "uraw": 0}

        def nxt(k):
            i = rr[k]
            rr[k] = 1 - i
            return i

        b_sc_q = {}
        b_sc_kT = {}
        b_sc_k = {}
        b_sc_v = {}
        b_sc_g = [Buf("scg%d" % i) for i in range(9)]

        def scb(d, key):
            if key not in d:
                d[key] = Buf("sc")
            return d[key]

        a2o = [8 * NM]

        def a2b(n):
            o = a2o[0]
            a2o[0] += n
            return arena2[:, o:o + n]

        def a2f_(n):
            o = a2o[0]
            a2o[0] += 2 * n
            return arena2[:, o:o + 2 * n].bitcast(F32)

        hk = sb("hk", [128, 2, 16], F32)
        SETS = [dict(raw=raw[0], b_raw=b_raw[0], acc=acc, b_acc=b_acc, sqb=sqb, b_sqb=b_sqb, rs=rs, b_rs=b_rs,
                     fmo=fmo[0], b_fmo=b_fmo[0], tmo=tmo[0], b_tmo=b_tmo[0], hk=hk[:, 0, :], b_hk=Buf("hk0"),
                     uraw=uraw[0], b_uraw=b_uraw[0], uA=uA, b_uA=b_uA, uB=uB, b_uB=b_uB),
                dict(raw=raw[1], b_raw=b_raw[1], acc=a2f_(512), b_acc=Buf("acc2"), sqb=a2b(512), b_sqb=Buf("sqb2"),
                     rs=a2f_(512), b_rs=Buf("rs2"), fmo=fmo[1], b_fmo=b_fmo[1], tmo=tmo[1], b_tmo=b_tmo[1],
                     hk=hk[:, 1, :], b_hk=Buf("hk1"), uraw=uraw[1], b_uraw=b_uraw[1], uA=a2f_(544), b_uA=Buf("uA2"),
                     uB=a2f_(544), b_uB=Buf("uB2"))]
        xb2 = a2b(2048); b_xb2 = Buf("xb2")
        assert a2o[0] <= 16 * NM, a2o[0]
        st1b = sb("st1b", [128, 8], F32); b_st1b = Buf("st1b")
        NT_SETS.append((xb, b_xb, st1, b_st1))
        NT_SETS.append((xb2, b_xb2, st1b, b_st1b))
        A2HI = [SETS[1]["b_acc"], SETS[1]["b_sqb"], SETS[1]["b_rs"], SETS[1]["b_uA"], SETS[1]["b_uB"], b_xb2]

        def qkv_chunk(wt, bw, j, ch, blocks, first_zero_halo, si=0, halo_mode=None, D=0, pre=None):
            if pre is not None:
                wt, bw = pre()
            Z = SETS[si]
            rw, brw = Z["raw"], Z["b_raw"]
            ac, bac = Z["acc"], Z["b_acc"]
            sq_, bsq = Z["sqb"], Z["b_sqb"]
            rs_, brs = Z["rs"], Z["b_rs"]
            fo, bfo = Z["fmo"], Z["b_fmo"]
            to, bto = Z["tmo"], Z["b_tmo"]
            hk_, bhk = Z["hk"], Z["b_hk"]
            kind = ch // 8
            h = ch % 8
            first = True
            for (c0, n, g0, bk) in blocks:
                pb, pbb, bb = nb()
                tiles = list(range(c0 // 128, (c0 + n - 1) // 128 + 1))
                for kc in range(16):
                    T(CALL("matmul", out=pb[:, 0:n], lhsT=wt[:, kc * 256 + j * 128:kc * 256 + (j + 1) * 128],
                           rhs=hT[:, kc, c0:c0 + n], start=(kc == 0), stop=(kc == 15)),
                      r=[bw] + [b_hT[t] for t in tiles], w=[bb])
                yield
                for _ in range(D):
                    yield
                if bk == "sample":
                    A(CALL("activation", out=rawtail[:, ch, 16:32], in_=pb[:, 0:16], func=AF.Copy), r=[bb], w=[b_rt])
                    yield
                    continue
                if halo_mode == "hk":
                    V(CALL("tensor_copy", out=rw[:, 0:3], in_=hk_[:, 0:3]), r=[bhk], w=[brw])
                elif first and first_zero_halo:
                    V(CALL("memset", rw[:, 0:3], 0.0), w=[brw])
                elif first:
                    V(CALL("tensor_copy", out=rw[:, 0:3], in_=halo_c[:, ch, :]), r=[b_hc], w=[brw])
                else:
                    V(CALL("tensor_copy", out=rw[:, 0:3], in_=hk_[:, 0:3]), r=[bhk], w=[brw])
                first = False
                A(CALL("activation", out=rw[:, 3:3 + n], in_=pb[:, 0:n], func=AF.Copy), r=[bb], w=[brw])
                yield
                V(CALL("tensor_copy", out=hk_[:, 0:3], in_=rw[:, n:n + 3]), r=[brw], w=[bhk])
                if bk in ("halo", "lastpre"):
                    V(CALL("tensor_copy", out=halo_c[:, ch, :], in_=rw[:, n:n + 3]), r=[brw], w=[b_hc])
                if bk == "halo":
                    yield
                    continue
                if bk == "lastmain":
                    V(CALL("tensor_copy", out=rawtail[:, ch, 0:16], in_=rw[:, 3 + n - 16:3 + n]), r=[brw], w=[b_rt])
                cw = lambda jj: pvec[:, PV_CONV + ch * 4 + jj:PV_CONV + ch * 4 + jj + 1]
                V(CALL("tensor_scalar_mul", out=ac[:, 0:n], in0=rw[:, 0:n], scalar1=cw(0)), r=[brw, b_pv], w=[bac])
                yield
                for jj in range(1, 4):
                    V(CALL("scalar_tensor_tensor", out=ac[:, 0:n], in0=rw[:, jj:jj + n], scalar=cw(jj), in1=ac[:, 0:n],
                           op0=ALU.mult, op1=ALU.add), r=[brw, b_pv, bac], w=[bac])
                    yield
                A(CALL("activation", out=rs_[:, 0:n], in_=ac[:, 0:n], func=AF.Exp, scale=-1.0), r=[bac], w=[brs])
                yield
                A(CALL("activation", out=rs_[:, 0:n], in_=rs_[:, 0:n], func=AF.Ln, bias=1.0), r=[brs], w=[brs])
                yield
                A(CALL("activation", out=rs_[:, 0:n], in_=rs_[:, 0:n], func=AF.Exp, scale=-1.0), r=[brs], w=[brs])
                yield
                V(CALL("tensor_tensor", out=fo[:, 0:n], in0=ac[:, 0:n], in1=rs_[:, 0:n], op=ALU.mult), r=[bac, brs], w=[bfo])
                yield
                if kind == 0:
                    DQ(CALL("dma_start", out=sc_qT.ap()[h, :, g0 - TP:g0 - TP + n], in_=fo[:, 0:n]), r=[bfo], w=[scb(b_sc_q, (h, g0))])
                elif kind == 1:
                    DQ(CALL("dma_start", out=sc_kT.ap()[h, :, g0:g0 + n], in_=fo[:, 0:n]), r=[bfo], w=[scb(b_sc_kT, (h, g0))])
                else:
                    DQ(CALL("dma_start", out=sc_vT.ap()[h, :, g0:g0 + n], in_=fo[:, 0:n]), r=[bfo], w=[scb(b_sc_v, (h, g0))])

        def interleave(gens, lag=0):
            gens = list(gens)
            active = []
            step = 0
            while gens or active:
                if gens and (lag == 0 or step % max(lag, 1) == 0 or not active):
                    if lag == 0:
                        active.extend(gens)
                        gens = []
                    else:
                        active.append(gens.pop(0))
                for g_ in list(active):
                    try:
                        next(g_)
                    except StopIteration:
                        active.remove(g_)
                step += 1

        def qkv_stream(units, blocks, first_zero_halo, lag=4, D=5):
            wl = {}

            def mk_pre(ui):
                def pre():
                    if ui not in wl:
                        k_, u_ = units[ui]
                        wl[ui] = load_w(w_in_unit(k_ * 1024 + u_ * 256))
                    if ui + 1 < len(units) and (ui + 1) not in wl:
                        k_, u_ = units[ui + 1]
                        wl[ui + 1] = load_w(w_in_unit(k_ * 1024 + u_ * 256))
                    return wl[ui]
                return pre

            chains = []
            for ui, (kind, u) in enumerate(units):
                for bi, blk in enumerate(blocks):
                    for j in range(2):
                        hm = "hk" if (bi > 0 and blk[3] != "sample") else None
                        chains.append(qkv_chunk(None, None, j, kind * 8 + u * 2 + j, [blk], first_zero_halo, j, hm, D, mk_pre(ui)))
            interleave(chains, lag=lag)

        def ab_tiles(tile_list):
            for (ct, slot, rows) in tile_list:
                pb, pbb, bb = nb()
                for kc in range(16):
                    T(CALL("matmul",
                        out=pb[0:rows, 0:16], lhsT=hT[:, kc, ct * 128:ct * 128 + rows], rhs=wab[:, kc, :],
                        start=(kc == 0), stop=(kc == 15)), r=[b_wab, b_hT[ct]], w=[bb])
                g_ = gb[0:rows, slot, 0:8]
                lb = gb[0:rows, slot, 8:16]
                be = gb[0:rows, slot, 16:24]
                V(CALL("tensor_tensor", out=g_, in0=pb[0:rows, 0:8],
                                                                     in1=rbc[0:rows, RB_DT:RB_DT + 8], op=ALU.add),
                  r=[bb, b_rbc], w=[b_gb])
                A(CALL("activation", out=g_, in_=g_, func=AF.Exp), r=[b_gb], w=[b_gb])
                A(CALL("activation", out=g_, in_=g_, func=AF.Ln, bias=1.0), r=[b_gb], w=[b_gb])
                V(CALL("tensor_tensor", out=g_, in0=g_, in1=nalog[0:rows, :], op=ALU.mult),
                  r=[b_gb, b_c], w=[b_gb])
                A(CALL("activation", out=lb, in_=pb[0:rows, 8:16], func=AF.Exp, scale=-1.0),
                  r=[bb], w=[b_gb])
                A(CALL("activation", out=lb, in_=lb, func=AF.Ln, bias=1.0), r=[b_gb], w=[b_gb])
                V(CALL("tensor_scalar_mul", out=lb, in0=lb, scalar1=-1.0), r=[b_gb], w=[b_gb])
                A(CALL("activation", out=be, in_=lb, func=AF.Exp), r=[b_gb], w=[b_gb])

        def g3(n=1024):
            return a1b(n).rearrange("p (h i) -> p h i", h=8)

        _alt = lambda w0: arena1b[:, 2 * w0:2 * w0 + 1024].rearrange("p (h i) -> p h i", h=8)
        kTc = [g3(), _alt(0)]; b_kTc = [Buf("kTc0"), Buf("kTc1")]
        qTc = [g3(), _alt(1024)]; b_qTc = [Buf("qTc0"), Buf("qTc1")]
        _v = g3(); _b = Buf("ktm"); ktm = [_v, _v]; b_ktm = [_b, _b]
        _v = g3(); _b = Buf("vtm"); vtm = [_v, _v]; b_vtm = [_b, _b]
        GU = a1f(1024).rearrange("p (h i) -> p h i", h=8); b_GU = Buf("GU")
        vTc_l = [g3(), _alt(512)]; b_vTc_l = [Buf("vTc0"), Buf("vTc1")]
        sqs_ = g3(); b_sqs_ = Buf("sqs_")
        nsc = sb("nsc", [128, 64], F32); b_nsc = Buf("nsc")
        bD = g3(); b_bD = Buf("bD")
        DT_ = g3(); b_DT = Buf("DT")
        Lm = [g3() for _ in range(2)]; b_Lm = [Buf("L0"), Buf("L1")]
        LTm = [g3() for _ in range(2)]; b_LTm = [Buf("LT0"), Buf("LT1")]
        Tt = [g3() for _ in range(2)]; b_Tt = [Buf("Tt0"), Buf("Tt1")]
        intraT = g3(); b_intra = Buf("intraT")
        vbt = g3(); b_vb = Buf("vb")
        kbg = g3(); b_kbg = Buf("kbg")
        kdt = g3(); b_kd = Buf("kd")
        nwT = g3(); b_nwT = Buf("nwT")
        vnew = g3(); b_vnew = Buf("vnew")
        otm = a1f(1024).rearrange("p (h i) -> p h i", h=8); b_otm = Buf("otm")
        oab = g3(); b_oab = Buf("oab")
        otm2 = oab; b_otm2 = b_oab
        oaT = g3(); b_oaT = Buf("oaT")
        gate_l = [g3(), _alt(1536)]; b_gate_l = [Buf("gate0"), Buf("gate1")]
        ALT_BUFS = [b_kTc[1], b_qTc[1], b_vTc_l[1], b_gate_l[1]]
        INPROJ_LOW = [b_raw[0], b_raw[1], b_acc, b_rs]
        gcs = a1f(48); b_gcs = Buf("gcs")
        assert a1off[0] <= 9 * 2048 + 640, a1off[0]

        def H2(name):
            return [Buf(name + "_0"), Buf(name + "_1")]
        hb_bD = H2("bD"); hb_DT = H2("DT"); hb_L = [H2("L0"), H2("L1")]; hb_LT = [H2("LT0"), H2("LT1")]
        hb_Tt = [H2("Tt0"), H2("Tt1")]; hb_intra = H2("intra"); hb_vb = H2("vb"); hb_kbg = H2("kbg"); hb_kd = H2("kd")
        hb_nwT = H2("nwT"); hb_vnew = H2("vnew"); hb_otm = H2("otm"); hb_oab = H2("oab"); hb_oaT = H2("oaT")
        hb_S = H2("S"); hb_Sb = H2("Sb"); hb_st = H2("st1h"); hb_km = H2("km"); hb_vm = H2("vm")
        st1h = sb("st1h", [128, 8], F32)
        GDN_BUFS = (hb_bD + hb_DT + hb_L[0] + hb_L[1] + hb_LT[0] + hb_LT[1] + hb_Tt[0] + hb_Tt[1] + hb_intra + hb_vb + hb_kbg
                    + hb_kd + hb_nwT + hb_vnew + hb_otm + hb_oab + hb_oaT)

        def interleave(gens, lag=0):
            gens = list(gens)
            active = []
            step = 0
            while gens or active:
                if gens and (lag == 0 or step % max(lag, 1) == 0 or not active):
                    if lag == 0:
                        active.extend(gens)
                        gens = []
                    else:
                        active.append(gens.pop(0))
                for g_ in list(active):
                    try:
                        next(g_)
                    except StopIteration:
                        active.remove(g_)
                step += 1

        def gdn_half(c, main, hh, kt_, bkt, km_, bkm, vm_, bvm, qt_, bqt, gv, lbv, bev, vTc, b_vTc, gate_t, b_gate):
            hs = slice(hh * 4, hh * 4 + 4)
            v4 = lambda pp: pp[:, :].rearrange("p (h i) -> p h i", h=4)
            bc4 = lambda ap: ap.unsqueeze(2).to_broadcast([128, 4, 128])
            pT_, pTb, bbT = nb()
            for h4 in range(4):
                h = hh * 4 + h4
                T(CALL("transpose", out=pTb[:, h4 * 128:(h4 + 1) * 128], in_=kt_[:, h, :], identity=identb[:]), r=[bkt, b_c], w=[bbT])
                T(CALL("transpose", out=pTb[:, 512 + h4 * 128:512 + (h4 + 1) * 128], in_=vTc[:, h, :], identity=identb[:]),
                  r=[b_vTc, b_c], w=[bbT])
            yield
            A(CALL("activation", out=km_[:, hs, :], in_=pTb[:, 0:512].rearrange("p (h i) -> p h i", h=4), func=AF.Copy), r=[bbT], w=[hb_km[hh]])
            V(CALL("tensor_copy", out=vm_[:, hs, :], in_=pTb[:, 512:1024].rearrange("p (h i) -> p h i", h=4)), r=[bbT], w=[hb_vm[hh]])
            yield
            pD, _, bbD = nb()
            for h4 in range(4):
                h = hh * 4 + h4
                o_ = pD[:, h4 * 128:(h4 + 1) * 128]
                T(CALL("matmul", out=o_, lhsT=GU[:, h, :], rhs=lstr[:], start=True, stop=False), r=[b_GU, b_c], w=[bbD])
                T(CALL("matmul", out=o_, lhsT=identf[:], rhs=negs[:], start=False, stop=True), r=[b_c], w=[bbD])
            yield
            pK, _, bbK = nb()
            for h4 in range(4):
                h = hh * 4 + h4
                T(CALL("matmul", out=pK[:, h4 * 128:(h4 + 1) * 128], lhsT=kt_[:, h, :], rhs=kt_[:, h, :], start=True, stop=True),
                  r=[bkt], w=[bbK])
            yield
            for h4 in range(4):
                h = hh * 4 + h4
                A(CALL("activation", out=bD[:, h, :], in_=pD[:, h4 * 128:(h4 + 1) * 128], func=AF.Exp, bias=nsc[:, 16 + h:17 + h]),
                  r=[bbD, b_nsc], w=[hb_bD[hh]])
            yield
            V(CALL("tensor_tensor", out=Lm[0][:, hs, :], in0=v4(pK), in1=bD[:, hs, :], op=ALU.mult), r=[bbK, hb_bD[hh]], w=[hb_L[0][hh]])
            yield
            pb, pbb, bb = nb()
            for h4 in range(4):
                h = hh * 4 + h4
                T(CALL("transpose", out=pbb[:, h4 * 128:(h4 + 1) * 128], in_=Lm[0][:, h, :], identity=identb[:]),
                  r=[hb_L[0][hh], b_c], w=[bb])
            yield
            pv3 = pbb[:, 0:512].rearrange("p (h i) -> p h i", h=4)
            A(CALL("activation", out=LTm[0][:, hs, :], in_=pv3, func=AF.Copy), r=[bb], w=[hb_LT[0][hh]])
            V(CALL("scalar_tensor_tensor", out=Tt[0][:, hs, :], in0=pv3, scalar=-1.0,
                   in1=identb[:].unsqueeze(1).to_broadcast([128, 4, 128]), op0=ALU.mult, op1=ALU.add), r=[bb, b_c], w=[hb_Tt[0][hh]])
            yield
            ci = 0
            for lev in range(6):
                ni = 1 - ci
                last = (lev == 5)
                p0, _, bb0 = nb()
                for h4 in range(4):
                    h = hh * 4 + h4
                    T(CALL("matmul", out=p0[:, h4 * 128:(h4 + 1) * 128], lhsT=LTm[ci][:, h, :], rhs=Lm[ci][:, h, :], start=True, stop=True),
                      r=[hb_L[ci][hh], hb_LT[ci][hh]], w=[bb0])
                yield
                if not last:
                    q0, _, qb0 = nb()
                    for h4 in range(4):
                        h = hh * 4 + h4
                        T(CALL("matmul", out=q0[:, h4 * 128:(h4 + 1) * 128], lhsT=Lm[ci][:, h, :], rhs=LTm[ci][:, h, :], start=True, stop=True),
                          r=[hb_L[ci][hh], hb_LT[ci][hh]], w=[qb0])
                    yield
                V(CALL("tensor_copy", out=Lm[ni][:, hs, :], in_=v4(p0)), r=[bb0], w=[hb_L[ni][hh]])
                yield
                if not last:
                    A(CALL("activation", out=LTm[ni][:, hs, :], in_=v4(q0), func=AF.Copy), r=[qb0], w=[hb_LT[ni][hh]])
                    yield
                r0, _, rb0 = nb()
                for h4 in range(4):
                    h = hh * 4 + h4
                    o_ = r0[:, h4 * 128:(h4 + 1) * 128]
                    T(CALL("matmul", out=o_, lhsT=Lm[ni][:, h, :], rhs=Tt[ci][:, h, :], start=True, stop=False),
                      r=[hb_L[ni][hh], hb_Tt[ci][hh]], w=[rb0])
                    T(CALL("matmul", out=o_, lhsT=identb[:], rhs=Tt[ci][:, h, :], start=False, stop=True), r=[b_c, hb_Tt[ci][hh]], w=[rb0])
                yield
                if lev % 2 == 0:
                    A(CALL("activation", out=Tt[ni][:, hs, :], in_=v4(r0), func=AF.Copy), r=[rb0], w=[hb_Tt[ni][hh]])
                else:
                    V(CALL("tensor_copy", out=Tt[ni][:, hs, :], in_=v4(r0)), r=[rb0], w=[hb_Tt[ni][hh]])
                yield
                ci = ni
            TT, bTT = Tt[ci], hb_Tt[ci][hh]
            V(CALL("tensor_tensor", out=vbt[:, hs, :], in0=vm_[:, hs, :], in1=bc4(nsc[:, 32 + hh * 4:36 + hh * 4]), op=ALU.mult),
              r=[hb_vm[hh], b_nsc], w=[hb_vb[hh]])
            yield
            V(CALL("tensor_tensor", out=kbg[:, hs, :], in0=km_[:, hs, :], in1=bc4(gcs[:, 24 + hh * 4:28 + hh * 4]), op=ALU.mult),
              r=[hb_km[hh], b_gcs], w=[hb_kbg[hh]])
            yield
            V(CALL("tensor_tensor", out=kdt[:, hs, :], in0=km_[:, hs, :], in1=bc4(gcs[:, 32 + hh * 4:36 + hh * 4]), op=ALU.mult),
              r=[hb_km[hh], b_gcs], w=[hb_kd[hh]])
            yield
            p0, _, bb0 = nb()
            for h4 in range(4):
                h = hh * 4 + h4
                T(CALL("matmul", out=p0[:, h4 * 128:(h4 + 1) * 128], lhsT=kbg[:, h, :], rhs=TT[:, h, :], start=True, stop=True),
                  r=[hb_kbg[hh], bTT], w=[bb0])
            yield
            A(CALL("activation", out=nwT[:, hs, :], in_=v4(p0), func=AF.Copy, scale=-1.0), r=[bb0], w=[hb_nwT[hh]])
            yield
            if main:
                p0, _, bb0 = nb()
                for h4 in range(4):
                    h = hh * 4 + h4
                    o_ = p0[:, h4 * 128:(h4 + 1) * 128]
                    T(CALL("matmul", out=o_, lhsT=lstr[:], rhs=GU[:, h, :], start=True, stop=False), r=[b_GU, b_c], w=[bb0])
                    T(CALL("matmul", out=o_, lhsT=identf[:], rhs=negi[:], start=False, stop=True), r=[b_c], w=[bb0])
                yield
                A(CALL("activation", out=DT_[:, hs, :], in_=v4(p0), func=AF.Exp), r=[bb0], w=[hb_DT[hh]])
                yield
                p1, _, bb1 = nb()
                for h4 in range(4):
                    h = hh * 4 + h4
                    T(CALL("matmul", out=p1[:, h4 * 128:(h4 + 1) * 128], lhsT=kt_[:, h, :], rhs=qt_[:, h, :], start=True, stop=True),
                      r=[bkt, bqt], w=[bb1])
                yield
                V(CALL("tensor_tensor", out=intraT[:, hs, :], in0=v4(p1), in1=DT_[:, hs, :], op=ALU.mult), r=[bb1, hb_DT[hh]], w=[hb_intra[hh]])
                yield
            p0, _, bb0 = nb()
            for h4 in range(4):
                h = hh * 4 + h4
                o_ = p0[:, h4 * 128:(h4 + 1) * 128]
                T(CALL("matmul", out=o_, lhsT=TT[:, h, :], rhs=vbt[:, h, :], start=True, stop=False), r=[bTT, hb_vb[hh]], w=[bb0])
                T(CALL("matmul", out=o_, lhsT=nwT[:, h, :], rhs=Sbf[:, h, :], start=False, stop=True), r=[hb_nwT[hh], hb_Sb[hh]], w=[bb0])
            yield
            A(CALL("activation", out=vnew[:, hs, :], in_=v4(p0), func=AF.Copy), r=[bb0], w=[hb_vnew[hh]])
            yield
            if main:
                p0, _, bb0 = nb()
                for h4 in range(4):
                    h = hh * 4 + h4
                    T(CALL("matmul", out=p0[:, h4 * 128:(h4 + 1) * 128], lhsT=qt_[:, h, :], rhs=Sbf[:, h, :], start=True, stop=True),
                      r=[bqt, hb_Sb[hh]], w=[bb0])
                yield
                V(CALL("tensor_tensor", out=otm[:, hs, :], in0=v4(p0), in1=bc4(gcs[:, 16 + hh * 4:20 + hh * 4]), op=ALU.mult),
                  r=[bb0, b_gcs], w=[hb_otm[hh]])
                yield
                p1, _, bb1 = nb()
                for h4 in range(4):
                    h = hh * 4 + h4
                    T(CALL("matmul", out=p1[:, h4 * 128:(h4 + 1) * 128], lhsT=intraT[:, h, :], rhs=vnew[:, h, :], start=True, stop=True),
                      r=[hb_intra[hh], hb_vnew[hh]], w=[bb1])
                yield
                V(CALL("tensor_tensor", out=otm[:, hs, :], in0=v4(p1), in1=otm[:, hs, :], op=ALU.add), r=[bb1, hb_otm[hh]], w=[hb_otm[hh]])
                yield
            p2, _, bb2 = nb()
            for h4 in range(4):
                h = hh * 4 + h4
                T(CALL("matmul", out=p2[:, h4 * 128:(h4 + 1) * 128], lhsT=kdt[:, h, :], rhs=vnew[:, h, :], start=True, stop=True),
                  r=[hb_kd[hh], hb_vnew[hh]], w=[bb2])
            yield
            V(CALL("tensor_tensor", out=Sst[:, hs, :], in0=Sst[:, hs, :], in1=bc4(gcs[:, 40 + hh * 4:44 + hh * 4]), op=ALU.mult),
              r=[hb_S[hh], b_gcs], w=[hb_S[hh]])
            yield
            V(CALL("tensor_tensor", out=Sst[:, hs, :], in0=v4(p2), in1=Sst[:, hs, :], op=ALU.add), r=[bb2, hb_S[hh]], w=[hb_S[hh]])
            yield
            A(CALL("activation", out=Sbf[:, hs, :], in_=Sst[:, hs, :], func=AF.Copy), r=[hb_S[hh]], w=[hb_Sb[hh]])
            yield
            if main:
                for h4 in range(4):
                    h = hh * 4 + h4
                    A(CALL("activation", out=oab[:, h, :], in_=otm[:, h, :], func=AF.Square, accum_out=st1h[:, h:h + 1]),
                      r=[hb_otm[hh]], w=[hb_oab[hh], hb_st[hh]])
                yield
                sv = st1h[:, hh * 4:hh * 4 + 4]
                V(CALL("tensor_tensor", out=sv, in0=sv, in1=nsc[:, 48 + hh * 4:52 + hh * 4], op=ALU.mult), r=[hb_st[hh], b_nsc], w=[hb_st[hh]])
                rstd_from_ss(sv, 128, 1.0 / 128, [hb_st[hh]], [hb_st[hh]])
                V(CALL("tensor_tensor", out=sv, in0=sv, in1=nsc[:, 40 + hh * 4:44 + hh * 4], op=ALU.mult), r=[hb_st[hh], b_nsc], w=[hb_st[hh]])
                yield
                V(CALL("tensor_tensor", out=otm[:, hs, :], in0=otm[:, hs, :], in1=bc4(sv), op=ALU.mult), r=[hb_otm[hh], hb_st[hh]], w=[hb_otm[hh]])
                yield
                V(CALL("tensor_tensor", out=otm[:, hs, :], in0=otm[:, hs, :],
                       in1=rbc[:, RB_GDNOUT:RB_GDNOUT + 128].unsqueeze(1).to_broadcast([128, 4, 128]), op=ALU.mult),
                  r=[hb_otm[hh], b_rbc], w=[hb_otm[hh]])
                yield
                V(CALL("tensor_tensor", out=oab[:, hs, :], in0=otm[:, hs, :], in1=gate_t[:, hs, :], op=ALU.mult),
                  r=[hb_otm[hh], b_gate], w=[hb_oab[hh]])
                yield
                pb, pbb, bb = nb()
                for h4 in range(4):
                    h = hh * 4 + h4
                    T(CALL("transpose", out=pbb[:, h4 * 128:(h4 + 1) * 128], in_=oab[:, h, :], identity=identb[:]), r=[hb_oab[hh], b_c], w=[bb])
                yield
                mc = (c - 8) * 128
                A(CALL("activation", out=oaT[:, hs, :], in_=pbb[:, 0:512].rearrange("p (h i) -> p h i", h=4), func=AF.Copy),
                  r=[bb], w=[hb_oaT[hh]])
                b_mixT[(0, c - 8, hh)] = Buf("mix")
                DQ(CALL("dma_start", out=sc_mixT.ap()[hh * 4:hh * 4 + 4, :, mc:mc + 128].rearrange("h e t -> e h t"), in_=oaT[:, hs, :]),
                   r=[hb_oaT[hh]], w=[b_mixT[(0, c - 8, hh)]])
                yield

        def gdn_loads(c, main, cur):
            g0 = c * 128
            blk = (g0 // 512) * 512
            DQ(CALL("dma_start", out=kTc[cur], in_=sc_kT.ap()[:, :, g0:g0 + 128].rearrange("h d t -> d h t")),
               r=[b_sc_kT[(h, blk)] for h in range(8)], w=[b_kTc[cur]])
            DQ(CALL("dma_start", out=vTc_l[cur], in_=sc_vT.ap()[:, :, g0:g0 + 128].rearrange("h d t -> d h t")),
               r=[b_sc_v[(h, blk)] for h in range(8)], w=[b_vTc_l[cur]])
            if main:
                DQ(CALL("dma_start", out=qTc[cur], in_=sc_qT.ap()[:, :, g0 - TP:g0 - TP + 128].rearrange("h d t -> d h t")),
                   r=[b_sc_q[(h, blk)] for h in range(8)], w=[b_qTc[cur]])
                ti = c - 8
                DQ(CALL("dma_start", out=gate_l[cur], in_=sc_gate.ap()[ti * 128:(ti + 1) * 128, :].rearrange("t (h d) -> t h d", h=8)),
                   r=[b_sc_g[ti]], w=[b_gate_l[cur]])

        def gdn_chunk(c, main, cur):
            g0 = c * 128
            slot = c
            kt_, bkt = kTc[cur], b_kTc[cur]
            km_, bkm = ktm[cur], b_ktm[cur]
            vm_, bvm = vtm[cur], b_vtm[cur]
            qt_, bqt = qTc[cur], b_qTc[cur]
            vTc, b_vTc = vTc_l[cur], b_vTc_l[cur]
            gate_t, b_gate = gate_l[cur], b_gate_l[cur]
            gv = gb[:, slot, 0:8]
            lbv = gb[:, slot, 8:16]
            bev = gb[:, slot, 16:24]
            pb, _, bb = nb()
            T(CALL("matmul", out=pb[:, 0:8], lhsT=utri[:], rhs=gv, start=True, stop=True), r=[b_gb, b_c], w=[bb])
            T(CALL("matmul", out=pb[:, 8:16], lhsT=onesf[:], rhs=gv, start=True, stop=True), r=[b_gb, b_c], w=[bb])
            V(CALL("tensor_copy", out=gcs[:, 0:16], in_=pb[:, 0:16]), r=[bb], w=[b_gcs])
            A(CALL("activation", out=gcs[:, 16:24], in_=gcs[:, 0:8], func=AF.Exp), r=[b_gcs], w=[b_gcs])
            pn, _, bbn = nb()
            A(CALL("activation", out=sqs_, in_=kt_, func=AF.Square), r=[bkt], w=[b_sqs_])
            for h in range(8):
                T(CALL("matmul", out=pn[:, h:h + 1], lhsT=sqs_[:, h, :], rhs=onesb[:, 0:1], start=True, stop=True), r=[b_sqs_, b_c], w=[bbn])
            if main:
                V(CALL("tensor_tensor", out=sqs_, in0=qt_, in1=qt_, op=ALU.mult), r=[bqt], w=[b_sqs_])
                for h in range(8):
                    T(CALL("matmul", out=pn[:, 8 + h:9 + h], lhsT=sqs_[:, h, :], rhs=onesb[:, 0:1], start=True, stop=True), r=[b_sqs_, b_c], w=[bbn])
            nq = 16 if main else 8
            A(CALL("activation", out=nsc[:, 0:nq], in_=pn[:, 0:nq], func=AF.Ln, bias=epst[:, 0:1]), r=[bbn, b_c], w=[b_nsc])
            V(CALL("tensor_tensor", out=nsc[:, 16:24], in0=lbv, in1=nsc[:, 0:8], op=ALU.subtract), r=[b_gb, b_nsc], w=[b_nsc])
            A(CALL("activation", out=nsc[:, 24:32], in_=nsc[:, 16:24], func=AF.Exp), r=[b_nsc], w=[b_nsc])
            V(CALL("scalar_tensor_tensor", out=nsc[:, 56:64], in0=nsc[:, 0:8], scalar=-0.5, in1=lbv, op0=ALU.mult, op1=ALU.add),
              r=[b_nsc, b_gb], w=[b_nsc])
            A(CALL("activation", out=nsc[:, 32:40], in_=nsc[:, 56:64], func=AF.Exp), r=[b_nsc], w=[b_nsc])
            if main:
                A(CALL("activation", out=nsc[:, 40:48], in_=nsc[:, 8:16], func=AF.Exp, scale=-0.5), r=[b_nsc], w=[b_nsc])
                V(CALL("tensor_scalar_mul", out=nsc[:, 40:48], in0=nsc[:, 40:48], scalar1=128.0 ** -0.5), r=[b_nsc], w=[b_nsc])
                V(CALL("tensor_tensor", out=nsc[:, 48:56], in0=nsc[:, 40:48], in1=nsc[:, 40:48], op=ALU.mult), r=[b_nsc], w=[b_nsc])
            V(CALL("tensor_tensor", out=gcs[:, 24:32], in0=gcs[:, 16:24], in1=nsc[:, 24:32], op=ALU.mult), r=[b_gcs, b_nsc], w=[b_gcs])
            V(CALL("tensor_tensor", out=gcs[:, 32:40], in0=gcs[:, 8:16], in1=gcs[:, 0:8], op=ALU.subtract), r=[b_gcs], w=[b_gcs])
            A(CALL("activation", out=gcs[:, 32:40], in_=gcs[:, 32:40], func=AF.Exp), r=[b_gcs], w=[b_gcs])
            A(CALL("activation", out=gcs[:, 40:48], in_=gcs[:, 8:16], func=AF.Exp), r=[b_gcs], w=[b_gcs])
            V(CALL("tensor_tensor", out=GU, in0=utri[:].unsqueeze(1).to_broadcast([128, 8, 128]),
                   in1=gv.unsqueeze(2).to_broadcast([128, 8, 128]), op=ALU.mult), r=[b_c, b_gb], w=[b_GU])
            interleave([gdn_half(c, main, hh, kt_, bkt, km_, bkm, vm_, bvm, qt_, bqt, gv, lbv, bev, vTc, b_vTc, gate_t, b_gate)
                        for hh in range(2)], lag=4)

        def u_chunk(wt, bw, j, pc, blocks, first_zero_halo, si=0):
            Z = SETS[si]
            rw, brw = Z["uraw"], Z["b_uraw"]
            hk_, bhk = Z["hk"], Z["b_hk"]
            g = pc // 2
            win = 2 << g
            first = True
            for (c0, n, bk) in blocks:
                pb, pbb, bb = nb()
                tiles = list(range(c0 // 128, (c0 + n - 1) // 128 + 1))
                for kc in range(16):
                    T(CALL("matmul", out=pb[:, 0:n], lhsT=wt[:, kc * 256 + j * 128:kc * 256 + (j + 1) * 128],
                           rhs=hT[:, kc, c0:c0 + n], start=(kc == 0), stop=(kc == 15)),
                      r=[bw] + [b_hT[t] for t in tiles], w=[bb])
                yield
                if bk == "sample":
                    A(CALL("activation", out=rawtail[:, 24 + pc, 16:32], in_=pb[:, 0:16], func=AF.Copy), r=[bb], w=[b_rt])
                    yield
                    continue
                if bk == "halo":
                    A(CALL("activation", out=halo_u[:, pc, :], in_=pb[:, 1:16], func=AF.Copy), r=[bb], w=[b_hu])
                    yield
                    continue
                if first:
                    V(CALL("tensor_copy", out=rw[:, 0:15], in_=halo_u[:, pc, :]), r=[b_hu], w=[brw])
                else:
                    V(CALL("tensor_copy", out=rw[:, 0:15], in_=hk_[:, 0:15]), r=[bhk], w=[brw])
                first = False
                A(CALL("activation", out=rw[:, 15:15 + n], in_=pb[:, 0:n], func=AF.Copy), r=[bb], w=[brw])
                yield
                V(CALL("tensor_copy", out=hk_[:, 0:15], in_=rw[:, n:n + 15]), r=[brw], w=[bhk])
                if bk == "lastmain":
                    V(CALL("tensor_copy", out=rawtail[:, 24 + pc, 0:16], in_=rw[:, 15 + n - 16:15 + n]), r=[brw], w=[b_rt])
                L = 15 + n
                src, bsrc = rw, brw
                k = 1
                flip = 0
                while k < win:
                    dst, bdst = (Z["uA"], Z["b_uA"]) if flip == 0 else (Z["uB"], Z["b_uB"])
                    V(CALL("tensor_tensor", out=dst[:, 2 * k - 1:L], in0=src[:, 2 * k - 1:L], in1=src[:, k - 1:L - k], op=ALU.add),
                      r=[bsrc], w=[bdst])
                    yield
                    src, bsrc = dst, bdst
                    flip = 1 - flip
                    k *= 2
                V(CALL("scalar_tensor_tensor", out=poolT[:, pc, c0:c0 + n], in0=src[:, 15:15 + n], scalar=1.0 / win, in1=rw[:, 15:15 + n],
                       op0=ALU.mult, op1=ALU.subtract), r=[bsrc, brw], w=[b_poolT[pc]])
                yield
                if c0 == 0:
                    fx, bfx = Z["rs"], Z["b_rs"]
                    V(CALL("tensor_tensor", out=fx[:, 0:16], in0=src[:, 15:31], in1=rbc[:, RB_INVC + g * 16:RB_INVC + g * 16 + 16], op=ALU.mult),
                      r=[bsrc, b_rbc], w=[bfx])
                    V(CALL("tensor_tensor", out=poolT[:, pc, 0:16], in0=fx[:, 0:16], in1=rw[:, 15:31], op=ALU.subtract),
                      r=[bfx, brw], w=[b_poolT[pc]])
                    yield

        def pool_proj(g, blocks):
            for dch in range(2):
                for (c0, n) in blocks:
                    pb, pbb, bb = nb()
                    for c2 in range(2):
                        T(CALL("matmul",
                            out=pb[:, 0:n], lhsT=wpl[:, g, c2, dch * 128:(dch + 1) * 128],
                            rhs=poolT[:, 2 * g + c2, c0:c0 + n], start=(c2 == 0), stop=(c2 == 1)),
                          r=[b_wpl, b_poolT[2 * g], b_poolT[2 * g + 1]], w=[bb])
                    fi = nxt("fmo")
                    fo, bfo = fmo[fi], b_fmo[fi]
                    A(CALL("activation",
                        out=fo[:, 0:n], in_=pb[:, 0:n], func=AF.Copy,
                        scale=pvec[:, PV_PSCALE + 2 * g + dch:PV_PSCALE + 2 * g + dch + 1]), r=[bb, b_pv], w=[bfo])
                    key = (8 + 2 * g + dch, c0)
                    b_mixT[key] = Buf("mixb")
                    DQ(CALL("dma_start", out=sc_mixT.ap()[8 + 2 * g + dch, :, c0:c0 + n], in_=fo[:, 0:n]),
                       r=[bfo], w=[b_mixT[key]])

        def gate_unit(wt, bw, u4):
            for i in range(9):
                rows = 128 if i < 8 else TS
                pb, pbb, bb = nb()
                for kc in range(16):
                    T(CALL("matmul",
                        out=pb[0:rows, 0:256], lhsT=hT[:, kc, i * 128:i * 128 + rows], rhs=wt[:, kc * 256:(kc + 1) * 256],
                        start=(kc == 0), stop=(kc == 15)), r=[bw, b_hT[i]], w=[bb])
                gi = nxt("gto")
                go, bgo = gto[gi], b_gto[gi]
                A(CALL("activation", out=go[0:rows, 0:256], in_=pb[0:rows, 0:256], func=AF.Silu),
                  r=[bb], w=[bgo])
                DQ(CALL("dma_start", out=sc_gate.ap()[i * 128:i * 128 + rows, u4 * 256:(u4 + 1) * 256],
                                                              in_=go[0:rows, 0:256]), r=[bgo], w=[b_sc_g[i]])

        GQ(CALL("dma_start", out=wab[:], in_=w_in.ap()[:, 4096:4112].rearrange("(k p) n -> p k n", p=128)), w=[b_wab])
        GQ(CALL("dma_start", out=wpl[:], in_=w_pool.ap().rearrange("g (c p) d -> p g c d", p=128)), w=[b_wpl])
        V(CALL("memset", Sst[:], 0.0), w=hb_S)
        V(CALL("memset", Sbf[:], 0.0), w=hb_Sb)
        for i2 in range(4):
            gens = []
            for xi in range(2):
                i = i2 * 2 + xi
                DQ(CALL("dma_start", out=xt[xi][:], in_=xall.ap()[i * 128:(i + 1) * 128, :]), w=[b_xt[xi]])
                gens.append(norm_transpose(xt[xi][:], [b_xt[xi]], 128, i * 128, i, PV_PREMIX, xi))
            interleave(gens)
        ab_tiles([(i, i, 128) for i in range(8)])
        pre_blocks = [(0, 512, 0, "n"), (512, 512, 512, "lastpre")]
        for u in range(4):
            wt, bw = load_w(w_in_unit(u * 256))
            interleave([qkv_chunk(wt, bw, j, u * 2 + j, [(1008, 16, None, "halo")], True, j) for j in range(2)])
        qkv_stream([(kind, u) for kind in (1, 2) for u in range(4)], pre_blocks, True)
        for u in range(4):
            wt, bw = load_w(w_in_unit(4112 + u * 256))
            interleave([u_chunk(wt, bw, j, u * 2 + j, [(1008, 16, "halo")], True, j) for j in range(2)])
        inherit(ALT_BUFS, INPROJ_LOW)
        gdn_loads(0, False, 0)
        for c in range(8):
            if c + 1 < 8:
                gdn_loads(c + 1, False, (c + 1) % 2)
            gdn_chunk(c, False, c % 2)
        inherit(INPROJ_LOW, ALT_BUFS)

        for i2 in range(5):
            gens = []
            for xi in range(2):
                i = i2 * 2 + xi
                if i > 8:
                    continue
                rows = 128 if i < 8 else TS
                DQ(CALL("dma_start", out=xt[xi][0:rows, :], in_=xall.ap()[TP + i * 128:TP + i * 128 + rows, :]), w=[b_xt[xi]])
                gens.append(norm_transpose(xt[xi][0:rows, :], [b_xt[xi]], rows, i * 128, i, PV_PREMIX, xi))
            interleave(gens)
        ab_tiles([(i, 8 + i, 128 if i < 8 else TS) for i in range(9)])
        main_blocks = [(0, 512, TP, "n"), (512, 512, TP + 512, "lastmain"), (1024, 16, None, "sample")]
        qkv_stream([(kind, u) for kind in (0, 1, 2) for u in range(4)], main_blocks, False)
        for u in range(4):
            wt, bw = load_w(w_in_unit(3072 + u * 256))
            gate_unit(wt, bw, u)
        for u in range(4):
            wt, bw = load_w(w_in_unit(4112 + u * 256))
            interleave([u_chunk(wt, bw, j, u * 2 + j, [(0, 512, "n"), (512, 512, "lastmain"), (1024, 16, "sample")], False, j)
                        for j in range(2)])
            pool_proj(u, [(0, 512), (512, 512)])

        tmT = arena2[:, 8 * NM:8 * NM + 8192].bitcast(F32) if False else None
        for half in range(2):
            for q4 in range(4):
                pb, pbb, bb = nb()
                for k in range(4):
                    chn = half * 16 + q4 * 4 + k
                    T(CALL("transpose", out=pb[0:32, k * 128:(k + 1) * 128], in_=rawtail[:, chn, :],
                                                                 identity=identf[:]), r=[b_rt, b_c], w=[bb])
                V(CALL("tensor_copy", out=xt[half][0:32, q4 * 512:(q4 + 1) * 512], in_=pb[0:32, :]),
                  r=[bb], w=[b_xt[half]])
        DQ(CALL("dma_start", out=o_conv_p.ap()[:, 0:2048], in_=xt[0][13:16, :]), r=[b_xt[0]], stream="out")
        DQ(CALL("dma_start", out=o_conv_p.ap()[:, 2048:3072], in_=xt[1][13:16, 0:1024]), r=[b_xt[1]], stream="out")
        DQ(CALL("dma_start", out=o_pool_p.ap(), in_=xt[1][1:16, 1024:2048]), r=[b_xt[1]], stream="out")
        DQ(CALL("dma_start", out=o_conv_s.ap()[:, 2, 0:2048], in_=xt[0][16:32, :]), r=[b_xt[0]], stream="out")
        DQ(CALL("dma_start", out=o_conv_s.ap()[:, 2, 2048:3072], in_=xt[1][16:32, 0:1024]), r=[b_xt[1]], stream="out")
        DQ(CALL("dma_start", out=o_pool_s.ap()[:, 14, :], in_=xt[1][16:32, 1024:2048]), r=[b_xt[1]], stream="out")
        DQ(CALL("dma_start", out=o_conv_s.ap()[:, 0:2, :], in_=st_conv.ap().rearrange("(t j) c -> t j c", j=3)[:, 1:3, :]), stream="out")
        DQ(CALL("dma_start", out=o_pool_s.ap()[:, 0:14, :], in_=st_pool.ap().rearrange("(t j) c -> t j c", j=15)[:, 1:15, :]), stream="out")

        a2f = arena2.bitcast(F32)
        stc = a2f[0:48, 0:3072]; b_stc = Buf("stc")
        inherit([b_stc], b_poolT + A2HI)
        DQ(CALL("dma_start", out=stc, in_=st_conv.ap()), w=[b_stc])
        stT = a2f[:, 5120:6272].rearrange("p (c t) -> p c t", c=24); b_stT = Buf("stT")
        inherit([b_stT], b_poolT + A2HI)
        for q6 in range(6):
            pb, pbb, bb = nb()
            for k in range(4):
                chn = q6 * 4 + k
                T(CALL("transpose", out=pb[:, k * 48:(k + 1) * 48], in_=stc[:, chn * 128:(chn + 1) * 128],
                                                             identity=identf[0:48, 0:48]), r=[b_stc, b_c], w=[bb])
            V(CALL("tensor_copy", out=stT[:, q6 * 4:q6 * 4 + 4, :], in_=pb[:, 0:192].rearrange("p (k t) -> p k t", k=4)),
              r=[bb], w=[b_stT])
        cwv = pvec[:, PV_CONV:PV_CONV + 96].rearrange("p (c j) -> p c j", j=4)
        stv = stT.rearrange("p c (t j) -> p c t j", j=3)
        sacc = sb("sacc", [128, 24, 16], F32); b_sacc = Buf("sacc")
        stmp = sb("stmp", [128, 24, 16], F32); b_stmp = Buf("stmp")
        V(CALL("tensor_tensor", out=sacc[:], in0=rawtail[:, 0:24, 16:32], in1=cwv[:, :, 3:4].to_broadcast([128, 24, 16]), op=ALU.mult),
          r=[b_rt, b_pv], w=[b_sacc])
        for jj in range(3):
            V(CALL("tensor_tensor", out=stmp[:], in0=stv[:, :, :, jj], in1=cwv[:, :, jj:jj + 1].to_broadcast([128, 24, 16]), op=ALU.mult),
              r=[b_stT, b_pv], w=[b_stmp])
            V(CALL("tensor_tensor", out=sacc[:], in0=sacc[:], in1=stmp[:], op=ALU.add), r=[b_sacc, b_stmp], w=[b_sacc])
        A(CALL("activation", out=stmp[:], in_=sacc[:], func=AF.Exp, scale=-1.0), r=[b_sacc], w=[b_stmp])
        A(CALL("activation", out=stmp[:], in_=stmp[:], func=AF.Ln, bias=1.0), r=[b_stmp], w=[b_stmp])
        A(CALL("activation", out=stmp[:], in_=stmp[:], func=AF.Exp, scale=-1.0), r=[b_stmp], w=[b_stmp])
        V(CALL("tensor_tensor", out=smp[:], in0=sacc[:], in1=stmp[:], op=ALU.mult), r=[b_sacc, b_stmp], w=[b_smp])
        sqs = sb("sqs", [128, 16, 16], BF16); b_sqs = Buf("sqs")
        V(CALL("tensor_tensor", out=sqs[:], in0=smp[:, 0:16, :], in1=smp[:, 0:16, :], op=ALU.mult), r=[b_smp], w=[b_sqs])
        pb, pbb, bb = nb()
        T(CALL("matmul", out=pb[:, 0:256], lhsT=onesb[:], rhs=sqs[:, :, :].rearrange("p c t -> p (c t)"), start=True, stop=True),
          r=[b_sqs, b_c], w=[bb])
        A(CALL("activation", out=stmp[:, 0:16, :].rearrange("p c t -> p (c t)"), in_=pb[:, 0:256], func=AF.Ln, bias=epst[:, 0:1]),
          r=[bb, b_c], w=[b_stmp])
        A(CALL("activation", out=stmp[:, 0:16, :], in_=stmp[:, 0:16, :], func=AF.Exp, scale=-0.5), r=[b_stmp], w=[b_stmp])
        V(CALL("tensor_tensor", out=smp[:, 0:16, :], in0=smp[:, 0:16, :], in1=stmp[:, 0:16, :], op=ALU.mult), r=[b_smp, b_stmp], w=[b_smp])
        V(CALL("tensor_scalar_mul", out=smp[:, 0:8, :], in0=smp[:, 0:8, :], scalar1=128.0 ** -0.5), r=[b_smp], w=[b_smp])
        spc = a2f[0:120, 3072:5120].rearrange("p (a c) -> p a c", a=2); b_spc = Buf("spc")
        inherit([b_spc], b_poolT + A2HI)
        DQ(CALL("dma_start", out=spc, in_=st_pool.ap().rearrange("(a r) c -> r a c", a=2)), w=[b_spc])
        spT = a2f[:, 6272:8192].rearrange("p (c t j) -> p c t j", c=8, j=15); b_spT = Buf("spT")
        inherit([b_spT], b_poolT + A2HI)
        for a_ in range(2):
            for q2 in range(2):
                pb, pbb, bb = nb()
                for k in range(4):
                    chn = q2 * 4 + k
                    T(CALL("transpose", out=pb[:, k * 120:(k + 1) * 120], in_=spc[:, a_, chn * 128:(chn + 1) * 128],
                                                                      identity=identf[0:120, 0:120]), r=[b_spc, b_c], w=[bb])
                V(CALL("tensor_copy",
                    out=spT[:, q2 * 4:q2 * 4 + 4, a_ * 8:a_ * 8 + 8, :],
                    in_=pb[:, 0:480].rearrange("p (k t j) -> p k t j", k=4, j=15)), r=[bb], w=[b_spT])
        spl = sb("spl", [128, 8, 16], F32); b_spl = Buf("spl")
        spb = sb("spb", [128, 8, 16], BF16); b_spb = Buf("spb")
        for g in range(4):
            win = 2 << g
            V(CALL("tensor_reduce", out=spl[:, 2 * g:2 * g + 2, :], in_=spT[:, 2 * g:2 * g + 2, :, 15 - (win - 1):15],
                                                      axis=mybir.AxisListType.X, op=ALU.add), r=[b_spT], w=[b_spl])
            V(CALL("tensor_tensor", out=spl[:, 2 * g:2 * g + 2, :], in0=spl[:, 2 * g:2 * g + 2, :],
                                             in1=rawtail[:, 24 + 2 * g:24 + 2 * g + 2, 16:32], op=ALU.add), r=[b_spl, b_rt], w=[b_spl])
            V(CALL("scalar_tensor_tensor", out=spb[:, 2 * g:2 * g + 2, :], in0=spl[:, 2 * g:2 * g + 2, :], scalar=1.0 / win,
                                                             in1=rawtail[:, 24 + 2 * g:24 + 2 * g + 2, 16:32], op0=ALU.mult, op1=ALU.subtract),
              r=[b_spl, b_rt], w=[b_spb])
        for g in range(4):
            for dch in range(2):
                pb, pbb, bb = nb()
                for c2 in range(2):
                    T(CALL("matmul", out=pb[:, 0:16], lhsT=wpl[:, g, c2, dch * 128:(dch + 1) * 128],
                                                                     rhs=spb[:, 2 * g + c2, :], start=(c2 == 0), stop=(c2 == 1)),
                      r=[b_wpl, b_spb], w=[bb])
                fi = nxt("fmo")
                fo, bfo = fmo[fi], b_fmo[fi]
                A(CALL("activation", out=fo[:, 0:16], in_=pb[:, 0:16], func=AF.Copy,
                                                                     scale=pvec[:, PV_PSCALE + 2 * g + dch:PV_PSCALE + 2 * g + dch + 1]),
                  r=[bb, b_pv], w=[bfo])
                key = (8 + 2 * g + dch, 1024)
                b_mixT[key] = Buf("mixs")
                DQ(CALL("dma_start", out=sc_mixT.ap()[8 + 2 * g + dch, :, 1024:1040], in_=fo[:, 0:16]),
                   r=[bfo], w=[b_mixT[key]])

        inherit(ALT_BUFS, INPROJ_LOW)
        gdn_loads(8, True, 0)
        for c in range(8, 16):
            if c + 1 < 16:
                gdn_loads(c + 1, True, (c + 1) % 2)
            gdn_chunk(c, True, c % 2)
        DQ(CALL("dma_start", out=o_gdn_p.ap().rearrange("h d e -> d h e"), in_=Sst[:]), r=hb_S, stream="out")

        A1ALL = [b_raw[0], b_raw[1], b_acc, b_sqb, b_rs, b_fmo[0], b_fmo[1], b_tmo[0], b_tmo[1], b_gto[0], b_uA, b_uB,
                 b_uraw[0], b_uraw[1], b_kTc[0], b_qTc[0], b_ktm[0], b_vtm[0], b_GU, b_bD, b_DT, b_Lm[0], b_Lm[1],
                 b_LTm[0], b_LTm[1], b_Tt[0], b_Tt[1], b_kbg, b_kd, b_nwT, b_vnew, b_vb, b_intra, b_otm, b_oab, b_oaT,
                 b_gcs, b_sqs_] + b_gate_l + b_vTc_l + b_kTc + b_qTc + list(GDN_BUFS) + hb_km + hb_vm
        gs = gb[0:16, 16, 0:8]
        bs = gb[0:16, 16, 16:24]
        gd = sb("gd", [16, 2, 8, 16], F32); b_gd = Buf("gd")
        i16 = identf[0:16, 0:16]
        V(CALL("tensor_tensor", out=gd[:, 0, :, :], in0=i16.unsqueeze(1).to_broadcast([16, 8, 16]),
                                    in1=gs.unsqueeze(2).to_broadcast([16, 8, 16]), op=ALU.mult), r=[b_gb, b_c], w=[b_gd])
        V(CALL("tensor_tensor", out=gd[:, 1, :, :], in0=i16.unsqueeze(1).to_broadcast([16, 8, 16]),
                                    in1=bs.unsqueeze(2).to_broadcast([16, 8, 16]), op=ALU.mult), r=[b_gb, b_c], w=[b_gd])
        pb, pbb, bb = nb()
        T(CALL("matmul", out=pb[:, 0:256], lhsT=onesf[0:16, :], rhs=gd[:, :, :, :].rearrange("t a h u -> t (a h u)"),
                                    start=True, stop=True), r=[b_gd, b_c], w=[bb])
        egb = sb("egb", [128, 2, 8, 16], F32); b_egb = Buf("egb")
        A(CALL("activation", out=egb[:, 0, :, :].rearrange("p h t -> p (h t)"), in_=pb[:, 0:128], func=AF.Exp), r=[bb], w=[b_egb])
        V(CALL("tensor_copy", out=egb[:, 1, :, :].rearrange("p h t -> p (h t)"), in_=pb[:, 128:256]), r=[bb], w=[b_egb])
        qk = arena1[:, 18304:18432].rearrange("p (h t) -> p h t", h=8); b_qk = Buf("qk")
        inherit([b_qk], A1ALL)
        V(CALL("tensor_tensor", out=qk, in0=smp[:, 0:8, :], in1=smp[:, 8:16, :], op=ALU.mult), r=[b_smp], w=[b_qk])
        pb2, _, bb2 = nb()
        T(CALL("matmul", out=pb2[:, 0:128], lhsT=onesf[:], rhs=qk.rearrange("p h t -> p (h t)"), start=True, stop=True),
          r=[b_qk, b_c], w=[bb2])
        V(CALL("tensor_copy", out=qk.rearrange("p h t -> p (h t)"), in_=pb2[:, 0:128]), r=[bb2], w=[b_qk])
        kq2 = arena1[:, 18432:18688].rearrange("p (h t a) -> p h t a", h=8, a=2); b_kq2 = Buf("kq2")
        inherit([b_kq2], A1ALL)
        V(CALL("tensor_copy", out=kq2[:, :, :, 0], in_=smp[:, 8:16, :]), r=[b_smp], w=[b_kq2])
        V(CALL("tensor_copy", out=kq2[:, :, :, 1], in_=smp[:, 0:8, :]), r=[b_smp], w=[b_kq2])
        SS = [arena1[:, 0:4096].rearrange("p (n e) -> p n e", e=128), arena1[:, 4096:8192].rearrange("p (n e) -> p n e", e=128)]
        b_SS = [Buf("SS0"), Buf("SS1")]
        inherit(b_SS, A1ALL)
        ksq = arena1[:, 18688:18944].rearrange("p (h t a) -> p h t a", h=8, a=2); b_ksq = Buf("ksq")
        inherit([b_ksq], A1ALL)
        for qtr in range(4):
            si = qtr % 2
            DQ(CALL("dma_start", out=SS[si].rearrange("p (t h) e -> p t h e", h=8),
                                                    in_=st_gdn.ap()[qtr * 4:(qtr + 1) * 4].rearrange("t h d e -> d t h e")),
               w=[b_SS[si]])
            pb, pbb, bb = nb()
            for tl in range(4):
                t = qtr * 4 + tl
                for h in range(8):
                    n_ = tl * 8 + h
                    T(CALL("matmul", out=pb[:, n_ * 2:n_ * 2 + 2], lhsT=SS[si][:, n_, :], rhs=kq2[:, h, t, :],
                                                                       start=True, stop=True), r=[b_SS[si], b_kq2], w=[bb])
            V(CALL("tensor_copy", out=ksq[:, :, qtr * 4:(qtr + 1) * 4, :],
                                                      in_=pb[:, 0:64].rearrange("p (t h a) -> p h t a", h=8, a=2)), r=[bb], w=[b_ksq])
            if qtr == 3:
                break
        dl = arena1[:, 17920:18048].rearrange("p (h t) -> p h t", h=8); b_dl = Buf("dl")
        osT = arena1[:, 18048:18176].rearrange("p (h t) -> p h t", h=8); b_osT = Buf("osT")
        inherit([b_dl, b_osT], A1ALL)
        V(CALL("tensor_tensor", out=dl, in0=ksq[:, :, :, 0], in1=egb[:, 0, :, :], op=ALU.mult), r=[b_ksq, b_egb], w=[b_dl])
        V(CALL("tensor_tensor", out=dl, in0=smp[:, 16:24, :], in1=dl, op=ALU.subtract), r=[b_smp, b_dl], w=[b_dl])
        V(CALL("tensor_tensor", out=dl, in0=dl, in1=egb[:, 1, :, :], op=ALU.mult), r=[b_dl, b_egb], w=[b_dl])
        V(CALL("tensor_tensor", out=osT, in0=ksq[:, :, :, 1], in1=egb[:, 0, :, :], op=ALU.mult), r=[b_ksq, b_egb], w=[b_osT])
        V(CALL("tensor_tensor", out=qk, in0=qk, in1=dl, op=ALU.mult), r=[b_qk, b_dl], w=[b_qk])
        V(CALL("tensor_tensor", out=osT, in0=osT, in1=qk, op=ALU.add), r=[b_osT, b_qk], w=[b_osT])
        V(CALL("tensor_tensor", out=sqs[:, 0:8, :], in0=osT, in1=osT, op=ALU.mult), r=[b_osT], w=[b_sqs])
        pb, pbb, bb = nb()
        T(CALL("matmul", out=pb[:, 0:128], lhsT=onesb[:], rhs=sqs[:, 0:8, :].rearrange("p h t -> p (h t)"), start=True, stop=True),
          r=[b_sqs, b_c], w=[bb])
        rso = arena1[:, 18176:18304].rearrange("p (h t) -> p h t", h=8); b_rso = Buf("rso")
        inherit([b_rso], A1ALL)
        V(CALL("tensor_scalar", out=rso.rearrange("p h t -> p (h t)"), in0=pb[:, 0:128], scalar1=1.0 / 128, scalar2=EPS,
                                           op0=ALU.mult, op1=ALU.add), r=[bb], w=[b_rso])
        A(CALL("activation", out=rso, in_=rso, func=AF.Ln), r=[b_rso], w=[b_rso])
        A(CALL("activation", out=rso, in_=rso, func=AF.Exp, scale=-0.5), r=[b_rso], w=[b_rso])
        V(CALL("tensor_tensor", out=osT, in0=osT, in1=rso, op=ALU.mult), r=[b_osT, b_rso], w=[b_osT])
        V(CALL("tensor_scalar_mul", out=osT, in0=osT, scalar1=pvec[:, PV_GDNOUT:PV_GDNOUT + 1]), r=[b_osT, b_pv], w=[b_osT])
        gsm = arena1b[0:16, 2 * 17408:2 * 17408 + 1024]; b_gsm = Buf("gsm")
        inherit([b_gsm], A1ALL)
        DQ(CALL("dma_start", out=gsm[:], in_=sc_gate.ap()[1024:1040, :]), r=[b_sc_g[8]], w=[b_gsm])
        pb, pbb, bb = nb()
        for h in range(8):
            T(CALL("transpose", out=pbb[:, h * 16:(h + 1) * 16], in_=gsm[:, h * 128:(h + 1) * 128], identity=identb[0:16, 0:16]),
              r=[b_gsm, b_c], w=[bb])
        osb = sb("osb", [128, 8, 16], BF16); b_osb = Buf("osb")
        V(CALL("tensor_tensor", out=osb[:], in0=osT, in1=pbb[:, 0:128].rearrange("p (h t) -> p h t", h=8), op=ALU.mult),
          r=[b_osT, bb], w=[b_osb])
        b_mixT[(0, 8)] = Buf("mixs0")
        DQ(CALL("dma_start", out=sc_mixT.ap()[0:8, :, 1024:1040].rearrange("h e t -> e h t"), in_=osb[:]), r=[b_osb], w=[b_mixT[(0, 8)]])
        dlb = sb("dlb", [128, 8, 16], BF16); b_dlb = Buf("dlb")
        V(CALL("tensor_copy", out=dlb[:], in_=dl), r=[b_dl], w=[b_dlb])
        ksb = sb("ksb", [128, 8, 16], BF16); b_ksb = Buf("ksb")
        V(CALL("tensor_copy", out=ksb[:], in_=smp[:, 8:16, :]), r=[b_smp], w=[b_ksb])
        ktmS = arena1b[0:16, 2 * 16384:2 * 16384 + 1024].rearrange("p (h d) -> p h d", h=8); b_ktmS = Buf("ktmS")
        dtmS = arena1b[0:16, 2 * 16896:2 * 16896 + 1024].rearrange("p (h d) -> p h d", h=8); b_dtmS = Buf("dtmS")
        inherit([b_ktmS, b_dtmS], A1ALL)
        for (srcT, dstT, bsrc, bdst) in ((ksb, ktmS, b_ksb, b_ktmS), (dlb, dtmS, b_dlb, b_dtmS)):
            pb, pbb, bb = nb()
            for h in range(8):
                T(CALL("transpose", out=pbb[0:16, h * 128:(h + 1) * 128], in_=srcT[:, h, :], identity=identb[:]),
                  r=[bsrc, b_c], w=[bb])
            V(CALL("tensor_copy", out=dstT[:], in_=pbb[0:16, 0:1024].rearrange("p (h d) -> p h d", h=8)), r=[bb], w=[bdst])
        dmask = arena1b[0:16, 2 * 8192:2 * 16384].rearrange("p (t h e) -> p t h e", t=16, h=8); b_dmask = Buf("dmask")
        inherit([b_dmask], A1ALL)
        for t in range(16):
            V(CALL("tensor_scalar_mul", out=dmask[:, t, :, :], in0=dtmS[:], scalar1=identf[0:16, t:t + 1]), r=[b_dtmS, b_c], w=[b_dmask])
        for qtr in range(4):
            si = qtr % 2
            DQ(CALL("dma_start", out=SS[si].rearrange("p (t h) e -> p t h e", h=8),
                                                    in_=st_gdn.ap()[qtr * 4:(qtr + 1) * 4].rearrange("t h d e -> d t h e")),
               w=[b_SS[si]])
            for tl in range(4):
                t = qtr * 4 + tl
                for hh in range(2):
                    pb, pbb, bb = nb()
                    for h4 in range(4):
                        h = hh * 4 + h4
                        T(CALL("matmul", out=pb[:, h4 * 128:(h4 + 1) * 128], lhsT=ktmS[:, h, :], rhs=dmask[:, t, h, :],
                                                                     start=True, stop=True), r=[b_ktmS, b_dmask], w=[bb])
                    sv = SS[si][:, tl * 8 + hh * 4:tl * 8 + hh * 4 + 4, :]
                    V(CALL("tensor_tensor", out=sv, in0=sv, in1=egb[:, 0, hh * 4:hh * 4 + 4, t:t + 1].to_broadcast([128, 4, 128]),
                                                                   op=ALU.mult), r=[b_SS[si], b_egb], w=[b_SS[si]])
                    V(CALL("tensor_tensor", out=sv, in0=sv, in1=pb[:, :].rearrange("p (h e) -> p h e", h=4), op=ALU.add),
                      r=[b_SS[si], bb], w=[b_SS[si]])
            DQ(CALL("dma_start", out=o_gdn_s.ap()[qtr * 4:(qtr + 1) * 4].rearrange("t h d e -> d t h e"),
                                                    in_=SS[si].rearrange("p (t h) e -> p t h e", h=8)), r=[b_SS[si]], stream="out")

        ffa = arena1[:, 0:9 * 2048].rearrange("p (i c) -> p i c", i=9)
        b_ffa = [Buf("ffa%d" % i) for i in range(9)]
        inherit(b_ffa, b_SS + A1ALL + [b_gsm, b_ktmS, b_dtmS, b_dmask, b_qk, b_kq2, b_ksq, b_dl, b_osT, b_rso])
        mxa = arena2[:, :].rearrange("p (k t) -> p k t", k=16)
        b_mxa = Buf("mxa")
        inherit([b_mxa], b_poolT + [b_stc, b_stT, b_spc, b_spT] + A2HI)
        DQ(CALL("dma_start", out=mxa, in_=sc_mixT.ap().rearrange("k e t -> e k t")), r=list(b_mixT.values()), w=[b_mxa])
        nrm = xt[1]; b_nrm = b_xt[1]
        DQ(CALL("dma_start", out=nrm[:], in_=nrm_post.ap()[0]), w=[b_nrm])

        def w_unit(src, c0):
            def f(t_):
                return t_[:, :].rearrange("p (k n) -> p k n", k=16)
            f.src = src.ap()[:, c0:c0 + 256].rearrange("(k p) n -> p k n", p=128)
            return f

        for n8 in range(8):
            wt, bw = load_w(w_unit(w_out, n8 * 256))
            for i in range(9):
                rows = 128 if i < 8 else TS
                pb, pbb, bb = nb()
                for kc in range(16):
                    T(CALL("matmul",
                        out=pb[0:rows, 0:256], lhsT=mxa[:, kc, i * 128:i * 128 + rows], rhs=wt[:, kc * 256:(kc + 1) * 256],
                        start=(kc == 0), stop=(kc == 15)), r=[bw, b_mxa], w=[bb])
                A(CALL("activation", out=ffa[0:rows, i, n8 * 256:(n8 + 1) * 256], in_=pb[0:rows, 0:256], func=AF.Copy),
                  r=[bb], w=[b_ffa[i]])

        xt2 = arena2[:, 0:4096].bitcast(F32); b_xt2 = Buf("xt2")
        RES = [(xt[0], b_xt[0]), (xt2, b_xt2)]
        b_y = [Buf("y%d" % i) for i in range(9)]

        def pnr_chain(i, si, phase):
            rows = 128 if i < 8 else TS
            xb_, bxb, stc_, bst = NT_SETS[si]
            rt, brt = RES[si]
            fv = ffa[0:rows, i, :]
            if phase == 4:
                DQ(CALL("dma_start", out=rt[0:rows, :], in_=xall.ap()[TP + i * 128:TP + i * 128 + rows, :]), w=[brt])
            else:
                DQ(CALL("dma_start", out=rt[0:rows, :], in_=y_main.ap()[i * 128:i * 128 + rows, :]), r=[b_y[i]], w=[brt])
            A(CALL("activation", out=xb_[0:rows, :], in_=fv, func=AF.Square, accum_out=stc_[0:rows, 0:1]), r=[b_ffa[i]], w=[bxb, bst])
            yield
            rstd_from_ss(stc_[0:rows, 0:1], rows, 1.0 / D, [bst], [bst])
            yield
            V(CALL("scalar_tensor_tensor", out=fv, in0=fv, scalar=stc_[0:rows, 0:1], in1=nrm[0:rows, :], op0=ALU.mult, op1=ALU.mult),
              r=[b_ffa[i], bst, b_nrm], w=[b_ffa[i]])
            yield
            V(CALL("tensor_tensor", out=fv, in0=fv, in1=rt[0:rows, :], op=ALU.add), r=[b_ffa[i], brt], w=[b_ffa[i]])
            yield
            if phase == 4:
                DQ(CALL("dma_start", out=y_main.ap()[i * 128:i * 128 + rows, :], in_=fv), r=[b_ffa[i]], w=[b_y[i]])
                for _ in norm_transpose(fv, [b_ffa[i]], rows, i * 128, i, PV_PREMLP, si):
                    yield
            else:
                DQ(CALL("dma_start", out=y_main.ap()[i * 128:i * 128 + rows, :], in_=fv), r=[b_ffa[i]], w=[b_y[i]], stream="out")

        inherit([b_xt2, b_xb2], [b_mxa])
        for i2 in range(5):
            interleave([pnr_chain(i, si, 4) for si, i in enumerate((2 * i2, 2 * i2 + 1)) if i < 9], lag=3)

        upT = [arena2[:, 0:8 * NM].rearrange("p (c t) -> p c t", c=8), arena2[:, 8 * NM:16 * NM].rearrange("p (c t) -> p c t", c=8)]
        b_upT = [Buf("upT0"), Buf("upT1")]
        inherit(b_upT, [b_mxa, b_xt2, b_xb2])
        DQ(CALL("dma_start", out=nrm[:], in_=nrm_post.ap()[1]), w=[b_nrm])
        blocks5 = [(0, 512), (512, 512), (1024, 16)]
        for G in range(8):
            ui = G % 2
            for u4 in range(4):
                wt, bw = load_w(w_unit(w_up, G * 1024 + u4 * 256))
                for j in range(2):
                    fc = u4 * 2 + j
                    for (c0, n) in blocks5:
                        pb, pbb, bb = nb()
                        tiles = list(range(c0 // 128, (c0 + n - 1) // 128 + 1))
                        for kc in range(16):
                            T(CALL("matmul",
                                out=pb[:, 0:n], lhsT=wt[:, kc * 256 + j * 128:kc * 256 + (j + 1) * 128], rhs=hT[:, kc, c0:c0 + n],
                                start=(kc == 0), stop=(kc == 15)), r=[bw] + [b_hT[t_] for t_ in tiles], w=[bb])
                        A(CALL("activation", out=upT[ui][:, fc, c0:c0 + n], in_=pb[:, 0:n], func=AF.Relu),
                          r=[bb], w=[b_upT[ui]])
                        V(CALL("tensor_tensor", out=upT[ui][:, fc, c0:c0 + n], in0=upT[ui][:, fc, c0:c0 + n],
                                                                             in1=upT[ui][:, fc, c0:c0 + n], op=ALU.mult),
                          r=[b_upT[ui]], w=[b_upT[ui]])
            for n8 in range(8):
                i_ = wrr[0]
                wrr[0] = (i_ + 1) % 2
                wt = wbuf[i_]
                bw = b_wbuf[i_]
                GQ(CALL("dma_start",
                    out=wt[:, 0:2048].rearrange("p (k n) -> p k n", k=8),
                    in_=w_down.ap()[G * 1024:(G + 1) * 1024, n8 * 256:(n8 + 1) * 256].rearrange("(k p) n -> p k n", p=128)), w=[bw])
                for i in range(9):
                    rows = 128 if i < 8 else TS
                    pb, pbb, bb = nb()
                    for fc in range(8):
                        T(CALL("matmul",
                            out=pb[0:rows, 0:256], lhsT=upT[ui][:, fc, i * 128:i * 128 + rows], rhs=wt[:, fc * 256:(fc + 1) * 256],
                            start=(fc == 0), stop=(fc == 7)), r=[bw, b_upT[ui]], w=[bb])
                    dv = ffa[0:rows, i, n8 * 256:(n8 + 1) * 256]
                    if G == 0:
                        V(CALL("tensor_copy", out=dv, in_=pb[0:rows, 0:256]), r=[bb], w=[b_ffa[i]])
                    else:
                        V(CALL("tensor_tensor", out=dv, in0=dv, in1=pb[0:rows, 0:256], op=ALU.add),
                          r=[bb, b_ffa[i]], w=[b_ffa[i]])
        inherit([b_xt2, b_xb2], b_upT)
        for i2 in range(5):
            interleave([pnr_chain(i, si, 5) for si, i in enumerate((2 * i2, 2 * i2 + 1)) if i < 9], lag=3)
        S.emit(st, final_streams=["out"])
    return nc


def _prep(inputs):
    f = lambda k: np.ascontiguousarray(np.asarray(inputs[k], dtype=np.float32))
    xp = f("x_prompt"); xs = f("x_sample")
    conv_w = f("conv_w")[0]
    pvec = np.zeros((128, PV), np.float32)
    pvec[:, PV_CONV:PV_CONV + 96] = conv_w.reshape(4, 24, 128).transpose(2, 1, 0).reshape(128, 96)
    pvec[:, PV_PREMIX:PV_PREMIX + 16] = f("norm_pre_mix")[0].reshape(16, 128).T
    pvec[:, PV_PREMLP:PV_PREMLP + 16] = f("norm_pre_mlp")[0].reshape(16, 128).T
    pvec[:, PV_PSCALE:PV_PSCALE + 8] = f("pool_scale")[0].reshape(8, 128).T
    pvec[:, PV_GDNOUT] = f("norm_gdn_out")[0]
    nrm_post = np.stack([np.broadcast_to(f("norm_post_mix")[0], (128, D)), np.broadcast_to(f("norm_post_mlp")[0], (128, D))]).copy()
    shared = dict(w_in=f("w_in")[0], w_pool=f("w_pool")[0], w_out=f("w_out")[0], w_up=f("w_up")[0], w_down=f("w_down")[0],
                  pvec=pvec, nrm_post=nrm_post)
    maps = []
    for c in range(8):
        b, s = c // 2, c % 2
        xall = np.zeros((NT, D), np.float32)
        if s == 1:
            xall[0:TP] = xp[b, 0:TP]
        xall[TP:2 * TP] = xp[b, s * TP:(s + 1) * TP]
        xall[2 * TP:] = xs[c * TS:(c + 1) * TS, 0]
        rows = np.zeros((RB,), np.float32)
        rows[RB_GDNOUT:RB_GDNOUT + 128] = f("norm_gdn_out")[0]
        rows[RB_ALOG:RB_ALOG + 8] = f("a_log")[0]
        rows[RB_DT:RB_DT + 8] = f("dt_bias")[0]
        for g in range(4):
            pos = s * TP + np.arange(16)
            rows[RB_INVC + g * 16:RB_INVC + (g + 1) * 16] = 1.0 / np.minimum(pos + 1, 2 << g)
        m = dict(shared)
        m.update(xall=xall, rows_bc=np.broadcast_to(rows, (128, RB)).copy(),
                 st_gdn=f("state_gdn")[0, c * TS:(c + 1) * TS], st_conv=f("state_conv")[0, c * TS:(c + 1) * TS].reshape(TS * 3, 3072),
                 st_pool=f("state_pool")[0, c * TS:(c + 1) * TS].reshape(TS * 15, 1024))
        maps.append(m)
    return maps


_NC = [None]


def kernel(**inputs):
    maps = _prep(inputs)
    if _NC[0] is None:
        _NC[0] = build_program()
    res = run_bass_kernel_spmd(_NC[0], maps, core_ids=list(range(8))).results
    yp = np.zeros((4, 2048, D), np.float32)
    ys = np.zeros((128, 1, D), np.float32)
    gp = np.zeros((1, 4, 8, 128, 128), np.float32)
    cp = np.zeros((1, 4, 3, 3072), np.float32)
    pp = np.zeros((1, 4, 15, 1024), np.float32)
    gs = np.zeros((1, 128, 8, 128, 128), np.float32)
    cs = np.zeros((1, 128, 3, 3072), np.float32)
    ps = np.zeros((1, 128, 15, 1024), np.float32)
    for c in range(8):
        b, s = c // 2, c % 2
        r = res[c]
        yp[b, s * TP:(s + 1) * TP] = r["y_main"][0:TP]
        ys[c * TS:(c + 1) * TS, 0] = r["y_main"][TP:]
        if s == 1:
            gp[0, b] = r["o_gdn_p"]
            cp[0, b] = r["o_conv_p"]
            pp[0, b] = r["o_pool_p"]
        gs[0, c * TS:(c + 1) * TS] = r["o_gdn_s"]
        cs[0, c * TS:(c + 1) * TS] = r["o_conv_s"]
        ps[0, c * TS:(c + 1) * TS] = r["o_pool_s"]
    return (yp, ys, gp, cp, pp, gs, cs, ps)
```

```python
from contextlib import ExitStack
import numpy as np
import concourse.bass as bass
import concourse.mybir as mybir
from concourse.bass_utils import run_bass_kernel_spmd

F32 = mybir.dt.float32
BF16 = mybir.dt.bfloat16
AF = mybir.ActivationFunctionType
ALU = mybir.AluOpType

ENGS = ("tensor", "vector", "scalar", "gpsimd", "sync")
D = 2048
NIN = 5136
TP = 1024
TS = 16
NT = 2 * TP + TS
NM = TP + TS
EPS = 1e-6
NEG = -30000.0
SAME_SYNC = True


class Buf:
    __slots__ = ("name", "w", "r", "excl")

    def __init__(self, name, excl=False):
        self.name = name
        self.w = None
        self.r = []
        self.excl = excl


import sys as _sys


def CALL(name, *a, **kw):
    ln = _sys._getframe(1).f_lineno

    def f(e):
        ins = getattr(e, name)(*a, **kw)
        try:
            ins.annotate("L%d" % ln)
        except Exception:
            pass
        return ins
    return f


def inherit(new_bufs, old_bufs):
    for nb in new_bufs:
        for ob in old_bufs:
            if ob.w is not None:
                nb.r.append(ob.w)
            nb.r.extend(ob.r)


class Op:
    __slots__ = ("eng", "fn", "deps", "dma", "stream", "flag", "cnt")

    def __init__(self, eng, fn, dma, stream):
        self.eng = eng
        self.fn = fn
        self.deps = []
        self.dma = dma
        self.stream = stream
        self.flag = False
        self.cnt = None


class Sched:
    def __init__(self, nc, same_engine_sync=True):
        self.nc = nc
        self.ops = {e: [] for e in ENGS}
        self.same = same_engine_sync

    def op(self, eng, fn, reads=(), writes=(), dma=False, stream=None):
        o = Op(eng, fn, dma, stream if dma else None)
        ex = [b for b in reads if b.excl]
        if ex:
            reads = [b for b in reads if not b.excl]
            writes = list(writes) + [b for b in ex if b not in writes]
        deps = []
        for b in reads:
            if b.w is not None:
                deps.append(b.w)
        for b in writes:
            if b.w is not None:
                deps.append(b.w)
            deps.extend(b.r)
        seen = set()
        for d in deps:
            if id(d) in seen or d is o:
                continue
            seen.add(id(d))
            if (not d.dma) and d.eng == eng and (eng == "tensor" or (not self.same and eng != "gpsimd")):
                continue
            o.deps.append(d)
            d.flag = True
        for b in reads:
            b.r.append(o)
        for b in writes:
            b.w = o
            b.r = []
        self.ops[eng].append(o)
        return o

    def emit(self, stack, final_streams=()):
        nc = self.nc
        pool_sz = {"sync": 16, "gpsimd": 6, "scalar": 4, "vector": 2, "tensor": 2}
        sems = {}
        for e in ENGS:
            sems[("eng", e)] = stack.enter_context(nc.semaphore("s_" + e))
        cnt = {k: 0 for k in sems}
        dma_i = {e: 0 for e in ENGS}
        final_waits = {}
        for e in ENGS:
            for o in self.ops[e]:
                if o.dma:
                    o.flag = True
                if not o.flag:
                    continue
                if o.dma:
                    i = dma_i[e]
                    dma_i[e] += 1
                    R = pool_sz[e]
                    k = ("dma", e, i % R)
                    if k not in sems:
                        sems[k] = stack.enter_context(nc.semaphore("d_%s_%d" % (e, i % R)))
                    o.cnt = (k, 16 * (i // R + 1))
                    if o.stream in final_streams:
                        final_waits[k] = max(final_waits.get(k, 0), o.cnt[1])
                else:
                    k = ("eng", e)
                    cnt[k] += 1
                    o.cnt = (k, cnt[k])
        block = stack.enter_context(nc.Block())

        def run(ename, eng):
            waited = {}
            for o in self.ops[ename]:
                need = {}
                for d in o.deps:
                    k, c = d.cnt
                    if c > need.get(k, 0):
                        need[k] = c
                if o.dma and o.cnt[1] > 16:
                    k, c = o.cnt
                    need[k] = max(need.get(k, 0), c - 16)
                for k, c in need.items():
                    if waited.get(k, 0) >= c:
                        continue
                    eng.wait_ge(sems[k], c)
                    waited[k] = c
                ins = o.fn(eng)
                if o.flag:
                    k, c = o.cnt
                    ins.then_inc(sems[k], 16 if o.dma else 1)
            if ename == "sync":
                for k, c in final_waits.items():
                    if waited.get(k, 0) < c:
                        eng.wait_ge(sems[k], c)

        @block.tensor
        def _(e):
            run("tensor", e)

        @block.vector
        def _(e):
            run("vector", e)

        @block.scalar
        def _(e):
            run("scalar", e)

        @block.gpsimd
        def _(e):
            run("gpsimd", e)

        @block.sync
        def _(e):
            run("sync", e)


RB_GDNOUT = 0
RB_ALOG = 128
RB_DT = 136
RB_INVC = 144
RB = 144 + 64
PV_CONV = 0
PV_PREMIX = 96
PV_PREMLP = 112
PV_PSCALE = 128
PV_GDNOUT = 136
PV = 137


def build_program(debug=False):
    nc = bass.Bass("TRN2", target_bir_lowering=False)
    dt_in = lambda n, s: nc.dram_tensor(n, s, F32, kind="ExternalInput")
    dt_out = lambda n, s: nc.dram_tensor(n, s, F32, kind="ExternalOutput")
    xall = dt_in("xall", [NT, D])
    st_gdn = dt_in("st_gdn", [TS, 8, 128, 128])
    st_conv = dt_in("st_conv", [TS * 3, 3072])
    st_pool = dt_in("st_pool", [TS * 15, 1024])
    w_in = dt_in("w_in", [D, NIN])
    w_pool = dt_in("w_pool", [4, 256, 256])
    w_out = dt_in("w_out", [D, D])
    w_up = dt_in("w_up", [D, 4 * D])
    w_down = dt_in("w_down", [4 * D, D])
    rows_bc = dt_in("rows_bc", [128, RB])
    nrm_post = dt_in("nrm_post", [2, 128, D])
    pvec_d = dt_in("pvec", [128, PV])
    y_main = dt_out("y_main", [NM, D])
    o_gdn_p = dt_out("o_gdn_p", [8, 128, 128])
    o_conv_p = dt_out("o_conv_p", [3, 3072])
    o_pool_p = dt_out("o_pool_p", [15, 1024])
    o_gdn_s = dt_out("o_gdn_s", [TS, 8, 128, 128])
    o_conv_s = dt_out("o_conv_s", [TS, 3, 3072])
    o_pool_s = dt_out("o_pool_s", [TS, 15, 1024])
    sc_qT = nc.dram_tensor("sc_qT", [8, 128, TP], BF16, kind="Internal")
    sc_kT = nc.dram_tensor("sc_kT", [8, 128, 2 * TP], BF16, kind="Internal")
    sc_vT = nc.dram_tensor("sc_vT", [8, 128, 2 * TP], BF16, kind="Internal")
    sc_gate = nc.dram_tensor("sc_gate", [NM, 1024], BF16, kind="Internal")
    sc_mixT = nc.dram_tensor("sc_mixT", [16, 128, NM], BF16, kind="Internal")

    with ExitStack() as st:
        S = Sched(nc, same_engine_sync=SAME_SYNC)
        sb = lambda n, s, d: st.enter_context(nc.sbuf_tensor(n, s, d))
        V = lambda fn, r=(), w=(): S.op("vector", fn, r, w)
        A = lambda fn, r=(), w=(): S.op("scalar", fn, r, w)
        T = lambda fn, r=(), w=(): S.op("tensor", fn, r, w)
        P = lambda fn, r=(), w=(): S.op("gpsimd", fn, r, w)
        DQ = lambda fn, r=(), w=(), stream="io": S.op("sync", fn, r, w, dma=True, stream=stream)
        GQ = lambda fn, r=(), w=(): S.op("gpsimd", fn, r, w, dma=True, stream="w")

        banks = []
        for i in range(8):
            t = st.enter_context(nc.psum_tensor("pb%d" % i, [128, 512], F32))
            banks.append((t, t.bitcast(BF16), Buf("pb%d" % i, True)))
        bank_rr = [0]

        def nb():
            i = bank_rr[0]
            bank_rr[0] = (i + 1) % 8
            return banks[i]

        def nb2():
            i = bank_rr[0]
            if i % 2:
                i = (i + 1) % 8
            bank_rr[0] = (i + 2) % 8
            return banks[i], banks[i + 1]

        identf = sb("identf", [128, 128], F32); b_c = Buf("consts")
        identb = sb("identb", [128, 128], BF16)
        onesb = sb("onesb", [128, 128], BF16)
        onesf = sb("onesf", [128, 128], F32)
        utri = sb("utri", [128, 128], F32)
        lstr = sb("lstr", [128, 128], F32)
        negs = sb("negs", [128, 128], F32)
        negi = sb("negi", [128, 128], F32)
        epst = sb("epst", [128, 1], F32)
        rbc = sb("rbc", [128, RB], F32); b_rbc = Buf("rbc")
        pvec = sb("pvec_sb", [128, PV], F32); b_pv = Buf("pvec")
        nalog = sb("nalog", [128, 8], F32)

        def aff(t, pattern, cm, op, fill):
            P(CALL("affine_select", out=t[:], in_=t[:], pattern=pattern, compare_op=op, fill=fill,
                                        base=0, channel_multiplier=cm), r=[b_c], w=[b_c])

        P(CALL("memset", identf[:], 1.0), w=[b_c])
        aff(identf, [[-1, 128]], 1, ALU.is_equal, 0.0)
        P(CALL("memset", utri[:], 1.0), w=[b_c])
        aff(utri, [[1, 128]], -1, ALU.is_ge, 0.0)
        P(CALL("memset", lstr[:], 1.0), w=[b_c])
        aff(lstr, [[-1, 128]], 1, ALU.is_gt, 0.0)
        P(CALL("memset", negs[:], 0.0), w=[b_c])
        aff(negs, [[-1, 128]], 1, ALU.is_gt, NEG)
        P(CALL("memset", negi[:], 0.0), w=[b_c])
        aff(negi, [[1, 128]], -1, ALU.is_ge, NEG)
        P(CALL("memset", onesf[:], 1.0), w=[b_c])
        P(CALL("memset", onesb[:], 1.0), w=[b_c])
        P(CALL("memset", epst[:], EPS), w=[b_c])
        V(CALL("tensor_copy", out=identb[:], in_=identf[:]), r=[b_c], w=[b_c])
        DQ(CALL("dma_start", out=rbc[:], in_=rows_bc.ap()), w=[b_rbc])
        DQ(CALL("dma_start", out=pvec[:], in_=pvec_d.ap()), w=[b_pv])
        A(CALL("activation", out=nalog[:], in_=rbc[:, RB_ALOG:RB_ALOG + 8], func=AF.Exp), r=[b_rbc], w=[b_c])
        V(CALL("tensor_scalar_mul", out=nalog[:], in0=nalog[:], scalar1=-1.0), r=[b_c], w=[b_c])

        hT = sb("hT", [128, 16, NM], BF16)
        b_hT = [Buf("hT%d" % i) for i in range(9)]
        wbuf = [sb("wbuf%d" % i, [128, 16 * 256], BF16) for i in range(2)]
        b_wbuf = [Buf("wbuf%d" % i) for i in range(2)]
        wrr = [0]
        wab = sb("wab", [128, 16, 16], BF16); b_wab = Buf("wab")
        wpl = sb("wpl", [128, 4, 2, 256], BF16); b_wpl = Buf("wpl")
        b_mixT = {}
        halo_c = sb("halo_c", [128, 24, 3], F32); b_hc = Buf("halo_c")
        halo_u = sb("halo_u", [128, 8, 15], F32); b_hu = Buf("halo_u")
        gb = sb("gb", [128, 17, 24], F32); b_gb = Buf("gb")
        rawtail = sb("rawtail", [128, 32, 32], F32); b_rt = Buf("rawtail")
        Sst = sb("Sst", [128, 8, 128], F32); b_S = Buf("S")
        Sbf = sb("Sbf", [128, 8, 128], BF16); b_Sb = Buf("Sb")
        smp = sb("smp", [128, 24, 16], F32); b_smp = Buf("smp")
        xt = [sb("xt%d" % i, [128, D], F32) for i in range(2)]
        b_xt = [Buf("xt%d" % i) for i in range(2)]
        xb = sb("xb", [128, D], BF16); b_xb = Buf("xb")
        junk = xb; b_junk = b_xb
        st1 = sb("st1", [128, 8], F32); b_st1 = Buf("st1")

        arena1 = sb("arena1", [128, 9 * 2048 + 640], F32)
        arena1b = arena1.bitcast(BF16)
        arena2 = sb("arena2", [128, 2 * 8 * NM], BF16)
        b_a1 = Buf("arena1_phase13")
        b_a2 = Buf("arena2_phase13")

        def rstd_from_ss(ss_ap, n, inv_d, rd, wr):
            V(CALL("tensor_scalar", out=ss_ap, in0=ss_ap, scalar1=inv_d, scalar2=EPS, op0=ALU.mult, op1=ALU.add),
              r=rd, w=wr)
            A(CALL("activation", out=ss_ap, in_=ss_ap, func=AF.Ln), r=wr, w=wr)
            A(CALL("activation", out=ss_ap, in_=ss_ap, func=AF.Exp, scale=-0.5), r=wr, w=wr)

        NT_SETS = []

        def norm_transpose(src_ap, src_bufs, rows, col0, tile_idx, pv_off, si=0):
            xb_, bxb, stc_, bst = NT_SETS[si]
            A(CALL("activation", out=xb_[0:rows, :], in_=src_ap, func=AF.Square, accum_out=stc_[0:rows, 0:1]),
              r=src_bufs, w=[bxb, bst])
            yield
            rstd_from_ss(stc_[0:rows, 0:1], rows, 1.0 / D, [bst], [bst])
            yield
            V(CALL("tensor_scalar_mul", out=xb_[0:rows, :], in0=src_ap, scalar1=stc_[0:rows, 0:1]),
              r=list(src_bufs) + [bst], w=[bxb])
            yield
            for half in range(2):
                pb, pbb, bb = nb()
                for k in range(8):
                    kc = half * 8 + k
                    T(CALL("transpose", out=pbb[:, k * 128:k * 128 + rows], in_=xb_[0:rows, kc * 128:(kc + 1) * 128],
                           identity=identb[0:rows, 0:rows]), r=[bxb, b_c], w=[bb])
                yield
                src = pbb[:, 0:1024].rearrange("p (k t) -> p k t", k=8)[:, :, 0:rows]
                wv = pvec[:, pv_off + half * 8:pv_off + half * 8 + 8].unsqueeze(2).to_broadcast([128, 8, rows])
                V(CALL("tensor_tensor", out=hT[:, half * 8:half * 8 + 8, col0:col0 + rows], in0=src, in1=wv, op=ALU.mult),
                  r=[bb, b_pv], w=[b_hT[tile_idx]])
                yield

        def interleave(gens, lag=0):
            gens = list(gens)
            active = []
            step = 0
            while gens or active:
                if gens and (lag == 0 or step % max(lag, 1) == 0 or not active):
                    if lag == 0:
                        active.extend(gens)
                        gens = []
                    else:
                        active.append(gens.pop(0))
                for g_ in list(active):
                    try:
                        next(g_)
                    except StopIteration:
                        active.remove(g_)
                step += 1

        def load_w(ap_fn):
            i = wrr[0]
            wrr[0] = (i + 1) % 2
            GQ(CALL("dma_start", out=ap_fn(wbuf[i]), in_=ap_fn.src), w=[b_wbuf[i]])
            return wbuf[i], b_wbuf[i]

        def w_in_unit(c0):
            def f(t):
                return t[:, :].rearrange("p (k n) -> p k n", k=16)
            f.src = w_in.ap()[:, c0:c0 + 256].rearrange("(k p) n -> p k n", p=128)
            return f

        a1off = [0]

        def a1f(n):
            o = a1off[0]
            a1off[0] += n
            return arena1[:, o:o + n]

        def a1b(n):
            o = a1off[0]
            a1off[0] += (n + 1) // 2
            return arena1b[:, 2 * o:2 * o + n]

        raw = [a1f(528) for _ in range(2)]; b_raw = [Buf("raw0"), Buf("raw1")]
        acc = a1f(512); b_acc = Buf("acc")
        sl = acc; b_sl = b_acc
        sqb = None; b_sqb = Buf("sqb")
        rs = a1f(512); b_rs = Buf("rs")
        fmo = [a1b(512) for _ in range(2)]; b_fmo = [Buf("fmo0"), Buf("fmo1")]
        tmo = fmo; b_tmo = b_fmo
        _g = a1b(512); _bg = Buf("gto"); gto = [_g, _g]; b_gto = [_bg, _bg]
        uA = a1f(544); uB = a1f(544); b_uA = Buf("uA"); b_uB = Buf("uB")
        uraw = [a1f(544) for _ in range(2)]; b_uraw = [Buf("uraw0"), Buf("uraw1")]
        poolT = arena2[:, 0:8 * NM].rearrange("p (c t) -> p c t", c=8); b_poolT = [Buf("poolT%d" % i) for i in range(8)]
        rr = {"raw": 0, "fmo": 0, "tmo": 0, "gto": 0, "uraw": 0}

        def nxt(k):
            i = rr[k]
            rr[k] = 1 - i
            return i

        b_sc_q = {}
        b_sc_kT = {}
        b_sc_k = {}
        b_sc_v = {}
        b_sc_g = [Buf("scg%d" % i) for i in range(9)]

        def scb(d, key):
            if key not in d:
                d[key] = Buf("sc")
            return d[key]

        a2o = [8 * NM]

        def a2b(n):
            o = a2o[0]
            a2o[0] += n
            return arena2[:, o:o + n]

        def a2f_(n):
            o = a2o[0]
            a2o[0] += 2 * n
            return arena2[:, o:o + 2 * n].bitcast(F32)

        hk = sb("hk", [128, 2, 16], F32)
        SETS = [dict(raw=raw[0], b_raw=b_raw[0], acc=acc, b_acc=b_acc, sqb=sqb, b_sqb=b_sqb, rs=rs, b_rs=b_rs,
                     fmo=fmo[0], b_fmo=b_fmo[0], tmo=tmo[0], b_tmo=b_tmo[0], hk=hk[:, 0, :], b_hk=Buf("hk0"),
                     uraw=uraw[0], b_uraw=b_uraw[0], uA=uA, b_uA=b_uA, uB=uB, b_uB=b_uB),
                dict(raw=raw[1], b_raw=b_raw[1], acc=a2f_(512), b_acc=Buf("acc2"), sqb=a2b(512), b_sqb=Buf("sqb2"),
                     rs=a2f_(512), b_rs=Buf("rs2"), fmo=fmo[1], b_fmo=b_fmo[1], tmo=tmo[1], b_tmo=b_tmo[1],
                     hk=hk[:, 1, :], b_hk=Buf("hk1"), uraw=uraw[1], b_uraw=b_uraw[1], uA=a2f_(544), b_uA=Buf("uA2"),
                     uB=a2f_(544), b_uB=Buf("uB2"))]
        xb2 = a2b(2048); b_xb2 = Buf("xb2")
        assert a2o[0] <= 16 * NM, a2o[0]
        st1b = sb("st1b", [128, 8], F32); b_st1b = Buf("st1b")
        NT_SETS.append((xb, b_xb, st1, b_st1))
        NT_SETS.append((xb2, b_xb2, st1b, b_st1b))
        A2HI = [SETS[1]["b_acc"], SETS[1]["b_sqb"], SETS[1]["b_rs"], SETS[1]["b_uA"], SETS[1]["b_uB"], b_xb2]

        def qkv_chunk(wt, bw, j, ch, blocks, first_zero_halo, si=0, halo_mode=None, D=0, pre=None):
            if pre is not None:
                wt, bw = pre()
            Z = SETS[si]
            rw, brw = Z["raw"], Z["b_raw"]
            ac, bac = Z["acc"], Z["b_acc"]
            sq_, bsq = Z["sqb"], Z["b_sqb"]
            rs_, brs = Z["rs"], Z["b_rs"]
            fo, bfo = Z["fmo"], Z["b_fmo"]
            to, bto = Z["tmo"], Z["b_tmo"]
            hk_, bhk = Z["hk"], Z["b_hk"]
            kind = ch // 8
            h = ch % 8
            first = True
            for (c0, n, g0, bk) in blocks:
                pb, pbb, bb = nb()
                tiles = list(range(c0 // 128, (c0 + n - 1) // 128 + 1))
                for kc in range(16):
                    T(CALL("matmul", out=pb[:, 0:n], lhsT=wt[:, kc * 256 + j * 128:kc * 256 + (j + 1) * 128],
                           rhs=hT[:, kc, c0:c0 + n], start=(kc == 0), stop=(kc == 15)),
                      r=[bw] + [b_hT[t] for t in tiles], w=[bb])
                yield
                for _ in range(D):
                    yield
                if bk == "sample":
                    A(CALL("activation", out=rawtail[:, ch, 16:32], in_=pb[:, 0:16], func=AF.Copy), r=[bb], w=[b_rt])
                    yield
                    continue
                if halo_mode == "hk":
                    V(CALL("tensor_copy", out=rw[:, 0:3], in_=hk_[:, 0:3]), r=[bhk], w=[brw])
                elif first and first_zero_halo:
                    V(CALL("memset", rw[:, 0:3], 0.0), w=[brw])
                elif first:
                    V(CALL("tensor_copy", out=rw[:, 0:3], in_=halo_c[:, ch, :]), r=[b_hc], w=[brw])
                else:
                    V(CALL("tensor_copy", out=rw[:, 0:3], in_=hk_[:, 0:3]), r=[bhk], w=[brw])
                first = False
                A(CALL("activation", out=rw[:, 3:3 + n], in_=pb[:, 0:n], func=AF.Copy), r=[bb], w=[brw])
                yield
                V(CALL("tensor_copy", out=hk_[:, 0:3], in_=rw[:, n:n + 3]), r=[brw], w=[bhk])
                if bk in ("halo", "lastpre"):
                    V(CALL("tensor_copy", out=halo_c[:, ch, :], in_=rw[:, n:n + 3]), r=[brw], w=[b_hc])
                if bk == "halo":
                    yield
                    continue
                if bk == "lastmain":
                    V(CALL("tensor_copy", out=rawtail[:, ch, 0:16], in_=rw[:, 3 + n - 16:3 + n]), r=[brw], w=[b_rt])
                cw = lambda jj: pvec[:, PV_CONV + ch * 4 + jj:PV_CONV + ch * 4 + jj + 1]
                V(CALL("tensor_scalar_mul", out=ac[:, 0:n], in0=rw[:, 0:n], scalar1=cw(0)), r=[brw, b_pv], w=[bac])
                yield
                for jj in range(1, 4):
                    V(CALL("scalar_tensor_tensor", out=ac[:, 0:n], in0=rw[:, jj:jj + n], scalar=cw(jj), in1=ac[:, 0:n],
                           op0=ALU.mult, op1=ALU.add), r=[brw, b_pv, bac], w=[bac])
                    yield
                A(CALL("activation", out=rs_[:, 0:n], in_=ac[:, 0:n], func=AF.Exp, scale=-1.0), r=[bac], w=[brs])
                yield
                A(CALL("activation", out=rs_[:, 0:n], in_=rs_[:, 0:n], func=AF.Ln, bias=1.0), r=[brs], w=[brs])
                yield
                A(CALL("activation", out=rs_[:, 0:n], in_=rs_[:, 0:n], func=AF.Exp, scale=-1.0), r=[brs], w=[brs])
                yield
                V(CALL("tensor_tensor", out=fo[:, 0:n], in0=ac[:, 0:n], in1=rs_[:, 0:n], op=ALU.mult), r=[bac, brs], w=[bfo])
                yield
                if kind == 0:
                    DQ(CALL("dma_start", out=sc_qT.ap()[h, :, g0 - TP:g0 - TP + n], in_=fo[:, 0:n]), r=[bfo], w=[scb(b_sc_q, (h, g0))])
                elif kind == 1:
                    DQ(CALL("dma_start", out=sc_kT.ap()[h, :, g0:g0 + n], in_=fo[:, 0:n]), r=[bfo], w=[scb(b_sc_kT, (h, g0))])
                else:
                    DQ(CALL("dma_start", out=sc_vT.ap()[h, :, g0:g0 + n], in_=fo[:, 0:n]), r=[bfo], w=[scb(b_sc_v, (h, g0))])

        def interleave(gens, lag=0):
            gens = list(gens)
            active = []
            step = 0
            while gens or active:
                if gens and (lag == 0 or step % max(lag, 1) == 0 or not active):
                    if lag == 0:
                        active.extend(gens)
                        gens = []
                    else:
                        active.append(gens.pop(0))
                for g_ in list(active):
                    try:
                        next(g_)
                    except StopIteration:
                        active.remove(g_)
                step += 1

        def qkv_stream(units, blocks, first_zero_halo, lag=4, D=5):
            wl = {}

            def mk_pre(ui):
                def pre():
                    if ui not in wl:
                        k_, u_ = units[ui]
                        wl[ui] = load_w(w_in_unit(k_ * 1024 + u_ * 256))
                    if ui + 1 < len(units) and (ui + 1) not in wl:
                        k_, u_ = units[ui + 1]
                        wl[ui + 1] = load_w(w_in_unit(k_ * 1024 + u_ * 256))
                    return wl[ui]
                return pre

            chains = []
            for ui, (kind, u) in enumerate(units):
                for bi, blk in enumerate(blocks):
                    for j in range(2):
                        hm = "hk" if (bi > 0 and blk[3] != "sample") else None
                        chains.append(qkv_chunk(None, None, j, kind * 8 + u * 2 + j, [blk], first_zero_halo, j, hm, D, mk_pre(ui)))
            interleave(chains, lag=lag)

        def ab_tiles(tile_list):
            for (ct, slot, rows) in tile_list:
                pb, pbb, bb = nb()
                for kc in range(16):
                    T(CALL("matmul",
                        out=pb[0:rows, 0:16], lhsT=hT[:, kc, ct * 128:ct * 128 + rows], rhs=wab[:, kc, :],
                        start=(kc == 0), stop=(kc == 15)), r=[b_wab, b_hT[ct]], w=[bb])
                g_ = gb[0:rows, slot, 0:8]
                lb = gb[0:rows, slot, 8:16]
                be = gb[0:rows, slot, 16:24]
                V(CALL("tensor_tensor", out=g_, in0=pb[0:rows, 0:8],
                                                                     in1=rbc[0:rows, RB_DT:RB_DT + 8], op=ALU.add),
                  r=[bb, b_rbc], w=[b_gb])
                A(CALL("activation", out=g_, in_=g_, func=AF.Exp), r=[b_gb], w=[b_gb])
                A(CALL("activation", out=g_, in_=g_, func=AF.Ln, bias=1.0), r=[b_gb], w=[b_gb])
                V(CALL("tensor_tensor", out=g_, in0=g_, in1=nalog[0:rows, :], op=ALU.mult),
                  r=[b_gb, b_c], w=[b_gb])
                A(CALL("activation", out=lb, in_=pb[0:rows, 8:16], func=AF.Exp, scale=-1.0),
                  r=[bb], w=[b_gb])
                A(CALL("activation", out=lb, in_=lb, func=AF.Ln, bias=1.0), r=[b_gb], w=[b_gb])
                V(CALL("tensor_scalar_mul", out=lb, in0=lb, scalar1=-1.0), r=[b_gb], w=[b_gb])
                A(CALL("activation", out=be, in_=lb, func=AF.Exp), r=[b_gb], w=[b_gb])

        def g3(n=1024):
            return a1b(n).rearrange("p (h i) -> p h i", h=8)

        _alt = lambda w0: arena1b[:, 2 * w0:2 * w0 + 1024].rearrange("p (h i) -> p h i", h=8)
        kTc = [g3(), _alt(0)]; b_kTc = [Buf("kTc0"), Buf("kTc1")]
        qTc = [g3(), _alt(1024)]; b_qTc = [Buf("qTc0"), Buf("qTc1")]
        _v = g3(); _b = Buf("ktm"); ktm = [_v, _v]; b_ktm = [_b, _b]
        _v = g3(); _b = Buf("vtm"); vtm = [_v, _v]; b_vtm = [_b, _b]
        GU = a1f(1024).rearrange("p (h i) -> p h i", h=8); b_GU = Buf("GU")
        vTc_l = [g3(), _alt(512)]; b_vTc_l = [Buf("vTc0"), Buf("vTc1")]
        sqs_ = g3(); b_sqs_ = Buf("sqs_")
        nsc = sb("nsc", [128, 64], F32); b_nsc = Buf("nsc")
        bD = g3(); b_bD = Buf("bD")
        DT_ = g3(); b_DT = Buf("DT")
        Lm = [g3() for _ in range(2)]; b_Lm = [Buf("L0"), Buf("L1")]
        LTm = [g3() for _ in range(2)]; b_LTm = [Buf("LT0"), Buf("LT1")]
        Tt = [g3() for _ in range(2)]; b_Tt = [Buf("Tt0"), Buf("Tt1")]
        intraT = g3(); b_intra = Buf("intraT")
        vbt = g3(); b_vb = Buf("vb")
        kbg = g3(); b_kbg = Buf("kbg")
        kdt = g3(); b_kd = Buf("kd")
        nwT = g3(); b_nwT = Buf("nwT")
        vnew = g3(); b_vnew = Buf("vnew")
        otm = a1f(1024).rearrange("p (h i) -> p h i", h=8); b_otm = Buf("otm")
        oab = g3(); b_oab = Buf("oab")
        otm2 = oab; b_otm2 = b_oab
        oaT = g3(); b_oaT = Buf("oaT")
        gate_l = [g3(), _alt(1536)]; b_gate_l = [Buf("gate0"), Buf("gate1")]
        ALT_BUFS = [b_kTc[1], b_qTc[1], b_vTc_l[1], b_gate_l[1]]
        INPROJ_LOW = [b_raw[0], b_raw[1], b_acc, b_rs]
        gcs = a1f(48); b_gcs = Buf("gcs")
        assert a1off[0] <= 9 * 2048 + 640, a1off[0]

        HS = 2
        NSTR = 8 // HS

        def H2(name):
            return [Buf(name + "_%d" % i) for i in range(NSTR)]
        hb_bD = H2("bD"); hb_DT = H2("DT"); hb_L = [H2("L0"), H2("L1")]; hb_LT = [H2("LT0"), H2("LT1")]
        hb_Tt = [H2("Tt0"), H2("Tt1")]; hb_intra = H2("intra"); hb_vb = H2("vb"); hb_kbg = H2("kbg"); hb_kd = H2("kd")
        hb_nwT = H2("nwT"); hb_vnew = H2("vnew"); hb_otm = H2("otm"); hb_oab = H2("oab"); hb_oaT = H2("oaT")
        hb_S = H2("S"); hb_Sb = H2("Sb"); hb_st = H2("st1h"); hb_km = H2("km"); hb_vm = H2("vm")
        st1h = sb("st1h", [128, 8], F32)
        GDN_BUFS = (hb_bD + hb_DT + hb_L[0] + hb_L[1] + hb_LT[0] + hb_LT[1] + hb_Tt[0] + hb_Tt[1] + hb_intra + hb_vb + hb_kbg
                    + hb_kd + hb_nwT + hb_vnew + hb_otm + hb_oab + hb_oaT)

        def interleave(gens, lag=0):
            gens = list(gens)
            active = []
            step = 0
            while gens or active:
                if gens and (lag == 0 or step % max(lag, 1) == 0 or not active):
                    if lag == 0:
                        active.extend(gens)
                        gens = []
                    else:
                        active.append(gens.pop(0))
                for g_ in list(active):
                    try:
                        next(g_)
                    except StopIteration:
                        active.remove(g_)
                step += 1

        def gdn_half(c, main, hh, kt_, bkt, km_, bkm, vm_, bvm, qt_, bqt, gv, lbv, bev, vTc, b_vTc, gate_t, b_gate):
            hs = slice(hh * HS, hh * HS + HS)
            v4 = lambda pp: pp[:, 0:HS * 128].rearrange("p (h i) -> p h i", h=HS)
            bc4 = lambda ap: ap.unsqueeze(2).to_broadcast([128, HS, 128])
            pT_, pTb, bbT = nb()
            for h4 in range(HS):
                h = hh * HS + h4
                T(CALL("transpose", out=pTb[:, h4 * 128:(h4 + 1) * 128], in_=kt_[:, h, :], identity=identb[:]), r=[bkt, b_c], w=[bbT])
                T(CALL("transpose", out=pTb[:, 512 + h4 * 128:512 + (h4 + 1) * 128], in_=vTc[:, h, :], identity=identb[:]),
                  r=[b_vTc, b_c], w=[bbT])
            yield
            A(CALL("activation", out=km_[:, hs, :], in_=pTb[:, 0:HS * 128].rearrange("p (h i) -> p h i", h=HS), func=AF.Copy), r=[bbT], w=[hb_km[hh]])
            V(CALL("tensor_copy", out=vm_[:, hs, :], in_=pTb[:, 512:512 + HS * 128].rearrange("p (h i) -> p h i", h=HS)), r=[bbT], w=[hb_vm[hh]])
            yield
            pD, _, bbD = nb()
            for h4 in range(HS):
                h = hh * HS + h4
                o_ = pD[:, h4 * 128:(h4 + 1) * 128]
                T(CALL("matmul", out=o_, lhsT=GU[:, h, :], rhs=lstr[:], start=True, stop=False), r=[b_GU, b_c], w=[bbD])
                T(CALL("matmul", out=o_, lhsT=identf[:], rhs=negs[:], start=False, stop=True), r=[b_c], w=[bbD])
            yield
            pK, _, bbK = nb()
            for h4 in range(HS):
                h = hh * HS + h4
                T(CALL("matmul", out=pK[:, h4 * 128:(h4 + 1) * 128], lhsT=kt_[:, h, :], rhs=kt_[:, h, :], start=True, stop=True),
                  r=[bkt], w=[bbK])
            yield
            for h4 in range(HS):
                h = hh * HS + h4
                A(CALL("activation", out=bD[:, h, :], in_=pD[:, h4 * 128:(h4 + 1) * 128], func=AF.Exp, bias=nsc[:, 16 + h:17 + h]),
                  r=[bbD, b_nsc], w=[hb_bD[hh]])
            yield
            V(CALL("tensor_tensor", out=Lm[0][:, hs, :], in0=v4(pK), in1=bD[:, hs, :], op=ALU.mult), r=[bbK, hb_bD[hh]], w=[hb_L[0][hh]])
            yield
            pb, pbb, bb = nb()
            for h4 in range(HS):
                h = hh * HS + h4
                T(CALL("transpose", out=pbb[:, h4 * 128:(h4 + 1) * 128], in_=Lm[0][:, h, :], identity=identb[:]),
                  r=[hb_L[0][hh], b_c], w=[bb])
            yield
            pv3 = pbb[:, 0:HS * 128].rearrange("p (h i) -> p h i", h=HS)
            A(CALL("activation", out=LTm[0][:, hs, :], in_=pv3, func=AF.Copy), r=[bb], w=[hb_LT[0][hh]])
            V(CALL("scalar_tensor_tensor", out=Tt[0][:, hs, :], in0=pv3, scalar=-1.0,
                   in1=identb[:].unsqueeze(1).to_broadcast([128, HS, 128]), op0=ALU.mult, op1=ALU.add), r=[bb, b_c], w=[hb_Tt[0][hh]])
            yield
            ci = 0
            for lev in range(6):
                ni = 1 - ci
                last = (lev == 5)
                p0, _, bb0 = nb()
                for h4 in range(HS):
                    h = hh * HS + h4
                    T(CALL("matmul", out=p0[:, h4 * 128:(h4 + 1) * 128], lhsT=LTm[ci][:, h, :], rhs=Lm[ci][:, h, :], start=True, stop=True),
                      r=[hb_L[ci][hh], hb_LT[ci][hh]], w=[bb0])
                yield
                if not last:
                    q0, _, qb0 = nb()
                    for h4 in range(HS):
                        h = hh * HS + h4
                        T(CALL("matmul", out=q0[:, h4 * 128:(h4 + 1) * 128], lhsT=Lm[ci][:, h, :], rhs=LTm[ci][:, h, :], start=True, stop=True),
                          r=[hb_L[ci][hh], hb_LT[ci][hh]], w=[qb0])
                    yield
                V(CALL("tensor_copy", out=Lm[ni][:, hs, :], in_=v4(p0)), r=[bb0], w=[hb_L[ni][hh]])
                yield
                if not last:
                    A(CALL("activation", out=LTm[ni][:, hs, :], in_=v4(q0), func=AF.Copy), r=[qb0], w=[hb_LT[ni][hh]])
                    yield
                r0, _, rb0 = nb()
                for h4 in range(HS):
                    h = hh * HS + h4
                    o_ = r0[:, h4 * 128:(h4 + 1) * 128]
                    T(CALL("matmul", out=o_, lhsT=Lm[ni][:, h, :], rhs=Tt[ci][:, h, :], start=True, stop=False),
                      r=[hb_L[ni][hh], hb_Tt[ci][hh]], w=[rb0])
                    T(CALL("matmul", out=o_, lhsT=identb[:], rhs=Tt[ci][:, h, :], start=False, stop=True), r=[b_c, hb_Tt[ci][hh]], w=[rb0])
                yield
                if lev % 2 == 0:
                    A(CALL("activation", out=Tt[ni][:, hs, :], in_=v4(r0), func=AF.Copy), r=[rb0], w=[hb_Tt[ni][hh]])
                else:
                    V(CALL("tensor_copy", out=Tt[ni][:, hs, :], in_=v4(r0)), r=[rb0], w=[hb_Tt[ni][hh]])
                yield
                ci = ni
            TT, bTT = Tt[ci], hb_Tt[ci][hh]
            V(CALL("tensor_tensor", out=vbt[:, hs, :], in0=vm_[:, hs, :], in1=bc4(nsc[:, 32 + hh * HS:32 + hh * HS + HS]), op=ALU.mult),
              r=[hb_vm[hh], b_nsc], w=[hb_vb[hh]])
            yield
            V(CALL("tensor_tensor", out=kbg[:, hs, :], in0=km_[:, hs, :], in1=bc4(gcs[:, 24 + hh * HS:24 + hh * HS + HS]), op=ALU.mult),
              r=[hb_km[hh], b_gcs], w=[hb_kbg[hh]])
            yield
            V(CALL("tensor_tensor", out=kdt[:, hs, :], in0=km_[:, hs, :], in1=bc4(gcs[:, 32 + hh * HS:32 + hh * HS + HS]), op=ALU.mult),
              r=[hb_km[hh], b_gcs], w=[hb_kd[hh]])
            yield
            p0, _, bb0 = nb()
            for h4 in range(HS):
                h = hh * HS + h4
                T(CALL("matmul", out=p0[:, h4 * 128:(h4 + 1) * 128], lhsT=kbg[:, h, :], rhs=TT[:, h, :], start=True, stop=True),
                  r=[hb_kbg[hh], bTT], w=[bb0])
            yield
            A(CALL("activation", out=nwT[:, hs, :], in_=v4(p0), func=AF.Copy, scale=-1.0), r=[bb0], w=[hb_nwT[hh]])
            yield
            if main:
                p0, _, bb0 = nb()
                for h4 in range(HS):
                    h = hh * HS + h4
                    o_ = p0[:, h4 * 128:(h4 + 1) * 128]
                    T(CALL("matmul", out=o_, lhsT=lstr[:], rhs=GU[:, h, :], start=True, stop=False), r=[b_GU, b_c], w=[bb0])
                    T(CALL("matmul", out=o_, lhsT=identf[:], rhs=negi[:], start=False, stop=True), r=[b_c], w=[bb0])
                yield
                A(CALL("activation", out=DT_[:, hs, :], in_=v4(p0), func=AF.Exp), r=[bb0], w=[hb_DT[hh]])
                yield
                p1, _, bb1 = nb()
                for h4 in range(HS):
                    h = hh * HS + h4
                    T(CALL("matmul", out=p1[:, h4 * 128:(h4 + 1) * 128], lhsT=kt_[:, h, :], rhs=qt_[:, h, :], start=True, stop=True),
                      r=[bkt, bqt], w=[bb1])
                yield
                V(CALL("tensor_tensor", out=intraT[:, hs, :], in0=v4(p1), in1=DT_[:, hs, :], op=ALU.mult), r=[bb1, hb_DT[hh]], w=[hb_intra[hh]])
                yield
            p0, _, bb0 = nb()
            for h4 in range(HS):
                h = hh * HS + h4
                o_ = p0[:, h4 * 128:(h4 + 1) * 128]
                T(CALL("matmul", out=o_, lhsT=TT[:, h, :], rhs=vbt[:, h, :], start=True, stop=False), r=[bTT, hb_vb[hh]], w=[bb0])
                T(CALL("matmul", out=o_, lhsT=nwT[:, h, :], rhs=Sbf[:, h, :], start=False, stop=True), r=[hb_nwT[hh], hb_Sb[hh]], w=[bb0])
            yield
            A(CALL("activation", out=vnew[:, hs, :], in_=v4(p0), func=AF.Copy), r=[bb0], w=[hb_vnew[hh]])
            yield
            if main:
                p0, _, bb0 = nb()
                for h4 in range(HS):
                    h = hh * HS + h4
                    T(CALL("matmul", out=p0[:, h4 * 128:(h4 + 1) * 128], lhsT=qt_[:, h, :], rhs=Sbf[:, h, :], start=True, stop=True),
                      r=[bqt, hb_Sb[hh]], w=[bb0])
                yield
                V(CALL("tensor_tensor", out=otm[:, hs, :], in0=v4(p0), in1=bc4(gcs[:, 16 + hh * HS:16 + hh * HS + HS]), op=ALU.mult),
                  r=[bb0, b_gcs], w=[hb_otm[hh]])
                yield
                p1, _, bb1 = nb()
                for h4 in range(HS):
                    h = hh * HS + h4
                    T(CALL("matmul", out=p1[:, h4 * 128:(h4 + 1) * 128], lhsT=intraT[:, h, :], rhs=vnew[:, h, :], start=True, stop=True),
                      r=[hb_intra[hh], hb_vnew[hh]], w=[bb1])
                yield
                V(CALL("tensor_tensor", out=otm[:, hs, :], in0=v4(p1), in1=otm[:, hs, :], op=ALU.add), r=[bb1, hb_otm[hh]], w=[hb_otm[hh]])
                yield
            p2, _, bb2 = nb()
            for h4 in range(HS):
                h = hh * HS + h4
                T(CALL("matmul", out=p2[:, h4 * 128:(h4 + 1) * 128], lhsT=kdt[:, h, :], rhs=vnew[:, h, :], start=True, stop=True),
                  r=[hb_kd[hh], hb_vnew[hh]], w=[bb2])
            yield
            V(CALL("tensor_tensor", out=Sst[:, hs, :], in0=Sst[:, hs, :], in1=bc4(gcs[:, 40 + hh * HS:40 + hh * HS + HS]), op=ALU.mult),
              r=[hb_S[hh], b_gcs], w=[hb_S[hh]])
            yield
            V(CALL("tensor_tensor", out=Sst[:, hs, :], in0=v4(p2), in1=Sst[:, hs, :], op=ALU.add), r=[bb2, hb_S[hh]], w=[hb_S[hh]])
            yield
            A(CALL("activation", out=Sbf[:, hs, :], in_=Sst[:, hs, :], func=AF.Copy), r=[hb_S[hh]], w=[hb_Sb[hh]])
            yield
            if main:
                for h4 in range(HS):
                    h = hh * HS + h4
                    A(CALL("activation", out=oab[:, h, :], in_=otm[:, h, :], func=AF.Square, accum_out=st1h[:, h:h + 1]),
                      r=[hb_otm[hh]], w=[hb_oab[hh], hb_st[hh]])
                yield
                sv = st1h[:, hh * HS:hh * HS + HS]
                V(CALL("tensor_tensor", out=sv, in0=sv, in1=nsc[:, 48 + hh * HS:48 + hh * HS + HS], op=ALU.mult), r=[hb_st[hh], b_nsc], w=[hb_st[hh]])
                rstd_from_ss(sv, 128, 1.0 / 128, [hb_st[hh]], [hb_st[hh]])
                V(CALL("tensor_tensor", out=sv, in0=sv, in1=nsc[:, 40 + hh * HS:40 + hh * HS + HS], op=ALU.mult), r=[hb_st[hh], b_nsc], w=[hb_st[hh]])
                yield
                V(CALL("tensor_tensor", out=otm[:, hs, :], in0=otm[:, hs, :], in1=bc4(sv), op=ALU.mult), r=[hb_otm[hh], hb_st[hh]], w=[hb_otm[hh]])
                yield
                V(CALL("tensor_tensor", out=otm[:, hs, :], in0=otm[:, hs, :],
                       in1=rbc[:, RB_GDNOUT:RB_GDNOUT + 128].unsqueeze(1).to_broadcast([128, HS, 128]), op=ALU.mult),
                  r=[hb_otm[hh], b_rbc], w=[hb_otm[hh]])
                yield
                V(CALL("tensor_tensor", out=oab[:, hs, :], in0=otm[:, hs, :], in1=gate_t[:, hs, :], op=ALU.mult),
                  r=[hb_otm[hh], b_gate], w=[hb_oab[hh]])
                yield
                pb, pbb, bb = nb()
                for h4 in range(HS):
                    h = hh * HS + h4
                    T(CALL("transpose", out=pbb[:, h4 * 128:(h4 + 1) * 128], in_=oab[:, h, :], identity=identb[:]), r=[hb_oab[hh], b_c], w=[bb])
                yield
                mc = (c - 8) * 128
                A(CALL("activation", out=oaT[:, hs, :], in_=pbb[:, 0:HS * 128].rearrange("p (h i) -> p h i", h=HS), func=AF.Copy),
                  r=[bb], w=[hb_oaT[hh]])
                b_mixT[(0, c - 8, hh)] = Buf("mix")
                DQ(CALL("dma_start", out=sc_mixT.ap()[hh * HS:hh * HS + HS, :, mc:mc + 128].rearrange("h e t -> e h t"), in_=oaT[:, hs, :]),
                   r=[hb_oaT[hh]], w=[b_mixT[(0, c - 8, hh)]])
                yield

        def gdn_loads(c, main, cur):
            g0 = c * 128
            blk = (g0 // 512) * 512
            DQ(CALL("dma_start", out=kTc[cur], in_=sc_kT.ap()[:, :, g0:g0 + 128].rearrange("h d t -> d h t")),
               r=[b_sc_kT[(h, blk)] for h in range(8)], w=[b_kTc[cur]])
            DQ(CALL("dma_start", out=vTc_l[cur], in_=sc_vT.ap()[:, :, g0:g0 + 128].rearrange("h d t -> d h t")),
               r=[b_sc_v[(h, blk)] for h in range(8)], w=[b_vTc_l[cur]])
            if main:
                DQ(CALL("dma_start", out=qTc[cur], in_=sc_qT.ap()[:, :, g0 - TP:g0 - TP + 128].rearrange("h d t -> d h t")),
                   r=[b_sc_q[(h, blk)] for h in range(8)], w=[b_qTc[cur]])
                ti = c - 8
                DQ(CALL("dma_start", out=gate_l[cur], in_=sc_gate.ap()[ti * 128:(ti + 1) * 128, :].rearrange("t (h d) -> t h d", h=8)),
                   r=[b_sc_g[ti]], w=[b_gate_l[cur]])

        def gdn_chunk(c, main, cur):
            g0 = c * 128
            slot = c
            kt_, bkt = kTc[cur], b_kTc[cur]
            km_, bkm = ktm[cur], b_ktm[cur]
            vm_, bvm = vtm[cur], b_vtm[cur]
            qt_, bqt = qTc[cur], b_qTc[cur]
            vTc, b_vTc = vTc_l[cur], b_vTc_l[cur]
            gate_t, b_gate = gate_l[cur], b_gate_l[cur]
            gv = gb[:, slot, 0:8]
            lbv = gb[:, slot, 8:16]
            bev = gb[:, slot, 16:24]
            pb, _, bb = nb()
            T(CALL("matmul", out=pb[:, 0:8], lhsT=utri[:], rhs=gv, start=True, stop=True), r=[b_gb, b_c], w=[bb])
            T(CALL("matmul", out=pb[:, 8:16], lhsT=onesf[:], rhs=gv, start=True, stop=True), r=[b_gb, b_c], w=[bb])
            V(CALL("tensor_copy", out=gcs[:, 0:16], in_=pb[:, 0:16]), r=[bb], w=[b_gcs])
            A(CALL("activation", out=gcs[:, 16:24], in_=gcs[:, 0:8], func=AF.Exp), r=[b_gcs], w=[b_gcs])
            pn, _, bbn = nb()
            A(CALL("activation", out=sqs_, in_=kt_, func=AF.Square), r=[bkt], w=[b_sqs_])
            for h in range(8):
                T(CALL("matmul", out=pn[:, h:h + 1], lhsT=sqs_[:, h, :], rhs=onesb[:, 0:1], start=True, stop=True), r=[b_sqs_, b_c], w=[bbn])
            if main:
                V(CALL("tensor_tensor", out=sqs_, in0=qt_, in1=qt_, op=ALU.mult), r=[bqt], w=[b_sqs_])
                for h in range(8):
                    T(CALL("matmul", out=pn[:, 8 + h:9 + h], lhsT=sqs_[:, h, :], rhs=onesb[:, 0:1], start=True, stop=True), r=[b_sqs_, b_c], w=[bbn])
            nq = 16 if main else 8
            A(CALL("activation", out=nsc[:, 0:nq], in_=pn[:, 0:nq], func=AF.Ln, bias=epst[:, 0:1]), r=[bbn, b_c], w=[b_nsc])
            V(CALL("tensor_tensor", out=nsc[:, 16:24], in0=lbv, in1=nsc[:, 0:8], op=ALU.subtract), r=[b_gb, b_nsc], w=[b_nsc])
            A(CALL("activation", out=nsc[:, 24:32], in_=nsc[:, 16:24], func=AF.Exp), r=[b_nsc], w=[b_nsc])
            V(CALL("scalar_tensor_tensor", out=nsc[:, 56:64], in0=nsc[:, 0:8], scalar=-0.5, in1=lbv, op0=ALU.mult, op1=ALU.add),
              r=[b_nsc, b_gb], w=[b_nsc])
            A(CALL("activation", out=nsc[:, 32:40], in_=nsc[:, 56:64], func=AF.Exp), r=[b_nsc], w=[b_nsc])
            if main:
                A(CALL("activation", out=nsc[:, 40:48], in_=nsc[:, 8:16], func=AF.Exp, scale=-0.5), r=[b_nsc], w=[b_nsc])
                V(CALL("tensor_scalar_mul", out=nsc[:, 40:48], in0=nsc[:, 40:48], scalar1=128.0 ** -0.5), r=[b_nsc], w=[b_nsc])
                V(CALL("tensor_tensor", out=nsc[:, 48:56], in0=nsc[:, 40:48], in1=nsc[:, 40:48], op=ALU.mult), r=[b_nsc], w=[b_nsc])
            V(CALL("tensor_tensor", out=gcs[:, 24:32], in0=gcs[:, 16:24], in1=nsc[:, 24:32], op=ALU.mult), r=[b_gcs, b_nsc], w=[b_gcs])
            V(CALL("tensor_tensor", out=gcs[:, 32:40], in0=gcs[:, 8:16], in1=gcs[:, 0:8], op=ALU.subtract), r=[b_gcs], w=[b_gcs])
            A(CALL("activation", out=gcs[:, 32:40], in_=gcs[:, 32:40], func=AF.Exp), r=[b_gcs], w=[b_gcs])
            A(CALL("activation", out=gcs[:, 40:48], in_=gcs[:, 8:16], func=AF.Exp), r=[b_gcs], w=[b_gcs])
            V(CALL("tensor_tensor", out=GU, in0=utri[:].unsqueeze(1).to_broadcast([128, 8, 128]),
                   in1=gv.unsqueeze(2).to_broadcast([128, 8, 128]), op=ALU.mult), r=[b_c, b_gb], w=[b_GU])
            interleave([gdn_half(c, main, hh, kt_, bkt, km_, bkm, vm_, bvm, qt_, bqt, gv, lbv, bev, vTc, b_vTc, gate_t, b_gate)
                        for hh in range(NSTR)], lag=2)

        def u_chunk(wt, bw, j, pc, blocks, first_zero_halo, si=0):
            Z = SETS[si]
            rw, brw = Z["uraw"], Z["b_uraw"]
            hk_, bhk = Z["hk"], Z["b_hk"]
            g = pc // 2
            win = 2 << g
            first = True
            for (c0, n, bk) in blocks:
                pb, pbb, bb = nb()
                tiles = list(range(c0 // 128, (c0 + n - 1) // 128 + 1))
                for kc in range(16):
                    T(CALL("matmul", out=pb[:, 0:n], lhsT=wt[:, kc * 256 + j * 128:kc * 256 + (j + 1) * 128],
                           rhs=hT[:, kc, c0:c0 + n], start=(kc == 0), stop=(kc == 15)),
                      r=[bw] + [b_hT[t] for t in tiles], w=[bb])
                yield
                if bk == "sample":
                    A(CALL("activation", out=rawtail[:, 24 + pc, 16:32], in_=pb[:, 0:16], func=AF.Copy), r=[bb], w=[b_rt])
                    yield
                    continue
                if bk == "halo":
                    A(CALL("activation", out=halo_u[:, pc, :], in_=pb[:, 1:16], func=AF.Copy), r=[bb], w=[b_hu])
                    yield
                    continue
                if first:
                    V(CALL("tensor_copy", out=rw[:, 0:15], in_=halo_u[:, pc, :]), r=[b_hu], w=[brw])
                else:
                    V(CALL("tensor_copy", out=rw[:, 0:15], in_=hk_[:, 0:15]), r=[bhk], w=[brw])
                first = False
                A(CALL("activation", out=rw[:, 15:15 + n], in_=pb[:, 0:n], func=AF.Copy), r=[bb], w=[brw])
                yield
                V(CALL("tensor_copy", out=hk_[:, 0:15], in_=rw[:, n:n + 15]), r=[brw], w=[bhk])
                if bk == "lastmain":
                    V(CALL("tensor_copy", out=rawtail[:, 24 + pc, 0:16], in_=rw[:, 15 + n - 16:15 + n]), r=[brw], w=[b_rt])
                L = 15 + n
                src, bsrc = rw, brw
                k = 1
                flip = 0
                while k < win:
                    dst, bdst = (Z["uA"], Z["b_uA"]) if flip == 0 else (Z["uB"], Z["b_uB"])
                    V(CALL("tensor_tensor", out=dst[:, 2 * k - 1:L], in0=src[:, 2 * k - 1:L], in1=src[:, k - 1:L - k], op=ALU.add),
                      r=[bsrc], w=[bdst])
                    yield
                    src, bsrc = dst, bdst
                    flip = 1 - flip
                    k *= 2
                V(CALL("scalar_tensor_tensor", out=poolT[:, pc, c0:c0 + n], in0=src[:, 15:15 + n], scalar=1.0 / win, in1=rw[:, 15:15 + n],
                       op0=ALU.mult, op1=ALU.subtract), r=[bsrc, brw], w=[b_poolT[pc]])
                yield
                if c0 == 0:
                    fx, bfx = Z["rs"], Z["b_rs"]
                    V(CALL("tensor_tensor", out=fx[:, 0:16], in0=src[:, 15:31], in1=rbc[:, RB_INVC + g * 16:RB_INVC + g * 16 + 16], op=ALU.mult),
                      r=[bsrc, b_rbc], w=[bfx])
                    V(CALL("tensor_tensor", out=poolT[:, pc, 0:16], in0=fx[:, 0:16], in1=rw[:, 15:31], op=ALU.subtract),
                      r=[bfx, brw], w=[b_poolT[pc]])
                    yield

        def pool_proj(g, blocks):
            for dch in range(2):
                for (c0, n) in blocks:
                    pb, pbb, bb = nb()
                    for c2 in range(2):
                        T(CALL("matmul",
                            out=pb[:, 0:n], lhsT=wpl[:, g, c2, dch * 128:(dch + 1) * 128],
                            rhs=poolT[:, 2 * g + c2, c0:c0 + n], start=(c2 == 0), stop=(c2 == 1)),
                          r=[b_wpl, b_poolT[2 * g], b_poolT[2 * g + 1]], w=[bb])
                    fi = nxt("fmo")
                    fo, bfo = fmo[fi], b_fmo[fi]
                    A(CALL("activation",
                        out=fo[:, 0:n], in_=pb[:, 0:n], func=AF.Copy,
                        scale=pvec[:, PV_PSCALE + 2 * g + dch:PV_PSCALE + 2 * g + dch + 1]), r=[bb, b_pv], w=[bfo])
                    key = (8 + 2 * g + dch, c0)
                    b_mixT[key] = Buf("mixb")
                    DQ(CALL("dma_start", out=sc_mixT.ap()[8 + 2 * g + dch, :, c0:c0 + n], in_=fo[:, 0:n]),
                       r=[bfo], w=[b_mixT[key]])

        def gate_unit(wt, bw, u4):
            for i in range(9):
                rows = 128 if i < 8 else TS
                pb, pbb, bb = nb()
                for kc in range(16):
                    T(CALL("matmul",
                        out=pb[0:rows, 0:256], lhsT=hT[:, kc, i * 128:i * 128 + rows], rhs=wt[:, kc * 256:(kc + 1) * 256],
                        start=(kc == 0), stop=(kc == 15)), r=[bw, b_hT[i]], w=[bb])
                gi = nxt("gto")
                go, bgo = gto[gi], b_gto[gi]
                A(CALL("activation", out=go[0:rows, 0:256], in_=pb[0:rows, 0:256], func=AF.Silu),
                  r=[bb], w=[bgo])
                DQ(CALL("dma_start", out=sc_gate.ap()[i * 128:i * 128 + rows, u4 * 256:(u4 + 1) * 256],
                                                              in_=go[0:rows, 0:256]), r=[bgo], w=[b_sc_g[i]])

        GQ(CALL("dma_start", out=wab[:], in_=w_in.ap()[:, 4096:4112].rearrange("(k p) n -> p k n", p=128)), w=[b_wab])
        GQ(CALL("dma_start", out=wpl[:], in_=w_pool.ap().rearrange("g (c p) d -> p g c d", p=128)), w=[b_wpl])
        V(CALL("memset", Sst[:], 0.0), w=hb_S)
        V(CALL("memset", Sbf[:], 0.0), w=hb_Sb)
        for i2 in range(4):
            gens = []
            for xi in range(2):
                i = i2 * 2 + xi
                DQ(CALL("dma_start", out=xt[xi][:], in_=xall.ap()[i * 128:(i + 1) * 128, :]), w=[b_xt[xi]])
                gens.append(norm_transpose(xt[xi][:], [b_xt[xi]], 128, i * 128, i, PV_PREMIX, xi))
            interleave(gens)
        ab_tiles([(i, i, 128) for i in range(8)])
        pre_blocks = [(0, 512, 0, "n"), (512, 512, 512, "lastpre")]
        for u in range(4):
            wt, bw = load_w(w_in_unit(u * 256))
            interleave([qkv_chunk(wt, bw, j, u * 2 + j, [(1008, 16, None, "halo")], True, j) for j in range(2)])
        qkv_stream([(kind, u) for kind in (1, 2) for u in range(4)], pre_blocks, True)
        for u in range(4):
            wt, bw = load_w(w_in_unit(4112 + u * 256))
            interleave([u_chunk(wt, bw, j, u * 2 + j, [(1008, 16, "halo")], True, j) for j in range(2)])
        inherit(ALT_BUFS, INPROJ_LOW)
        gdn_loads(0, False, 0)
        for c in range(8):
            if c + 1 < 8:
                gdn_loads(c + 1, False, (c + 1) % 2)
            gdn_chunk(c, False, c % 2)
        inherit(INPROJ_LOW, ALT_BUFS)

        for i2 in range(5):
            gens = []
            for xi in range(2):
                i = i2 * 2 + xi
                if i > 8:
                    continue
                rows = 128 if i < 8 else TS
                DQ(CALL("dma_start", out=xt[xi][0:rows, :], in_=xall.ap()[TP + i * 128:TP + i * 128 + rows, :]), w=[b_xt[xi]])
                gens.append(norm_transpose(xt[xi][0:rows, :], [b_xt[xi]], rows, i * 128, i, PV_PREMIX, xi))
            interleave(gens)
        ab_tiles([(i, 8 + i, 128 if i < 8 else TS) for i in range(9)])
        main_blocks = [(0, 512, TP, "n"), (512, 512, TP + 512, "lastmain"), (1024, 16, None, "sample")]
        qkv_stream([(kind, u) for kind in (0, 1, 2) for u in range(4)], main_blocks, False)
        for u in range(4):
            wt, bw = load_w(w_in_unit(3072 + u * 256))
            gate_unit(wt, bw, u)
        for u in range(4):
            wt, bw = load_w(w_in_unit(4112 + u * 256))
            interleave([u_chunk(wt, bw, j, u * 2 + j, [(0, 512, "n"), (512, 512, "lastmain"), (1024, 16, "sample")], False, j)
                        for j in range(2)])
            pool_proj(u, [(0, 512), (512, 512)])

        tmT = arena2[:, 8 * NM:8 * NM + 8192].bitcast(F32) if False else None
        for half in range(2):
            for q4 in range(4):
                pb, pbb, bb = nb()
                for k in range(4):
                    chn = half * 16 + q4 * 4 + k
                    T(CALL("transpose", out=pb[0:32, k * 128:(k + 1) * 128], in_=rawtail[:, chn, :],
                                                                 identity=identf[:]), r=[b_rt, b_c], w=[bb])
                V(CALL("tensor_copy", out=xt[half][0:32, q4 * 512:(q4 + 1) * 512], in_=pb[0:32, :]),
                  r=[bb], w=[b_xt[half]])
        DQ(CALL("dma_start", out=o_conv_p.ap()[:, 0:2048], in_=xt[0][13:16, :]), r=[b_xt[0]], stream="out")
        DQ(CALL("dma_start", out=o_conv_p.ap()[:, 2048:3072], in_=xt[1][13:16, 0:1024]), r=[b_xt[1]], stream="out")
        DQ(CALL("dma_start", out=o_pool_p.ap(), in_=xt[1][1:16, 1024:2048]), r=[b_xt[1]], stream="out")
        DQ(CALL("dma_start", out=o_conv_s.ap()[:, 2, 0:2048], in_=xt[0][16:32, :]), r=[b_xt[0]], stream="out")
        DQ(CALL("dma_start", out=o_conv_s.ap()[:, 2, 2048:3072], in_=xt[1][16:32, 0:1024]), r=[b_xt[1]], stream="out")
        DQ(CALL("dma_start", out=o_pool_s.ap()[:, 14, :], in_=xt[1][16:32, 1024:2048]), r=[b_xt[1]], stream="out")
        DQ(CALL("dma_start", out=o_conv_s.ap()[:, 0:2, :], in_=st_conv.ap().rearrange("(t j) c -> t j c", j=3)[:, 1:3, :]), stream="out")
        DQ(CALL("dma_start", out=o_pool_s.ap()[:, 0:14, :], in_=st_pool.ap().rearrange("(t j) c -> t j c", j=15)[:, 1:15, :]), stream="out")

        a2f = arena2.bitcast(F32)
        stc = a2f[0:48, 0:3072]; b_stc = Buf("stc")
        inherit([b_stc], b_poolT + A2HI)
        DQ(CALL("dma_start", out=stc, in_=st_conv.ap()), w=[b_stc])
        stT = a2f[:, 5120:6272].rearrange("p (c t) -> p c t", c=24); b_stT = Buf("stT")
        inherit([b_stT], b_poolT + A2HI)
        for q6 in range(6):
            pb, pbb, bb = nb()
            for k in range(4):
                chn = q6 * 4 + k
                T(CALL("transpose", out=pb[:, k * 48:(k + 1) * 48], in_=stc[:, chn * 128:(chn + 1) * 128],
                                                             identity=identf[0:48, 0:48]), r=[b_stc, b_c], w=[bb])
            V(CALL("tensor_copy", out=stT[:, q6 * 4:q6 * 4 + 4, :], in_=pb[:, 0:192].rearrange("p (k t) -> p k t", k=4)),
              r=[bb], w=[b_stT])
        cwv = pvec[:, PV_CONV:PV_CONV + 96].rearrange("p (c j) -> p c j", j=4)
        stv = stT.rearrange("p c (t j) -> p c t j", j=3)
        sacc = sb("sacc", [128, 24, 16], F32); b_sacc = Buf("sacc")
        stmp = sb("stmp", [128, 24, 16], F32); b_stmp = Buf("stmp")
        V(CALL("tensor_tensor", out=sacc[:], in0=rawtail[:, 0:24, 16:32], in1=cwv[:, :, 3:4].to_broadcast([128, 24, 16]), op=ALU.mult),
          r=[b_rt, b_pv], w=[b_sacc])
        for jj in range(3):
            V(CALL("tensor_tensor", out=stmp[:], in0=stv[:, :, :, jj], in1=cwv[:, :, jj:jj + 1].to_broadcast([128, 24, 16]), op=ALU.mult),
              r=[b_stT, b_pv], w=[b_stmp])
            V(CALL("tensor_tensor", out=sacc[:], in0=sacc[:], in1=stmp[:], op=ALU.add), r=[b_sacc, b_stmp], w=[b_sacc])
        A(CALL("activation", out=stmp[:], in_=sacc[:], func=AF.Exp, scale=-1.0), r=[b_sacc], w=[b_stmp])
        A(CALL("activation", out=stmp[:], in_=stmp[:], func=AF.Ln, bias=1.0), r=[b_stmp], w=[b_stmp])
        A(CALL("activation", out=stmp[:], in_=stmp[:], func=AF.Exp, scale=-1.0), r=[b_stmp], w=[b_stmp])
        V(CALL("tensor_tensor", out=smp[:], in0=sacc[:], in1=stmp[:], op=ALU.mult), r=[b_sacc, b_stmp], w=[b_smp])
        sqs = sb("sqs", [128, 16, 16], BF16); b_sqs = Buf("sqs")
        V(CALL("tensor_tensor", out=sqs[:], in0=smp[:, 0:16, :], in1=smp[:, 0:16, :], op=ALU.mult), r=[b_smp], w=[b_sqs])
        pb, pbb, bb = nb()
        T(CALL("matmul", out=pb[:, 0:256], lhsT=onesb[:], rhs=sqs[:, :, :].rearrange("p c t -> p (c t)"), start=True, stop=True),
          r=[b_sqs, b_c], w=[bb])
        A(CALL("activation", out=stmp[:, 0:16, :].rearrange("p c t -> p (c t)"), in_=pb[:, 0:256], func=AF.Ln, bias=epst[:, 0:1]),
          r=[bb, b_c], w=[b_stmp])
        A(CALL("activation", out=stmp[:, 0:16, :], in_=stmp[:, 0:16, :], func=AF.Exp, scale=-0.5), r=[b_stmp], w=[b_stmp])
        V(CALL("tensor_tensor", out=smp[:, 0:16, :], in0=smp[:, 0:16, :], in1=stmp[:, 0:16, :], op=ALU.mult), r=[b_smp, b_stmp], w=[b_smp])
        V(CALL("tensor_scalar_mul", out=smp[:, 0:8, :], in0=smp[:, 0:8, :], scalar1=128.0 ** -0.5), r=[b_smp], w=[b_smp])
        spc = a2f[0:120, 3072:5120].rearrange("p (a c) -> p a c", a=2); b_spc = Buf("spc")
        inherit([b_spc], b_poolT + A2HI)
        DQ(CALL("dma_start", out=spc, in_=st_pool.ap().rearrange("(a r) c -> r a c", a=2)), w=[b_spc])
        spT = a2f[:, 6272:8192].rearrange("p (c t j) -> p c t j", c=8, j=15); b_spT = Buf("spT")
        inherit([b_spT], b_poolT + A2HI)
        for a_ in range(2):
            for q2 in range(2):
                pb, pbb, bb = nb()
                for k in range(4):
                    chn = q2 * 4 + k
                    T(CALL("transpose", out=pb[:, k * 120:(k + 1) * 120], in_=spc[:, a_, chn * 128:(chn + 1) * 128],
                                                                      identity=identf[0:120, 0:120]), r=[b_spc, b_c], w=[bb])
                V(CALL("tensor_copy",
                    out=spT[:, q2 * 4:q2 * 4 + 4, a_ * 8:a_ * 8 + 8, :],
                    in_=pb[:, 0:480].rearrange("p (k t j) -> p k t j", k=4, j=15)), r=[bb], w=[b_spT])
        spl = sb("spl", [128, 8, 16], F32); b_spl = Buf("spl")
        spb = sb("spb", [128, 8, 16], BF16); b_spb = Buf("spb")
        for g in range(4):
            win = 2 << g
            V(CALL("tensor_reduce", out=spl[:, 2 * g:2 * g + 2, :], in_=spT[:, 2 * g:2 * g + 2, :, 15 - (win - 1):15],
                                                      axis=mybir.AxisListType.X, op=ALU.add), r=[b_spT], w=[b_spl])
            V(CALL("tensor_tensor", out=spl[:, 2 * g:2 * g + 2, :], in0=spl[:, 2 * g:2 * g + 2, :],
                                             in1=rawtail[:, 24 + 2 * g:24 + 2 * g + 2, 16:32], op=ALU.add), r=[b_spl, b_rt], w=[b_spl])
            V(CALL("scalar_tensor_tensor", out=spb[:, 2 * g:2 * g + 2, :], in0=spl[:, 2 * g:2 * g + 2, :], scalar=1.0 / win,
                                                             in1=rawtail[:, 24 + 2 * g:24 + 2 * g + 2, 16:32], op0=ALU.mult, op1=ALU.subtract),
              r=[b_spl, b_rt], w=[b_spb])
        for g in range(4):
            for dch in range(2):
                pb, pbb, bb = nb()
                for c2 in range(2):
                    T(CALL("matmul", out=pb[:, 0:16], lhsT=wpl[:, g, c2, dch * 128:(dch + 1) * 128],
                                                                     rhs=spb[:, 2 * g + c2, :], start=(c2 == 0), stop=(c2 == 1)),
                      r=[b_wpl, b_spb], w=[bb])
                fi = nxt("fmo")
                fo, bfo = fmo[fi], b_fmo[fi]
                A(CALL("activation", out=fo[:, 0:16], in_=pb[:, 0:16], func=AF.Copy,
                                                                     scale=pvec[:, PV_PSCALE + 2 * g + dch:PV_PSCALE + 2 * g + dch + 1]),
                  r=[bb, b_pv], w=[bfo])
                key = (8 + 2 * g + dch, 1024)
                b_mixT[key] = Buf("mixs")
                DQ(CALL("dma_start", out=sc_mixT.ap()[8 + 2 * g + dch, :, 1024:1040], in_=fo[:, 0:16]),
                   r=[bfo], w=[b_mixT[key]])

        inherit(ALT_BUFS, INPROJ_LOW)
        gdn_loads(8, True, 0)
        for c in range(8, 16):
            if c + 1 < 16:
                gdn_loads(c + 1, True, (c + 1) % 2)
            gdn_chunk(c, True, c % 2)
        DQ(CALL("dma_start", out=o_gdn_p.ap().rearrange("h d e -> d h e"), in_=Sst[:]), r=hb_S, stream="out")

        A1ALL = [b_raw[0], b_raw[1], b_acc, b_sqb, b_rs, b_fmo[0], b_fmo[1], b_tmo[0], b_tmo[1], b_gto[0], b_uA, b_uB,
                 b_uraw[0], b_uraw[1], b_kTc[0], b_qTc[0], b_ktm[0], b_vtm[0], b_GU, b_bD, b_DT, b_Lm[0], b_Lm[1],
                 b_LTm[0], b_LTm[1], b_Tt[0], b_Tt[1], b_kbg, b_kd, b_nwT, b_vnew, b_vb, b_intra, b_otm, b_oab, b_oaT,
                 b_gcs, b_sqs_] + b_gate_l + b_vTc_l + b_kTc + b_qTc + list(GDN_BUFS) + hb_km + hb_vm
        gs = gb[0:16, 16, 0:8]
        bs = gb[0:16, 16, 16:24]
        gd = sb("gd", [16, 2, 8, 16], F32); b_gd = Buf("gd")
        i16 = identf[0:16, 0:16]
        V(CALL("tensor_tensor", out=gd[:, 0, :, :], in0=i16.unsqueeze(1).to_broadcast([16, 8, 16]),
                                    in1=gs.unsqueeze(2).to_broadcast([16, 8, 16]), op=ALU.mult), r=[b_gb, b_c], w=[b_gd])
        V(CALL("tensor_tensor", out=gd[:, 1, :, :], in0=i16.unsqueeze(1).to_broadcast([16, 8, 16]),
                                    in1=bs.unsqueeze(2).to_broadcast([16, 8, 16]), op=ALU.mult), r=[b_gb, b_c], w=[b_gd])
        pb, pbb, bb = nb()
        T(CALL("matmul", out=pb[:, 0:256], lhsT=onesf[0:16, :], rhs=gd[:, :, :, :].rearrange("t a h u -> t (a h u)"),
                                    start=True, stop=True), r=[b_gd, b_c], w=[bb])
        egb = sb("egb", [128, 2, 8, 16], F32); b_egb = Buf("egb")
        A(CALL("activation", out=egb[:, 0, :, :].rearrange("p h t -> p (h t)"), in_=pb[:, 0:128], func=AF.Exp), r=[bb], w=[b_egb])
        V(CALL("tensor_copy", out=egb[:, 1, :, :].rearrange("p h t -> p (h t)"), in_=pb[:, 128:256]), r=[bb], w=[b_egb])
        qk = arena1[:, 18304:18432].rearrange("p (h t) -> p h t", h=8); b_qk = Buf("qk")
        inherit([b_qk], A1ALL)
        V(CALL("tensor_tensor", out=qk, in0=smp[:, 0:8, :], in1=smp[:, 8:16, :], op=ALU.mult), r=[b_smp], w=[b_qk])
        pb2, _, bb2 = nb()
        T(CALL("matmul", out=pb2[:, 0:128], lhsT=onesf[:], rhs=qk.rearrange("p h t -> p (h t)"), start=True, stop=True),
          r=[b_qk, b_c], w=[bb2])
        V(CALL("tensor_copy", out=qk.rearrange("p h t -> p (h t)"), in_=pb2[:, 0:128]), r=[bb2], w=[b_qk])
        kq2 = arena1[:, 18432:18688].rearrange("p (h t a) -> p h t a", h=8, a=2); b_kq2 = Buf("kq2")
        inherit([b_kq2], A1ALL)
        V(CALL("tensor_copy", out=kq2[:, :, :, 0], in_=smp[:, 8:16, :]), r=[b_smp], w=[b_kq2])
        V(CALL("tensor_copy", out=kq2[:, :, :, 1], in_=smp[:, 0:8, :]), r=[b_smp], w=[b_kq2])
        SS = [arena1[:, 0:4096].rearrange("p (n e) -> p n e", e=128), arena1[:, 4096:8192].rearrange("p (n e) -> p n e", e=128)]
        b_SS = [Buf("SS0"), Buf("SS1")]
        inherit(b_SS, A1ALL)
        ksq = arena1[:, 18688:18944].rearrange("p (h t a) -> p h t a", h=8, a=2); b_ksq = Buf("ksq")
        inherit([b_ksq], A1ALL)
        for qtr in range(4):
            si = qtr % 2
            DQ(CALL("dma_start", out=SS[si].rearrange("p (t h) e -> p t h e", h=8),
                                                    in_=st_gdn.ap()[qtr * 4:(qtr + 1) * 4].rearrange("t h d e -> d t h e")),
               w=[b_SS[si]])
            pb, pbb, bb = nb()
            for tl in range(4):
                t = qtr * 4 + tl
                for h in range(8):
                    n_ = tl * 8 + h
                    T(CALL("matmul", out=pb[:, n_ * 2:n_ * 2 + 2], lhsT=SS[si][:, n_, :], rhs=kq2[:, h, t, :],
                                                                       start=True, stop=True), r=[b_SS[si], b_kq2], w=[bb])
            V(CALL("tensor_copy", out=ksq[:, :, qtr * 4:(qtr + 1) * 4, :],
                                                      in_=pb[:, 0:64].rearrange("p (t h a) -> p h t a", h=8, a=2)), r=[bb], w=[b_ksq])
            if qtr == 3:
                break
        dl = arena1[:, 17920:18048].rearrange("p (h t) -> p h t", h=8); b_dl = Buf("dl")
        osT = arena1[:, 18048:18176].rearrange("p (h t) -> p h t", h=8); b_osT = Buf("osT")
        inherit([b_dl, b_osT], A1ALL)
        V(CALL("tensor_tensor", out=dl, in0=ksq[:, :, :, 0], in1=egb[:, 0, :, :], op=ALU.mult), r=[b_ksq, b_egb], w=[b_dl])
        V(CALL("tensor_tensor", out=dl, in0=smp[:, 16:24, :], in1=dl, op=ALU.subtract), r=[b_smp, b_dl], w=[b_dl])
        V(CALL("tensor_tensor", out=dl, in0=dl, in1=egb[:, 1, :, :], op=ALU.mult), r=[b_dl, b_egb], w=[b_dl])
        V(CALL("tensor_tensor", out=osT, in0=ksq[:, :, :, 1], in1=egb[:, 0, :, :], op=ALU.mult), r=[b_ksq, b_egb], w=[b_osT])
        V(CALL("tensor_tensor", out=qk, in0=qk, in1=dl, op=ALU.mult), r=[b_qk, b_dl], w=[b_qk])
        V(CALL("tensor_tensor", out=osT, in0=osT, in1=qk, op=ALU.add), r=[b_osT, b_qk], w=[b_osT])
        V(CALL("tensor_tensor", out=sqs[:, 0:8, :], in0=osT, in1=osT, op=ALU.mult), r=[b_osT], w=[b_sqs])
        pb, pbb, bb = nb()
        T(CALL("matmul", out=pb[:, 0:128], lhsT=onesb[:], rhs=sqs[:, 0:8, :].rearrange("p h t -> p (h t)"), start=True, stop=True),
          r=[b_sqs, b_c], w=[bb])
        rso = arena1[:, 18176:18304].rearrange("p (h t) -> p h t", h=8); b_rso = Buf("rso")
        inherit([b_rso], A1ALL)
        V(CALL("tensor_scalar", out=rso.rearrange("p h t -> p (h t)"), in0=pb[:, 0:128], scalar1=1.0 / 128, scalar2=EPS,
                                           op0=ALU.mult, op1=ALU.add), r=[bb], w=[b_rso])
        A(CALL("activation", out=rso, in_=rso, func=AF.Ln), r=[b_rso], w=[b_rso])
        A(CALL("activation", out=rso, in_=rso, func=AF.Exp, scale=-0.5), r=[b_rso], w=[b_rso])
        V(CALL("tensor_tensor", out=osT, in0=osT, in1=rso, op=ALU.mult), r=[b_osT, b_rso], w=[b_osT])
        V(CALL("tensor_scalar_mul", out=osT, in0=osT, scalar1=pvec[:, PV_GDNOUT:PV_GDNOUT + 1]), r=[b_osT, b_pv], w=[b_osT])
        gsm = arena1b[0:16, 2 * 17408:2 * 17408 + 1024]; b_gsm = Buf("gsm")
        inherit([b_gsm], A1ALL)
        DQ(CALL("dma_start", out=gsm[:], in_=sc_gate.ap()[1024:1040, :]), r=[b_sc_g[8]], w=[b_gsm])
        pb, pbb, bb = nb()
        for h in range(8):
            T(CALL("transpose", out=pbb[:, h * 16:(h + 1) * 16], in_=gsm[:, h * 128:(h + 1) * 128], identity=identb[0:16, 0:16]),
              r=[b_gsm, b_c], w=[bb])
        osb = sb("osb", [128, 8, 16], BF16); b_osb = Buf("osb")
        V(CALL("tensor_tensor", out=osb[:], in0=osT, in1=pbb[:, 0:128].rearrange("p (h t) -> p h t", h=8), op=ALU.mult),
          r=[b_osT, bb], w=[b_osb])
        b_mixT[(0, 8)] = Buf("mixs0")
        DQ(CALL("dma_start", out=sc_mixT.ap()[0:8, :, 1024:1040].rearrange("h e t -> e h t"), in_=osb[:]), r=[b_osb], w=[b_mixT[(0, 8)]])
        dlb = sb("dlb", [128, 8, 16], BF16); b_dlb = Buf("dlb")
        V(CALL("tensor_copy", out=dlb[:], in_=dl), r=[b_dl], w=[b_dlb])
        ksb = sb("ksb", [128, 8, 16], BF16); b_ksb = Buf("ksb")
        V(CALL("tensor_copy", out=ksb[:], in_=smp[:, 8:16, :]), r=[b_smp], w=[b_ksb])
        ktmS = arena1b[0:16, 2 * 16384:2 * 16384 + 1024].rearrange("p (h d) -> p h d", h=8); b_ktmS = Buf("ktmS")
        dtmS = arena1b[0:16, 2 * 16896:2 * 16896 + 1024].rearrange("p (h d) -> p h d", h=8); b_dtmS = Buf("dtmS")
        inherit([b_ktmS, b_dtmS], A1ALL)
        for (srcT, dstT, bsrc, bdst) in ((ksb, ktmS, b_ksb, b_ktmS), (dlb, dtmS, b_dlb, b_dtmS)):
            pb, pbb, bb = nb()
            for h in range(8):
                T(CALL("transpose", out=pbb[0:16, h * 128:(h + 1) * 128], in_=srcT[:, h, :], identity=identb[:]),
                  r=[bsrc, b_c], w=[bb])
            V(CALL("tensor_copy", out=dstT[:], in_=pbb[0:16, 0:1024].rearrange("p (h d) -> p h d", h=8)), r=[bb], w=[bdst])
        dmask = arena1b[0:16, 2 * 8192:2 * 16384].rearrange("p (t h e) -> p t h e", t=16, h=8); b_dmask = Buf("dmask")
        inherit([b_dmask], A1ALL)
        for t in range(16):
            V(CALL("tensor_scalar_mul", out=dmask[:, t, :, :], in0=dtmS[:], scalar1=identf[0:16, t:t + 1]), r=[b_dtmS, b_c], w=[b_dmask])
        for qtr in range(4):
            si = qtr % 2
            DQ(CALL("dma_start", out=SS[si].rearrange("p (t h) e -> p t h e", h=8),
                                                    in_=st_gdn.ap()[qtr * 4:(qtr + 1) * 4].rearrange("t h d e -> d t h e")),
               w=[b_SS[si]])
            for tl in range(4):
                t = qtr * 4 + tl
                for hh in range(2):
                    pb, pbb, bb = nb()
                    for h4 in range(4):
                        h = hh * 4 + h4
                        T(CALL("matmul", out=pb[:, h4 * 128:(h4 + 1) * 128], lhsT=ktmS[:, h, :], rhs=dmask[:, t, h, :],
                                                                     start=True, stop=True), r=[b_ktmS, b_dmask], w=[bb])
                    sv = SS[si][:, tl * 8 + hh * 4:tl * 8 + hh * 4 + 4, :]
                    V(CALL("tensor_tensor", out=sv, in0=sv, in1=egb[:, 0, hh * 4:hh * 4 + 4, t:t + 1].to_broadcast([128, 4, 128]),
                                                                   op=ALU.mult), r=[b_SS[si], b_egb], w=[b_SS[si]])
                    V(CALL("tensor_tensor", out=sv, in0=sv, in1=pb[:, :].rearrange("p (h e) -> p h e", h=4), op=ALU.add),
                      r=[b_SS[si], bb], w=[b_SS[si]])
            DQ(CALL("dma_start", out=o_gdn_s.ap()[qtr * 4:(qtr + 1) * 4].rearrange("t h d e -> d t h e"),
                                                    in_=SS[si].rearrange("p (t h) e -> p t h e", h=8)), r=[b_SS[si]], stream="out")

        ffa = arena1[:, 0:9 * 2048].rearrange("p (i c) -> p i c", i=9)
        b_ffa = [Buf("ffa%d" % i) for i in range(9)]
        inherit(b_ffa, b_SS + A1ALL + [b_gsm, b_ktmS, b_dtmS, b_dmask, b_qk, b_kq2, b_ksq, b_dl, b_osT, b_rso])
        mxa = arena2[:, :].rearrange("p (k t) -> p k t", k=16)
        b_mxa = Buf("mxa")
        inherit([b_mxa], b_poolT + [b_stc, b_stT, b_spc, b_spT] + A2HI)
        DQ(CALL("dma_start", out=mxa, in_=sc_mixT.ap().rearrange("k e t -> e k t")), r=list(b_mixT.values()), w=[b_mxa])
        nrm = xt[1]; b_nrm = b_xt[1]
        DQ(CALL("dma_start", out=nrm[:], in_=nrm_post.ap()[0]), w=[b_nrm])

        def w_unit(src, c0):
            def f(t_):
                return t_[:, :].rearrange("p (k n) -> p k n", k=16)
            f.src = src.ap()[:, c0:c0 + 256].rearrange("(k p) n -> p k n", p=128)
            return f

        for n8 in range(8):
            wt, bw = load_w(w_unit(w_out, n8 * 256))
            for i in range(9):
                rows = 128 if i < 8 else TS
                pb, pbb, bb = nb()
                for kc in range(16):
                    T(CALL("matmul",
                        out=pb[0:rows, 0:256], lhsT=mxa[:, kc, i * 128:i * 128 + rows], rhs=wt[:, kc * 256:(kc + 1) * 256],
                        start=(kc == 0), stop=(kc == 15)), r=[bw, b_mxa], w=[bb])
                A(CALL("activation", out=ffa[0:rows, i, n8 * 256:(n8 + 1) * 256], in_=pb[0:rows, 0:256], func=AF.Copy),
                  r=[bb], w=[b_ffa[i]])

        xt2 = arena2[:, 0:4096].bitcast(F32); b_xt2 = Buf("xt2")
        RES = [(xt[0], b_xt[0]), (xt2, b_xt2)]
        b_y = [Buf("y%d" % i) for i in range(9)]

        def pnr_chain(i, si, phase):
            rows = 128 if i < 8 else TS
            xb_, bxb, stc_, bst = NT_SETS[si]
            rt, brt = RES[si]
            fv = ffa[0:rows, i, :]
            if phase == 4:
                DQ(CALL("dma_start", out=rt[0:rows, :], in_=xall.ap()[TP + i * 128:TP + i * 128 + rows, :]), w=[brt])
            else:
                DQ(CALL("dma_start", out=rt[0:rows, :], in_=y_main.ap()[i * 128:i * 128 + rows, :]), r=[b_y[i]], w=[brt])
            A(CALL("activation", out=xb_[0:rows, :], in_=fv, func=AF.Square, accum_out=stc_[0:rows, 0:1]), r=[b_ffa[i]], w=[bxb, bst])
            yield
            rstd_from_ss(stc_[0:rows, 0:1], rows, 1.0 / D, [bst], [bst])
            yield
            V(CALL("scalar_tensor_tensor", out=fv, in0=fv, scalar=stc_[0:rows, 0:1], in1=nrm[0:rows, :], op0=ALU.mult, op1=ALU.mult),
              r=[b_ffa[i], bst, b_nrm], w=[b_ffa[i]])
            yield
            V(CALL("tensor_tensor", out=fv, in0=fv, in1=rt[0:rows, :], op=ALU.add), r=[b_ffa[i], brt], w=[b_ffa[i]])
            yield
            if phase == 4:
                DQ(CALL("dma_start", out=y_main.ap()[i * 128:i * 128 + rows, :], in_=fv), r=[b_ffa[i]], w=[b_y[i]])
                for _ in norm_transpose(fv, [b_ffa[i]], rows, i * 128, i, PV_PREMLP, si):
                    yield
            else:
                DQ(CALL("dma_start", out=y_main.ap()[i * 128:i * 128 + rows, :], in_=fv), r=[b_ffa[i]], w=[b_y[i]], stream="out")

        inherit([b_xt2, b_xb2], [b_mxa])
        for i2 in range(5):
            interleave([pnr_chain(i, si, 4) for si, i in enumerate((2 * i2, 2 * i2 + 1)) if i < 9], lag=3)

        upT = [arena2[:, 0:8 * NM].rearrange("p (c t) -> p c t", c=8), arena2[:, 8 * NM:16 * NM].rearrange("p (c t) -> p c t", c=8)]
        b_upT = [Buf("upT0"), Buf("upT1")]
        inherit(b_upT, [b_mxa, b_xt2, b_xb2])
        DQ(CALL("dma_start", out=nrm[:], in_=nrm_post.ap()[1]), w=[b_nrm])
        blocks5 = [(0, 512), (512, 512), (1024, 16)]
        for G in range(8):
            ui = G % 2
            for u4 in range(4):
                wt, bw = load_w(w_unit(w_up, G * 1024 + u4 * 256))
                for j in range(2):
                    fc = u4 * 2 + j
                    for (c0, n) in blocks5:
                        pb, pbb, bb = nb()
                        tiles = list(range(c0 // 128, (c0 + n - 1) // 128 + 1))
                        for kc in range(16):
                            T(CALL("matmul",
                                out=pb[:, 0:n], lhsT=wt[:, kc * 256 + j * 128:kc * 256 + (j + 1) * 128], rhs=hT[:, kc, c0:c0 + n],
                                start=(kc == 0), stop=(kc == 15)), r=[bw] + [b_hT[t_] for t_ in tiles], w=[bb])
                        A(CALL("activation", out=upT[ui][:, fc, c0:c0 + n], in_=pb[:, 0:n], func=AF.Relu),
                          r=[bb], w=[b_upT[ui]])
                        V(CALL("tensor_tensor", out=upT[ui][:, fc, c0:c0 + n], in0=upT[ui][:, fc, c0:c0 + n],
                                                                             in1=upT[ui][:, fc, c0:c0 + n], op=ALU.mult),
                          r=[b_upT[ui]], w=[b_upT[ui]])
            for n8 in range(8):
                i_ = wrr[0]
                wrr[0] = (i_ + 1) % 2
                wt = wbuf[i_]
                bw = b_wbuf[i_]
                GQ(CALL("dma_start",
                    out=wt[:, 0:2048].rearrange("p (k n) -> p k n", k=8),
                    in_=w_down.ap()[G * 1024:(G + 1) * 1024, n8 * 256:(n8 + 1) * 256].rearrange("(k p) n -> p k n", p=128)), w=[bw])
                for i in range(9):
                    rows = 128 if i < 8 else TS
                    pb, pbb, bb = nb()
                    for fc in range(8):
                        T(CALL("matmul",
                            out=pb[0:rows, 0:256], lhsT=upT[ui][:, fc, i * 128:i * 128 + rows], rhs=wt[:, fc * 256:(fc + 1) * 256],
                            start=(fc == 0), stop=(fc == 7)), r=[bw, b_upT[ui]], w=[bb])
                    dv = ffa[0:rows, i, n8 * 256:(n8 + 1) * 256]
                    if G == 0:
                        V(CALL("tensor_copy", out=dv, in_=pb[0:rows, 0:256]), r=[bb], w=[b_ffa[i]])
                    else:
                        V(CALL("tensor_tensor", out=dv, in0=dv, in1=pb[0:rows, 0:256], op=ALU.add),
                          r=[bb, b_ffa[i]], w=[b_ffa[i]])
        inherit([b_xt2, b_xb2], b_upT)
        for i2 in range(5):
            interleave([pnr_chain(i, si, 5) for si, i in enumerate((2 * i2, 2 * i2 + 1)) if i < 9], lag=3)
        S.emit(st, final_streams=["out"])
    return nc


def _prep(inputs):
    f = lambda k: np.ascontiguousarray(np.asarray(inputs[k], dtype=np.float32))
    xp = f("x_prompt"); xs = f("x_sample")
    conv_w = f("conv_w")[0]
    pvec = np.zeros((128, PV), np.float32)
    pvec[:, PV_CONV:PV_CONV + 96] = conv_w.reshape(4, 24, 128).transpose(2, 1, 0).reshape(128, 96)
    pvec[:, PV_PREMIX:PV_PREMIX + 16] = f("norm_pre_mix")[0].reshape(16, 128).T
    pvec[:, PV_PREMLP:PV_PREMLP + 16] = f("norm_pre_mlp")[0].reshape(16, 128).T
    pvec[:, PV_PSCALE:PV_PSCALE + 8] = f("pool_scale")[0].reshape(8, 128).T
    pvec[:, PV_GDNOUT] = f("norm_gdn_out")[0]
    nrm_post = np.stack([np.broadcast_to(f("norm_post_mix")[0], (128, D)), np.broadcast_to(f("norm_post_mlp")[0], (128, D))]).copy()
    shared = dict(w_in=f("w_in")[0], w_pool=f("w_pool")[0], w_out=f("w_out")[0], w_up=f("w_up")[0], w_down=f("w_down")[0],
                  pvec=pvec, nrm_post=nrm_post)
    maps = []
    for c in range(8):
        b, s = c // 2, c % 2
        xall = np.zeros((NT, D), np.float32)
        if s == 1:
            xall[0:TP] = xp[b, 0:TP]
        xall[TP:2 * TP] = xp[b, s * TP:(s + 1) * TP]
        xall[2 * TP:] = xs[c * TS:(c + 1) * TS, 0]
        rows = np.zeros((RB,), np.float32)
        rows[RB_GDNOUT:RB_GDNOUT + 128] = f("norm_gdn_out")[0]
        rows[RB_ALOG:RB_ALOG + 8] = f("a_log")[0]
        rows[RB_DT:RB_DT + 8] = f("dt_bias")[0]
        for g in range(4):
            pos = s * TP + np.arange(16)
            rows[RB_INVC + g * 16:RB_INVC + (g + 1) * 16] = 1.0 / np.minimum(pos + 1, 2 << g)
        m = dict(shared)
        m.update(xall=xall, rows_bc=np.broadcast_to(rows, (128, RB)).copy(),
                 st_gdn=f("state_gdn")[0, c * TS:(c + 1) * TS], st_conv=f("state_conv")[0, c * TS:(c + 1) * TS].reshape(TS * 3, 3072),
                 st_pool=f("state_pool")[0, c * TS:(c + 1) * TS].reshape(TS * 15, 1024))
        maps.append(m)
    return maps


_NC = [None]


def kernel(**inputs):
    maps = _prep(inputs)
    if _NC[0] is None:
        _NC[0] = build_program()
    res = run_bass_kernel_spmd(_NC[0], maps, core_ids=list(range(8))).results
    yp = np.zeros((4, 2048, D), np.float32)
    ys = np.zeros((128, 1, D), np.float32)
    gp = np.zeros((1, 4, 8, 128, 128), np.float32)
    cp = np.zeros((1, 4, 3, 3072), np.float32)
    pp = np.zeros((1, 4, 15, 1024), np.float32)
    gs = np.zeros((1, 128, 8, 128, 128), np.float32)
    cs = np.zeros((1, 128, 3, 3072), np.float32)
    ps = np.zeros((1, 128, 15, 1024), np.float32)
    for c in range(8):
        b, s = c // 2, c % 2
        r = res[c]
        yp[b, s * TP:(s + 1) * TP] = r["y_main"][0:TP]
        ys[c * TS:(c + 1) * TS, 0] = r["y_main"][TP:]
        if s == 1:
            gp[0, b] = r["o_gdn_p"]
            cp[0, b] = r["o_conv_p"]
            pp[0, b] = r["o_pool_p"]
        gs[0, c * TS:(c + 1) * TS] = r["o_gdn_s"]
        cs[0, c * TS:(c + 1) * TS] = r["o_conv_s"]
        ps[0, c * TS:(c + 1) * TS] = r["o_pool_s"]
    return (yp, ys, gp, cp, pp, gs, cs, ps)
```

```python
from contextlib import ExitStack
import numpy as np
import concourse.bass as bass
import concourse.mybir as mybir
from concourse.bass_utils import run_bass_kernel_spmd

F32 = mybir.dt.float32
BF16 = mybir.dt.bfloat16
AF = mybir.ActivationFunctionType
ALU = mybir.AluOpType

ENGS = ("tensor", "vector", "scalar", "gpsimd", "sync")
D = 2048
NIN = 5136
TP = 1024
TS = 16
NT = 2 * TP + TS
NM = TP + TS
EPS = 1e-6
NEG = -30000.0
SAME_SYNC = True


class Buf:
    __slots__ = ("name", "w", "r", "excl")

    def __init__(self, name, excl=False):
        self.name = name
        self.w = None
        self.r = []
        self.excl = excl


import sys as _sys


def CALL(name, *a, **kw):
    ln = _sys._getframe(1).f_lineno

    def f(e):
        ins = getattr(e, name)(*a, **kw)
        try:
            ins.annotate("L%d" % ln)
        except Exception:
            pass
        return ins
    return f


def inherit(new_bufs, old_bufs):
    for nb in new_bufs:
        for ob in old_bufs:
            if ob.w is not None:
                nb.r.append(ob.w)
            nb.r.extend(ob.r)


class Op:
    __slots__ = ("eng", "fn", "deps", "dma", "stream", "flag", "cnt")

    def __init__(self, eng, fn, dma, stream):
        self.eng = eng
        self.fn = fn
        self.deps = []
        self.dma = dma
        self.stream = stream
        self.flag = False
        self.cnt = None


class Sched:
    def __init__(self, nc, same_engine_sync=True):
        self.nc = nc
        self.ops = {e: [] for e in ENGS}
        self.same = same_engine_sync

    def op(self, eng, fn, reads=(), writes=(), dma=False, stream=None):
        o = Op(eng, fn, dma, stream if dma else None)
        ex = [b for b in reads if b.excl]
        if ex:
            reads = [b for b in reads if not b.excl]
            writes = list(writes) + [b for b in ex if b not in writes]
        deps = []
        for b in reads:
            if b.w is not None:
                deps.append(b.w)
        for b in writes:
            if b.w is not None:
                deps.append(b.w)
            deps.extend(b.r)
        seen = set()
        for d in deps:
            if id(d) in seen or d is o:
                continue
            seen.add(id(d))
            if (not d.dma) and d.eng == eng and (eng == "tensor" or (not self.same and eng != "gpsimd")):
                continue
            o.deps.append(d)
            d.flag = True
        for b in reads:
            b.r.append(o)
        for b in writes:
            b.w = o
            b.r = []
        self.ops[eng].append(o)
        return o

    def emit(self, stack, final_streams=()):
        nc = self.nc
        pool_sz = {"sync": 16, "gpsimd": 6, "scalar": 4, "vector": 2, "tensor": 2}
        sems = {}
        for e in ENGS:
            sems[("eng", e)] = stack.enter_context(nc.semaphore("s_" + e))
        cnt = {k: 0 for k in sems}
        dma_i = {e: 0 for e in ENGS}
        final_waits = {}
        for e in ENGS:
            for o in self.ops[e]:
                if o.dma:
                    o.flag = True
                if not o.flag:
                    continue
                if o.dma:
                    i = dma_i[e]
                    dma_i[e] += 1
                    R = pool_sz[e]
                    k = ("dma", e, i % R)
                    if k not in sems:
                        sems[k] = stack.enter_context(nc.semaphore("d_%s_%d" % (e, i % R)))
                    o.cnt = (k, 16 * (i // R + 1))
                    if o.stream in final_streams:
                        final_waits[k] = max(final_waits.get(k, 0), o.cnt[1])
                else:
                    k = ("eng", e)
                    cnt[k] += 1
                    o.cnt = (k, cnt[k])
        block = stack.enter_context(nc.Block())

        def run(ename, eng):
            waited = {}
            for o in self.ops[ename]:
                need = {}
                for d in o.deps:
                    k, c = d.cnt
                    if c > need.get(k, 0):
                        need[k] = c
                if o.dma and o.cnt[1] > 16:
                    k, c = o.cnt
                    need[k] = max(need.get(k, 0), c - 16)
                for k, c in need.items():
                    if waited.get(k, 0) >= c:
                        continue
                    eng.wait_ge(sems[k], c)
                    waited[k] = c
                ins = o.fn(eng)
                if o.flag:
                    k, c = o.cnt
                    ins.then_inc(sems[k], 16 if o.dma else 1)
            if ename == "sync":
                for k, c in final_waits.items():
                    if waited.get(k, 0) < c:
                        eng.wait_ge(sems[k], c)

        @block.tensor
        def _(e):
            run("tensor", e)

        @block.vector
        def _(e):
            run("vector", e)

        @block.scalar
        def _(e):
            run("scalar", e)

        @block.gpsimd
        def _(e):
            run("gpsimd", e)

        @block.sync
        def _(e):
            run("sync", e)


RB_GDNOUT = 0
RB_ALOG = 128
RB_DT = 136
RB_INVC = 144
RB = 144 + 64
PV_CONV = 0
PV_PREMIX = 96
PV_PREMLP = 112
PV_PSCALE = 128
PV_GDNOUT = 136
PV = 137


def build_program(debug=False):
    nc = bass.Bass("TRN2", target_bir_lowering=False)
    dt_in = lambda n, s: nc.dram_tensor(n, s, F32, kind="ExternalInput")
    dt_out = lambda n, s: nc.dram_tensor(n, s, F32, kind="ExternalOutput")
    xall = dt_in("xall", [NT, D])
    st_gdn = dt_in("st_gdn", [TS, 8, 128, 128])
    st_conv = dt_in("st_conv", [TS * 3, 3072])
    st_pool = dt_in("st_pool", [TS * 15, 1024])
    w_in = dt_in("w_in", [D, NIN])
    w_pool = dt_in("w_pool", [4, 256, 256])
    w_out = dt_in("w_out", [D, D])
    w_up = dt_in("w_up", [D, 4 * D])
    w_down = dt_in("w_down", [4 * D, D])
    rows_bc = dt_in("rows_bc", [128, RB])
    nrm_post = dt_in("nrm_post", [2, 128, D])
    pvec_d = dt_in("pvec", [128, PV])
    y_main = dt_out("y_main", [NM, D])
    o_gdn_p = dt_out("o_gdn_p", [8, 128, 128])
    o_conv_p = dt_out("o_conv_p", [3, 3072])
    o_pool_p = dt_out("o_pool_p", [15, 1024])
    o_gdn_s = dt_out("o_gdn_s", [TS, 8, 128, 128])
    o_conv_s = dt_out("o_conv_s", [TS, 3, 3072])
    o_pool_s = dt_out("o_pool_s", [TS, 15, 1024])
    sc_qT = nc.dram_tensor("sc_qT", [8, 128, TP], BF16, kind="Internal")
    sc_kT = nc.dram_tensor("sc_kT", [8, 128, 2 * TP], BF16, kind="Internal")
    sc_vT = nc.dram_tensor("sc_vT", [8, 128, 2 * TP], BF16, kind="Internal")
    sc_gate = nc.dram_tensor("sc_gate", [NM, 1024], BF16, kind="Internal")
    sc_mixT = nc.dram_tensor("sc_mixT", [16, 128, NM], BF16, kind="Internal")

    with ExitStack() as st:
        S = Sched(nc, same_engine_sync=SAME_SYNC)
        sb = lambda n, s, d: st.enter_context(nc.sbuf_tensor(n, s, d))
        V = lambda fn, r=(), w=(): S.op("vector", fn, r, w)
        A = lambda fn, r=(), w=(): S.op("scalar", fn, r, w)
        T = lambda fn, r=(), w=(): S.op("tensor", fn, r, w)
        P = lambda fn, r=(), w=(): S.op("gpsimd", fn, r, w)
        DQ = lambda fn, r=(), w=(), stream="io": S.op("sync", fn, r, w, dma=True, stream=stream)
        GQ = lambda fn, r=(), w=(): S.op("gpsimd", fn, r, w, dma=True, stream="w")

        banks = []
        for i in range(8):
            t = st.enter_context(nc.psum_tensor("pb%d" % i, [128, 512], F32))
            banks.append((t, t.bitcast(BF16), Buf("pb%d" % i, True)))
        bank_rr = [0]

        def nb():
            i = bank_rr[0]
            bank_rr[0] = (i + 1) % 8
            return banks[i]

        def nb2():
            i = bank_rr[0]
            if i % 2:
                i = (i + 1) % 8
            bank_rr[0] = (i + 2) % 8
            return banks[i], banks[i + 1]

        identf = sb("identf", [128, 128], F32); b_c = Buf("consts")
        identb = sb("identb", [128, 128], BF16)
        onesb = sb("onesb", [128, 128], BF16)
        onesf = sb("onesf", [128, 128], F32)
        utri = sb("utri", [128, 128], F32)
        lstr = sb("lstr", [128, 128], F32)
        negs = sb("negs", [128, 128], F32)
        negi = sb("negi", [128, 128], F32)
        epst = sb("epst", [128, 1], F32)
        rbc = sb("rbc", [128, RB], F32); b_rbc = Buf("rbc")
        pvec = sb("pvec_sb", [128, PV], F32); b_pv = Buf("pvec")
        nalog = sb("nalog", [128, 8], F32)

        def aff(t, pattern, cm, op, fill):
            P(CALL("affine_select", out=t[:], in_=t[:], pattern=pattern, compare_op=op, fill=fill,
                                        base=0, channel_multiplier=cm), r=[b_c], w=[b_c])

        P(CALL("memset", identf[:], 1.0), w=[b_c])
        aff(identf, [[-1, 128]], 1, ALU.is_equal, 0.0)
        P(CALL("memset", utri[:], 1.0), w=[b_c])
        aff(utri, [[1, 128]], -1, ALU.is_ge, 0.0)
        P(CALL("memset", lstr[:], 1.0), w=[b_c])
        aff(lstr, [[-1, 128]], 1, ALU.is_gt, 0.0)
        P(CALL("memset", negs[:], 0.0), w=[b_c])
        aff(negs, [[-1, 128]], 1, ALU.is_gt, NEG)
        P(CALL("memset", negi[:], 0.0), w=[b_c])
        aff(negi, [[1, 128]], -1, ALU.is_ge, NEG)
        P(CALL("memset", onesf[:], 1.0), w=[b_c])
        P(CALL("memset", onesb[:], 1.0), w=[b_c])
        P(CALL("memset", epst[:], EPS), w=[b_c])
        V(CALL("tensor_copy", out=identb[:], in_=identf[:]), r=[b_c], w=[b_c])
        DQ(CALL("dma_start", out=rbc[:], in_=rows_bc.ap()), w=[b_rbc])
        DQ(CALL("dma_start", out=pvec[:], in_=pvec_d.ap()), w=[b_pv])
        A(CALL("activation", out=nalog[:], in_=rbc[:, RB_ALOG:RB_ALOG + 8], func=AF.Exp), r=[b_rbc], w=[b_c])
        V(CALL("tensor_scalar_mul", out=nalog[:], in0=nalog[:], scalar1=-1.0), r=[b_c], w=[b_c])

        hT = sb("hT", [128, 16, NM], BF16)
        b_hT = [Buf("hT%d" % i) for i in range(9)]
        wbuf = [sb("wbuf%d" % i, [128, 16 * 256], BF16) for i in range(2)]
        b_wbuf = [Buf("wbuf%d" % i) for i in range(2)]
        wrr = [0]
        wab = sb("wab", [128, 16, 16], BF16); b_wab = Buf("wab")
        wpl = sb("wpl", [128, 4, 2, 256], BF16); b_wpl = Buf("wpl")
        b_mixT = {}
        halo_c = sb("halo_c", [128, 24, 3], F32); b_hc = Buf("halo_c")
        halo_u = sb("halo_u", [128, 8, 15], F32); b_hu = Buf("halo_u")
        gb = sb("gb", [128, 17, 24], F32); b_gb = Buf("gb")
        rawtail = sb("rawtail", [128, 32, 32], F32); b_rt = Buf("rawtail")
        Sst = sb("Sst", [128, 8, 128], F32); b_S = Buf("S")
        Sbf = sb("Sbf", [128, 8, 128], BF16); b_Sb = Buf("Sb")
        smp = sb("smp", [128, 24, 16], F32); b_smp = Buf("smp")
        xt = [sb("xt%d" % i, [128, D], F32) for i in range(2)]
        b_xt = [Buf("xt%d" % i) for i in range(2)]
        xb = sb("xb", [128, D], BF16); b_xb = Buf("xb")
        junk = xb; b_junk = b_xb
        st1 = sb("st1", [128, 8], F32); b_st1 = Buf("st1")

        arena1 = sb("arena1", [128, 9 * 2048 + 640], F32)
        arena1b = arena1.bitcast(BF16)
        arena2 = sb("arena2", [128, 2 * 8 * NM], BF16)
        b_a1 = Buf("arena1_phase13")
        b_a2 = Buf("arena2_phase13")

        def rstd_from_ss(ss_ap, n, inv_d, rd, wr):
            V(CALL("tensor_scalar", out=ss_ap, in0=ss_ap, scalar1=inv_d, scalar2=EPS, op0=ALU.mult, op1=ALU.add),
              r=rd, w=wr)
            A(CALL("activation", out=ss_ap, in_=ss_ap, func=AF.Ln), r=wr, w=wr)
            A(CALL("activation", out=ss_ap, in_=ss_ap, func=AF.Exp, scale=-0.5), r=wr, w=wr)

        NT_SETS = []

        def norm_transpose(src_ap, src_bufs, rows, col0, tile_idx, pv_off, si=0):
            xb_, bxb, stc_, bst = NT_SETS[si]
            A(CALL("activation", out=xb_[0:rows, :], in_=src_ap, func=AF.Square, accum_out=stc_[0:rows, 0:1]),
              r=src_bufs, w=[bxb, bst])
            yield
            rstd_from_ss(stc_[0:rows, 0:1], rows, 1.0 / D, [bst], [bst])
            yield
            V(CALL("tensor_scalar_mul", out=xb_[0:rows, :], in0=src_ap, scalar1=stc_[0:rows, 0:1]),
              r=list(src_bufs) + [bst], w=[bxb])
            yield
            for half in range(2):
                pb, pbb, bb = nb()
                for k in range(8):
                    kc = half * 8 + k
                    T(CALL("transpose", out=pbb[:, k * 128:k * 128 + rows], in_=xb_[0:rows, kc * 128:(kc + 1) * 128],
                           identity=identb[0:rows, 0:rows]), r=[bxb, b_c], w=[bb])
                yield
                src = pbb[:, 0:1024].rearrange("p (k t) -> p k t", k=8)[:, :, 0:rows]
                wv = pvec[:, pv_off + half * 8:pv_off + half * 8 + 8].unsqueeze(2).to_broadcast([128, 8, rows])
                V(CALL("tensor_tensor", out=hT[:, half * 8:half * 8 + 8, col0:col0 + rows], in0=src, in1=wv, op=ALU.mult),
                  r=[bb, b_pv], w=[b_hT[tile_idx]])
                yield

        def interleave(gens, lag=0):
            gens = list(gens)
            active = []
            step = 0
            while gens or active:
                if gens and (lag == 0 or step % max(lag, 1) == 0 or not active):
                    if lag == 0:
                        active.extend(gens)
                        gens = []
                    else:
                        active.append(gens.pop(0))
                for g_ in list(active):
                    try:
                        next(g_)
                    except StopIteration:
                        active.remove(g_)
                step += 1

        def load_w(ap_fn):
            i = wrr[0]
            wrr[0] = (i + 1) % 2
            GQ(CALL("dma_start", out=ap_fn(wbuf[i]), in_=ap_fn.src), w=[b_wbuf[i]])
            return wbuf[i], b_wbuf[i]

        def w_in_unit(c0):
            def f(t):
                return t[:, :].rearrange("p (k n) -> p k n", k=16)
            f.src = w_in.ap()[:, c0:c0 + 256].rearrange("(k p) n -> p k n", p=128)
            return f

        a1off = [0]

        def a1f(n):
            o = a1off[0]
            a1off[0] += n
            return arena1[:, o:o + n]

        def a1b(n):
            o = a1off[0]
            a1off[0] += (n + 1) // 2
            return arena1b[:, 2 * o:2 * o + n]

        raw = [a1f(528) for _ in range(2)]; b_raw = [Buf("raw0"), Buf("raw1")]
        acc = a1f(512); b_acc = Buf("acc")
        sl = acc; b_sl = b_acc
        sqb = None; b_sqb = Buf("sqb")
        rs = a1f(512); b_rs = Buf("rs")
        fmo = [a1b(512) for _ in range(2)]; b_fmo = [Buf("fmo0"), Buf("fmo1")]
        tmo = fmo; b_tmo = b_fmo
        _g = a1b(512); _bg = Buf("gto"); gto = [_g, _g]; b_gto = [_bg, _bg]
        uA = a1f(544); uB = a1f(544); b_uA = Buf("uA"); b_uB = Buf("uB")
        uraw = [a1f(544) for _ in range(2)]; b_uraw = [Buf("uraw0"), Buf("uraw1")]
        poolT = arena2[:, 0:8 * NM].rearrange("p (c t) -> p c t", c=8); b_poolT = [Buf("poolT%d" % i) for i in range(8)]
        rr = {"raw": 0, "fmo": 0, "tmo": 0, "gto": 0, "uraw": 0}

        def nxt(k):
            i = rr[k]
            rr[k] = 1 - i
            return i

        b_sc_q = {}
        b_sc_kT = {}
        b_sc_k = {}
        b_sc_v = {}
        b_sc_g = [Buf("scg%d" % i) for i in range(9)]

        def scb(d, key):
            if key not in d:
                d[key] = Buf("sc")
            return d[key]

        a2o = [8 * NM]

        def a2b(n):
            o = a2o[0]
            a2o[0] += n
            return arena2[:, o:o + n]

        def a2f_(n):
            o = a2o[0]
            a2o[0] += 2 * n
            return arena2[:, o:o + 2 * n].bitcast(F32)

        hk = sb("hk", [128, 2, 16], F32)
        SETS = [dict(raw=raw[0], b_raw=b_raw[0], acc=acc, b_acc=b_acc, sqb=sqb, b_sqb=b_sqb, rs=rs, b_rs=b_rs,
                     fmo=fmo[0], b_fmo=b_fmo[0], tmo=tmo[0], b_tmo=b_tmo[0], hk=hk[:, 0, :], b_hk=Buf("hk0"),
                     uraw=uraw[0], b_uraw=b_uraw[0], uA=uA, b_uA=b_uA, uB=uB, b_uB=b_uB),
                dict(raw=raw[1], b_raw=b_raw[1], acc=a2f_(512), b_acc=Buf("acc2"), sqb=a2b(512), b_sqb=Buf("sqb2"),
                     rs=a2f_(512), b_rs=Buf("rs2"), fmo=fmo[1], b_fmo=b_fmo[1], tmo=tmo[1], b_tmo=b_tmo[1],
                     hk=hk[:, 1, :], b_hk=Buf("hk1"), uraw=uraw[1], b_uraw=b_uraw[1], uA=a2f_(544), b_uA=Buf("uA2"),
                     uB=a2f_(544), b_uB=Buf("uB2"))]
        xb2 = a2b(2048); b_xb2 = Buf("xb2")
        assert a2o[0] <= 16 * NM, a2o[0]
        st1b = sb("st1b", [128, 8], F32); b_st1b = Buf("st1b")
        NT_SETS.append((xb, b_xb, st1, b_st1))
        NT_SETS.append((xb2, b_xb2, st1b, b_st1b))
        A2HI = [SETS[1]["b_acc"], SETS[1]["b_sqb"], SETS[1]["b_rs"], SETS[1]["b_uA"], SETS[1]["b_uB"], b_xb2]

        def qkv_chunk(wt, bw, j, ch, blocks, first_zero_halo, si=0, halo_mode=None, D=0, pre=None):
            if pre is not None:
                wt, bw = pre()
            Z = SETS[si]
            rw, brw = Z["raw"], Z["b_raw"]
            ac, bac = Z["acc"], Z["b_acc"]
            sq_, bsq = Z["sqb"], Z["b_sqb"]
            rs_, brs = Z["rs"], Z["b_rs"]
            fo, bfo = Z["fmo"], Z["b_fmo"]
            to, bto = Z["tmo"], Z["b_tmo"]
            hk_, bhk = Z["hk"], Z["b_hk"]
            kind = ch // 8
            h = ch % 8
            first = True
            for (c0, n, g0, bk) in blocks:
                pb, pbb, bb = nb()
                tiles = list(range(c0 // 128, (c0 + n - 1) // 128 + 1))
                for kc in range(16):
                    T(CALL("matmul", out=pb[:, 0:n], lhsT=wt[:, kc * 256 + j * 128:kc * 256 + (j + 1) * 128],
                           rhs=hT[:, kc, c0:c0 + n], start=(kc == 0), stop=(kc == 15)),
                      r=[bw] + [b_hT[t] for t in tiles], w=[bb])
                yield
                for _ in range(D):
                    yield
                if bk == "sample":
                    A(CALL("activation", out=rawtail[:, ch, 16:32], in_=pb[:, 0:16], func=AF.Copy), r=[bb], w=[b_rt])
                    yield
                    continue
                if halo_mode == "hk":
                    V(CALL("tensor_copy", out=rw[:, 0:3], in_=hk_[:, 0:3]), r=[bhk], w=[brw])
                elif first and first_zero_halo:
                    V(CALL("memset", rw[:, 0:3], 0.0), w=[brw])
                elif first:
                    V(CALL("tensor_copy", out=rw[:, 0:3], in_=halo_c[:, ch, :]), r=[b_hc], w=[brw])
                else:
                    V(CALL("tensor_copy", out=rw[:, 0:3], in_=hk_[:, 0:3]), r=[bhk], w=[brw])
                first = False
                A(CALL("activation", out=rw[:, 3:3 + n], in_=pb[:, 0:n], func=AF.Copy), r=[bb], w=[brw])
                yield
                V(CALL("tensor_copy", out=hk_[:, 0:3], in_=rw[:, n:n + 3]), r=[brw], w=[bhk])
                if bk in ("halo", "lastpre"):
                    V(CALL("tensor_copy", out=halo_c[:, ch, :], in_=rw[:, n:n + 3]), r=[brw], w=[b_hc])
                if bk == "halo":
                    yield
                    continue
                if bk == "lastmain":
                    V(CALL("tensor_copy", out=rawtail[:, ch, 0:16], in_=rw[:, 3 + n - 16:3 + n]), r=[brw], w=[b_rt])
                cw = lambda jj: pvec[:, PV_CONV + ch * 4 + jj:PV_CONV + ch * 4 + jj + 1]
                V(CALL("tensor_scalar_mul", out=ac[:, 0:n], in0=rw[:, 0:n], scalar1=cw(0)), r=[brw, b_pv], w=[bac])
                yield
                for jj in range(1, 4):
                    V(CALL("scalar_tensor_tensor", out=ac[:, 0:n], in0=rw[:, jj:jj + n], scalar=cw(jj), in1=ac[:, 0:n],
                           op0=ALU.mult, op1=ALU.add), r=[brw, b_pv, bac], w=[bac])
                    yield
                A(CALL("activation", out=rs_[:, 0:n], in_=ac[:, 0:n], func=AF.Exp, scale=-1.0), r=[bac], w=[brs])
                yield
                A(CALL("activation", out=rs_[:, 0:n], in_=rs_[:, 0:n], func=AF.Ln, bias=1.0), r=[brs], w=[brs])
                yield
                A(CALL("activation", out=rs_[:, 0:n], in_=rs_[:, 0:n], func=AF.Exp, scale=-1.0), r=[brs], w=[brs])
                yield
                V(CALL("tensor_tensor", out=fo[:, 0:n], in0=ac[:, 0:n], in1=rs_[:, 0:n], op=ALU.mult), r=[bac, brs], w=[bfo])
                yield
                if kind == 0:
                    DQ(CALL("dma_start", out=sc_qT.ap()[h, :, g0 - TP:g0 - TP + n], in_=fo[:, 0:n]), r=[bfo], w=[scb(b_sc_q, (h, g0))])
                elif kind == 1:
                    DQ(CALL("dma_start", out=sc_kT.ap()[h, :, g0:g0 + n], in_=fo[:, 0:n]), r=[bfo], w=[scb(b_sc_kT, (h, g0))])
                else:
                    DQ(CALL("dma_start", out=sc_vT.ap()[h, :, g0:g0 + n], in_=fo[:, 0:n]), r=[bfo], w=[scb(b_sc_v, (h, g0))])

        def interleave(gens, lag=0):
            gens = list(gens)
            active = []
            step = 0
            while gens or active:
                if gens and (lag == 0 or step % max(lag, 1) == 0 or not active):
                    if lag == 0:
                        active.extend(gens)
                        gens = []
                    else:
                        active.append(gens.pop(0))
                for g_ in list(active):
                    try:
                        next(g_)
                    except StopIteration:
                        active.remove(g_)
                step += 1

        def qkv_stream(units, blocks, first_zero_halo, lag=4, D=5):
            wl = {}

            def mk_pre(ui):
                def pre():
                    if ui not in wl:
                        k_, u_ = units[ui]
                        wl[ui] = load_w(w_in_unit(k_ * 1024 + u_ * 256))
                    if ui + 1 < len(units) and (ui + 1) not in wl:
                        k_, u_ = units[ui + 1]
                        wl[ui + 1] = load_w(w_in_unit(k_ * 1024 + u_ * 256))
                    return wl[ui]
                return pre

            chains = []
            for ui, (kind, u) in enumerate(units):
                for bi, blk in enumerate(blocks):
                    for j in range(2):
                        hm = "hk" if (bi > 0 and blk[3] != "sample") else None
                        chains.append(qkv_chunk(None, None, j, kind * 8 + u * 2 + j, [blk], first_zero_halo, j, hm, D, mk_pre(ui)))
            interleave(chains, lag=lag)

        def ab_tiles(tile_list):
            for (ct, slot, rows) in tile_list:
                pb, pbb, bb = nb()
                for kc in range(16):
                    T(CALL("matmul",
                        out=pb[0:rows, 0:16], lhsT=hT[:, kc, ct * 128:ct * 128 + rows], rhs=wab[:, kc, :],
                        start=(kc == 0), stop=(kc == 15)), r=[b_wab, b_hT[ct]], w=[bb])
                g_ = gb[0:rows, slot, 0:8]
                lb = gb[0:rows, slot, 8:16]
                be = gb[0:rows, slot, 16:24]
                V(CALL("tensor_tensor", out=g_, in0=pb[0:rows, 0:8],
                                                                     in1=rbc[0:rows, RB_DT:RB_DT + 8], op=ALU.add),
                  r=[bb, b_rbc], w=[b_gb])
                A(CALL("activation", out=g_, in_=g_, func=AF.Exp), r=[b_gb], w=[b_gb])
                A(CALL("activation", out=g_, in_=g_, func=AF.Ln, bias=1.0), r=[b_gb], w=[b_gb])
                V(CALL("tensor_tensor", out=g_, in0=g_, in1=nalog[0:rows, :], op=ALU.mult),
                  r=[b_gb, b_c], w=[b_gb])
                A(CALL("activation", out=lb, in_=pb[0:rows, 8:16], func=AF.Exp, scale=-1.0),
                  r=[bb], w=[b_gb])
                A(CALL("activation", out=lb, in_=lb, func=AF.Ln, bias=1.0), r=[b_gb], w=[b_gb])
                V(CALL("tensor_scalar_mul", out=lb, in0=lb, scalar1=-1.0), r=[b_gb], w=[b_gb])
                A(CALL("activation", out=be, in_=lb, func=AF.Exp), r=[b_gb], w=[b_gb])

        def g3(n=1024):
            return a1b(n).rearrange("p (h i) -> p h i", h=8)

        _alt = lambda w0: arena1b[:, 2 * w0:2 * w0 + 1024].rearrange("p (h i) -> p h i", h=8)
        kTc = [g3(), _alt(0)]; b_kTc = [Buf("kTc0"), Buf("kTc1")]
        qTc = [g3(), _alt(1024)]; b_qTc = [Buf("qTc0"), Buf("qTc1")]
        _v = g3(); _b = Buf("ktm"); ktm = [_v, _v]; b_ktm = [_b, _b]
        _v = g3(); _b = Buf("vtm"); vtm = [_v, _v]; b_vtm = [_b, _b]
        GU = a1f(1024).rearrange("p (h i) -> p h i", h=8); b_GU = Buf("GU")
        vTc_l = [g3(), _alt(512)]; b_vTc_l = [Buf("vTc0"), Buf("vTc1")]
        sqs_ = g3(); b_sqs_ = Buf("sqs_")
        nsc = sb("nsc", [128, 64], F32); b_nsc = Buf("nsc")
        bD = g3(); b_bD = Buf("bD")
        DT_ = g3(); b_DT = Buf("DT")
        Lm = [g3() for _ in range(2)]; b_Lm = [Buf("L0"), Buf("L1")]
        LTm = [g3() for _ in range(2)]; b_LTm = [Buf("LT0"), Buf("LT1")]
        Tt = [g3() for _ in range(2)]; b_Tt = [Buf("Tt0"), Buf("Tt1")]
        intraT = g3(); b_intra = Buf("intraT")
        vbt = g3(); b_vb = Buf("vb")
        kbg = g3(); b_kbg = Buf("kbg")
        kdt = g3(); b_kd = Buf("kd")
        nwT = g3(); b_nwT = Buf("nwT")
        vnew = g3(); b_vnew = Buf("vnew")
        otm = a1f(1024).rearrange("p (h i) -> p h i", h=8); b_otm = Buf("otm")
        oab = g3(); b_oab = Buf("oab")
        otm2 = oab; b_otm2 = b_oab
        oaT = g3(); b_oaT = Buf("oaT")
        gate_l = [g3(), _alt(1536)]; b_gate_l = [Buf("gate0"), Buf("gate1")]
        gcs = a1f(48); b_gcs = Buf("gcs")
        assert a1off[0] <= 9 * 2048 + 640, a1off[0]
        HS = 2
        NSTR = 8 // HS

        def H2(name):
            return [Buf(name + "_%d" % i) for i in range(NSTR)]
        def _altw(w0, n):
            return arena1b[:, 2 * w0:2 * w0 + n]
        _a3 = lambda w0: _altw(w0, 1024).rearrange("p (h i) -> p h i", h=8)
        Tt3 = [Tt[0], Tt[1], _a3(2080)]
        hb_Tt3 = [H2("Tt0"), H2("Tt1"), H2("Tt2")]
        CTX = [dict(vbt=vbt, kdt=kdt, nwT=nwT, intraT=intraT, gcs=gcs, nsc=nsc, hb_vb=H2("vb"), hb_kd=H2("kd"), hb_nwT=H2("nwT"),
                    hb_intra=H2("intra"), b_gcs=Buf("gcs0"), b_nsc=Buf("nsc0")),
               dict(vbt=_a3(2592), kdt=_a3(3104), nwT=_a3(3616), intraT=_a3(4128), gcs=arena1[:, 4640:4688], nsc=arena1[:, 4704:4768],
                    hb_vb=H2("vb1"), hb_kd=H2("kd1"), hb_nwT=H2("nwT1"), hb_intra=H2("intra1"), b_gcs=Buf("gcs1"), b_nsc=Buf("nsc1"))]
        CTX1_BUFS = (hb_Tt3[2] + CTX[1]["hb_vb"] + CTX[1]["hb_kd"] + CTX[1]["hb_nwT"] + CTX[1]["hb_intra"]
                     + [CTX[1]["b_gcs"], CTX[1]["b_nsc"]])
        CTX0_BUFS = (hb_Tt3[0] + hb_Tt3[1] + CTX[0]["hb_vb"] + CTX[0]["hb_kd"] + CTX[0]["hb_nwT"] + CTX[0]["hb_intra"]
                     + [CTX[0]["b_gcs"], CTX[0]["b_nsc"]])
        ALT_BUFS = [b_kTc[1], b_qTc[1], b_vTc_l[1], b_gate_l[1]] + list(CTX1_BUFS)
        INPROJ_LOW = [b_raw[0], b_raw[1], b_acc, b_rs, b_fmo[0], b_fmo[1], b_gto[0], b_uA, b_uB, b_uraw[0], b_uraw[1]]

        hb_bD = H2("bD"); hb_DT = H2("DT"); hb_L = [H2("L0"), H2("L1")]; hb_LT = [H2("LT0"), H2("LT1")]
        hb_Tt = [H2("Tt0"), H2("Tt1")]; hb_intra = H2("intra"); hb_vb = H2("vb"); hb_kbg = H2("kbg"); hb_kd = H2("kd")
        hb_nwT = H2("nwT"); hb_vnew = H2("vnew"); hb_otm = H2("otm"); hb_oab = H2("oab"); hb_oaT = H2("oaT")
        hb_S = H2("S"); hb_Sb = H2("Sb"); hb_st = H2("st1h"); hb_km = H2("km"); hb_vm = H2("vm")
        st1h = sb("st1h", [128, 8], F32)
        GDN_BUFS = (hb_bD + hb_DT + hb_L[0] + hb_L[1] + hb_LT[0] + hb_LT[1] + hb_Tt[0] + hb_Tt[1] + hb_intra + hb_vb + hb_kbg
                    + hb_kd + hb_nwT + hb_vnew + hb_otm + hb_oab + hb_oaT)

        def interleave(gens, lag=0):
            gens = list(gens)
            active = []
            step = 0
            while gens or active:
                if gens and (lag == 0 or step % max(lag, 1) == 0 or not active):
                    if lag == 0:
                        active.extend(gens)
                        gens = []
                    else:
                        active.append(gens.pop(0))
                for g_ in list(active):
                    try:
                        next(g_)
                    except StopIteration:
                        active.remove(g_)
                step += 1

        def gdn_pre(c, main, hh, kt_, bkt, km_, bkm, vm_, bvm, qt_, bqt, gv, lbv, bev, vTc, b_vTc, gate_t, b_gate):
            X = CTX[c % 2]
            vbt, kdt, nwT, intraT, gcs, nsc = X["vbt"], X["kdt"], X["nwT"], X["intraT"], X["gcs"], X["nsc"]
            hb_vb, hb_kd, hb_nwT, hb_intra, b_gcs, b_nsc = X["hb_vb"], X["hb_kd"], X["hb_nwT"], X["hb_intra"], X["b_gcs"], X["b_nsc"]
            Tt = [Tt3[c % 3], Tt3[(c + 1) % 3]]
            hb_Tt = [hb_Tt3[c % 3], hb_Tt3[(c + 1) % 3]]
            hs = slice(hh * HS, hh * HS + HS)
            v4 = lambda pp: pp[:, 0:HS * 128].rearrange("p (h i) -> p h i", h=HS)
            bc4 = lambda ap: ap.unsqueeze(2).to_broadcast([128, HS, 128])
            pT_, pTb, bbT = nb()
            for h4 in range(HS):
                h = hh * HS + h4
                T(CALL("transpose", out=pTb[:, h4 * 128:(h4 + 1) * 128], in_=kt_[:, h, :], identity=identb[:]), r=[bkt, b_c], w=[bbT])
                T(CALL("transpose", out=pTb[:, 512 + h4 * 128:512 + (h4 + 1) * 128], in_=vTc[:, h, :], identity=identb[:]),
                  r=[b_vTc, b_c], w=[bbT])
            yield
            A(CALL("activation", out=km_[:, hs, :], in_=pTb[:, 0:HS * 128].rearrange("p (h i) -> p h i", h=HS), func=AF.Copy), r=[bbT], w=[hb_km[hh]])
            V(CALL("tensor_copy", out=vm_[:, hs, :], in_=pTb[:, 512:512 + HS * 128].rearrange("p (h i) -> p h i", h=HS)), r=[bbT], w=[hb_vm[hh]])
            yield
            pD, _, bbD = nb()
            for h4 in range(HS):
                h = hh * HS + h4
                o_ = pD[:, h4 * 128:(h4 + 1) * 128]
                T(CALL("matmul", out=o_, lhsT=GU[:, h, :], rhs=lstr[:], start=True, stop=False), r=[b_GU, b_c], w=[bbD])
                T(CALL("matmul", out=o_, lhsT=identf[:], rhs=negs[:], start=False, stop=True), r=[b_c], w=[bbD])
            yield
            pK, _, bbK = nb()
            for h4 in range(HS):
                h = hh * HS + h4
                T(CALL("matmul", out=pK[:, h4 * 128:(h4 + 1) * 128], lhsT=kt_[:, h, :], rhs=kt_[:, h, :], start=True, stop=True),
                  r=[bkt], w=[bbK])
            yield
            for h4 in range(HS):
                h = hh * HS + h4
                A(CALL("activation", out=bD[:, h, :], in_=pD[:, h4 * 128:(h4 + 1) * 128], func=AF.Exp, bias=nsc[:, 16 + h:17 + h]),
                  r=[bbD, b_nsc], w=[hb_bD[hh]])
            yield
            V(CALL("tensor_tensor", out=Lm[0][:, hs, :], in0=v4(pK), in1=bD[:, hs, :], op=ALU.mult), r=[bbK, hb_bD[hh]], w=[hb_L[0][hh]])
            yield
            pb, pbb, bb = nb()
            for h4 in range(HS):
                h = hh * HS + h4
                T(CALL("transpose", out=pbb[:, h4 * 128:(h4 + 1) * 128], in_=Lm[0][:, h, :], identity=identb[:]),
                  r=[hb_L[0][hh], b_c], w=[bb])
            yield
            pv3 = pbb[:, 0:HS * 128].rearrange("p (h i) -> p h i", h=HS)
            A(CALL("activation", out=LTm[0][:, hs, :], in_=pv3, func=AF.Copy), r=[bb], w=[hb_LT[0][hh]])
            V(CALL("scalar_tensor_tensor", out=Tt[0][:, hs, :], in0=pv3, scalar=-1.0,
                   in1=identb[:].unsqueeze(1).to_broadcast([128, HS, 128]), op0=ALU.mult, op1=ALU.add), r=[bb, b_c], w=[hb_Tt[0][hh]])
            yield
            ci = 0
            for lev in range(6):
                ni = 1 - ci
                last = (lev == 5)
                p0, _, bb0 = nb()
                for h4 in range(HS):
                    h = hh * HS + h4
                    T(CALL("matmul", out=p0[:, h4 * 128:(h4 + 1) * 128], lhsT=LTm[ci][:, h, :], rhs=Lm[ci][:, h, :], start=True, stop=True),
                      r=[hb_L[ci][hh], hb_LT[ci][hh]], w=[bb0])
                yield
                if not last:
                    q0, _, qb0 = nb()
                    for h4 in range(HS):
                        h = hh * HS + h4
                        T(CALL("matmul", out=q0[:, h4 * 128:(h4 + 1) * 128], lhsT=Lm[ci][:, h, :], rhs=LTm[ci][:, h, :], start=True, stop=True),
                          r=[hb_L[ci][hh], hb_LT[ci][hh]], w=[qb0])
                    yield
                V(CALL("tensor_copy", out=Lm[ni][:, hs, :], in_=v4(p0)), r=[bb0], w=[hb_L[ni][hh]])
                yield
                if not last:
                    A(CALL("activation", out=LTm[ni][:, hs, :], in_=v4(q0), func=AF.Copy), r=[qb0], w=[hb_LT[ni][hh]])
                    yield
                r0, _, rb0 = nb()
                for h4 in range(HS):
                    h = hh * HS + h4
                    o_ = r0[:, h4 * 128:(h4 + 1) * 128]
                    T(CALL("matmul", out=o_, lhsT=Lm[ni][:, h, :], rhs=Tt[ci][:, h, :], start=True, stop=False),
                      r=[hb_L[ni][hh], hb_Tt[ci][hh]], w=[rb0])
                    T(CALL("matmul", out=o_, lhsT=identb[:], rhs=Tt[ci][:, h, :], start=False, stop=True), r=[b_c, hb_Tt[ci][hh]], w=[rb0])
                yield
                if lev % 2 == 0:
                    A(CALL("activation", out=Tt[ni][:, hs, :], in_=v4(r0), func=AF.Copy), r=[rb0], w=[hb_Tt[ni][hh]])
                else:
                    V(CALL("tensor_copy", out=Tt[ni][:, hs, :], in_=v4(r0)), r=[rb0], w=[hb_Tt[ni][hh]])
                yield
                ci = ni
            TT, bTT = Tt[ci], hb_Tt[ci][hh]
            P(CALL("tensor_tensor", out=vbt[:, hs, :], in0=vm_[:, hs, :], in1=bc4(nsc[:, 32 + hh * HS:32 + hh * HS + HS]), op=ALU.mult),
              r=[hb_vm[hh], b_nsc], w=[hb_vb[hh]])
            yield
            P(CALL("tensor_tensor", out=kbg[:, hs, :], in0=km_[:, hs, :], in1=bc4(gcs[:, 24 + hh * HS:24 + hh * HS + HS]), op=ALU.mult),
              r=[hb_km[hh], b_gcs], w=[hb_kbg[hh]])
            yield
            P(CALL("tensor_tensor", out=kdt[:, hs, :], in0=km_[:, hs, :], in1=bc4(gcs[:, 32 + hh * HS:32 + hh * HS + HS]), op=ALU.mult),
              r=[hb_km[hh], b_gcs], w=[hb_kd[hh]])
            yield
            p0, _, bb0 = nb()
            for h4 in range(HS):
                h = hh * HS + h4
                T(CALL("matmul", out=p0[:, h4 * 128:(h4 + 1) * 128], lhsT=kbg[:, h, :], rhs=TT[:, h, :], start=True, stop=True),
                  r=[hb_kbg[hh], bTT], w=[bb0])
            yield
            A(CALL("activation", out=nwT[:, hs, :], in_=v4(p0), func=AF.Copy, scale=-1.0), r=[bb0], w=[hb_nwT[hh]])
            yield
            if main:
                p0, _, bb0 = nb()
                for h4 in range(HS):
                    h = hh * HS + h4
                    o_ = p0[:, h4 * 128:(h4 + 1) * 128]
                    T(CALL("matmul", out=o_, lhsT=lstr[:], rhs=GU[:, h, :], start=True, stop=False), r=[b_GU, b_c], w=[bb0])
                    T(CALL("matmul", out=o_, lhsT=identf[:], rhs=negi[:], start=False, stop=True), r=[b_c], w=[bb0])
                yield
                A(CALL("activation", out=DT_[:, hs, :], in_=v4(p0), func=AF.Exp), r=[bb0], w=[hb_DT[hh]])
                yield
                p1, _, bb1 = nb()
                for h4 in range(HS):
                    h = hh * HS + h4
                    T(CALL("matmul", out=p1[:, h4 * 128:(h4 + 1) * 128], lhsT=kt_[:, h, :], rhs=qt_[:, h, :], start=True, stop=True),
                      r=[bkt, bqt], w=[bb1])
                yield
                V(CALL("tensor_tensor", out=intraT[:, hs, :], in0=v4(p1), in1=DT_[:, hs, :], op=ALU.mult), r=[bb1, hb_DT[hh]], w=[hb_intra[hh]])
                yield

        def gdn_seq(c, main, hh, kt_, bkt, km_, bkm, vm_, bvm, qt_, bqt, gv, lbv, bev, vTc, b_vTc, gate_t, b_gate):
            X = CTX[c % 2]
            vbt, kdt, nwT, intraT, gcs, nsc = X["vbt"], X["kdt"], X["nwT"], X["intraT"], X["gcs"], X["nsc"]
            hb_vb, hb_kd, hb_nwT, hb_intra, b_gcs, b_nsc = X["hb_vb"], X["hb_kd"], X["hb_nwT"], X["hb_intra"], X["b_gcs"], X["b_nsc"]
            Tt = [Tt3[c % 3], Tt3[(c + 1) % 3]]
            hb_Tt = [hb_Tt3[c % 3], hb_Tt3[(c + 1) % 3]]
            hs = slice(hh * HS, hh * HS + HS)
            v4 = lambda pp: pp[:, 0:HS * 128].rearrange("p (h i) -> p h i", h=HS)
            bc4 = lambda ap: ap.unsqueeze(2).to_broadcast([128, HS, 128])
            TT, bTT = Tt[0], hb_Tt[0][hh]
            p0, _, bb0 = nb()
            for h4 in range(HS):
                h = hh * HS + h4
                o_ = p0[:, h4 * 128:(h4 + 1) * 128]
                T(CALL("matmul", out=o_, lhsT=TT[:, h, :], rhs=vbt[:, h, :], start=True, stop=False), r=[bTT, hb_vb[hh]], w=[bb0])
                T(CALL("matmul", out=o_, lhsT=nwT[:, h, :], rhs=Sbf[:, h, :], start=False, stop=True), r=[hb_nwT[hh], hb_Sb[hh]], w=[bb0])
            yield
            A(CALL("activation", out=vnew[:, hs, :], in_=v4(p0), func=AF.Copy), r=[bb0], w=[hb_vnew[hh]])
            yield
            if main:
                p0, _, bb0 = nb()
                for h4 in range(HS):
                    h = hh * HS + h4
                    T(CALL("matmul", out=p0[:, h4 * 128:(h4 + 1) * 128], lhsT=qt_[:, h, :], rhs=Sbf[:, h, :], start=True, stop=True),
                      r=[bqt, hb_Sb[hh]], w=[bb0])
                yield
                V(CALL("tensor_tensor", out=otm[:, hs, :], in0=v4(p0), in1=bc4(gcs[:, 16 + hh * HS:16 + hh * HS + HS]), op=ALU.mult),
                  r=[bb0, b_gcs], w=[hb_otm[hh]])
                yield
                p1, _, bb1 = nb()
                for h4 in range(HS):
                    h = hh * HS + h4
                    T(CALL("matmul", out=p1[:, h4 * 128:(h4 + 1) * 128], lhsT=intraT[:, h, :], rhs=vnew[:, h, :], start=True, stop=True),
                      r=[hb_intra[hh], hb_vnew[hh]], w=[bb1])
                yield
                V(CALL("tensor_tensor", out=otm[:, hs, :], in0=v4(p1), in1=otm[:, hs, :], op=ALU.add), r=[bb1, hb_otm[hh]], w=[hb_otm[hh]])
                yield
            p2, _, bb2 = nb()
            for h4 in range(HS):
                h = hh * HS + h4
                T(CALL("matmul", out=p2[:, h4 * 128:(h4 + 1) * 128], lhsT=kdt[:, h, :], rhs=vnew[:, h, :], start=True, stop=True),
                  r=[hb_kd[hh], hb_vnew[hh]], w=[bb2])
            yield
            P(CALL("tensor_tensor", out=Sst[:, hs, :], in0=Sst[:, hs, :], in1=bc4(gcs[:, 40 + hh * HS:40 + hh * HS + HS]), op=ALU.mult),
              r=[hb_S[hh], b_gcs], w=[hb_S[hh]])
            yield
            V(CALL("tensor_tensor", out=Sst[:, hs, :], in0=v4(p2), in1=Sst[:, hs, :], op=ALU.add), r=[bb2, hb_S[hh]], w=[hb_S[hh]])
            yield
            A(CALL("activation", out=Sbf[:, hs, :], in_=Sst[:, hs, :], func=AF.Copy), r=[hb_S[hh]], w=[hb_Sb[hh]])
            yield
            if main:
                for h4 in range(HS):
                    h = hh * HS + h4
                    A(CALL("activation", out=oab[:, h, :], in_=otm[:, h, :], func=AF.Square, accum_out=st1h[:, h:h + 1]),
                      r=[hb_otm[hh]], w=[hb_oab[hh], hb_st[hh]])
                yield
                sv = st1h[:, hh * HS:hh * HS + HS]
                V(CALL("tensor_tensor", out=sv, in0=sv, in1=nsc[:, 48 + hh * HS:48 + hh * HS + HS], op=ALU.mult), r=[hb_st[hh], b_nsc], w=[hb_st[hh]])
                rstd_from_ss(sv, 128, 1.0 / 128, [hb_st[hh]], [hb_st[hh]])
                V(CALL("tensor_tensor", out=sv, in0=sv, in1=nsc[:, 40 + hh * HS:40 + hh * HS + HS], op=ALU.mult), r=[hb_st[hh], b_nsc], w=[hb_st[hh]])
                yield
                V(CALL("tensor_tensor", out=otm[:, hs, :], in0=otm[:, hs, :], in1=bc4(sv), op=ALU.mult), r=[hb_otm[hh], hb_st[hh]], w=[hb_otm[hh]])
                yield
                V(CALL("tensor_tensor", out=otm[:, hs, :], in0=otm[:, hs, :],
                       in1=rbc[:, RB_GDNOUT:RB_GDNOUT + 128].unsqueeze(1).to_broadcast([128, HS, 128]), op=ALU.mult),
                  r=[hb_otm[hh], b_rbc], w=[hb_otm[hh]])
                yield
                V(CALL("tensor_tensor", out=oab[:, hs, :], in0=otm[:, hs, :], in1=gate_t[:, hs, :], op=ALU.mult),
                  r=[hb_otm[hh], b_gate], w=[hb_oab[hh]])
                yield
                pb, pbb, bb = nb()
                for h4 in range(HS):
                    h = hh * HS + h4
                    T(CALL("transpose", out=pbb[:, h4 * 128:(h4 + 1) * 128], in_=oab[:, h, :], identity=identb[:]), r=[hb_oab[hh], b_c], w=[bb])
                yield
                mc = (c - 8) * 128
                A(CALL("activation", out=oaT[:, hs, :], in_=pbb[:, 0:HS * 128].rearrange("p (h i) -> p h i", h=HS), func=AF.Copy),
                  r=[bb], w=[hb_oaT[hh]])
                b_mixT[(0, c - 8, hh)] = Buf("mix")
                DQ(CALL("dma_start", out=sc_mixT.ap()[hh * HS:hh * HS + HS, :, mc:mc + 128].rearrange("h e t -> e h t"), in_=oaT[:, hs, :]),
                   r=[hb_oaT[hh]], w=[b_mixT[(0, c - 8, hh)]])
                yield

        def gdn_loads(c, main, cur):
            g0 = c * 128
            blk = (g0 // 512) * 512
            DQ(CALL("dma_start", out=kTc[cur], in_=sc_kT.ap()[:, :, g0:g0 + 128].rearrange("h d t -> d h t")),
               r=[b_sc_kT[(h, blk)] for h in range(8)], w=[b_kTc[cur]])
            DQ(CALL("dma_start", out=vTc_l[cur], in_=sc_vT.ap()[:, :, g0:g0 + 128].rearrange("h d t -> d h t")),
               r=[b_sc_v[(h, blk)] for h in range(8)], w=[b_vTc_l[cur]])
            if main:
                DQ(CALL("dma_start", out=qTc[cur], in_=sc_qT.ap()[:, :, g0 - TP:g0 - TP + 128].rearrange("h d t -> d h t")),
                   r=[b_sc_q[(h, blk)] for h in range(8)], w=[b_qTc[cur]])
                ti = c - 8
                DQ(CALL("dma_start", out=gate_l[cur], in_=sc_gate.ap()[ti * 128:(ti + 1) * 128, :].rearrange("t (h d) -> t h d", h=8)),
                   r=[b_sc_g[ti]], w=[b_gate_l[cur]])

        def gdn_pre_chunk(c, main, cur):
            X = CTX[c % 2]
            vbt, kdt, nwT, intraT, gcs, nsc = X["vbt"], X["kdt"], X["nwT"], X["intraT"], X["gcs"], X["nsc"]
            hb_vb, hb_kd, hb_nwT, hb_intra, b_gcs, b_nsc = X["hb_vb"], X["hb_kd"], X["hb_nwT"], X["hb_intra"], X["b_gcs"], X["b_nsc"]
            Tt = [Tt3[c % 3], Tt3[(c + 1) % 3]]
            hb_Tt = [hb_Tt3[c % 3], hb_Tt3[(c + 1) % 3]]
            g0 = c * 128
            slot = c
            kt_, bkt = kTc[cur], b_kTc[cur]
            km_, bkm = ktm[cur], b_ktm[cur]
            vm_, bvm = vtm[cur], b_vtm[cur]
            qt_, bqt = qTc[cur], b_qTc[cur]
            vTc, b_vTc = vTc_l[cur], b_vTc_l[cur]
            gate_t, b_gate = gate_l[cur], b_gate_l[cur]
            gv = gb[:, slot, 0:8]
            lbv = gb[:, slot, 8:16]
            bev = gb[:, slot, 16:24]
            pb, _, bb = nb()
            T(CALL("matmul", out=pb[:, 0:8], lhsT=utri[:], rhs=gv, start=True, stop=True), r=[b_gb, b_c], w=[bb])
            T(CALL("matmul", out=pb[:, 8:16], lhsT=onesf[:], rhs=gv, start=True, stop=True), r=[b_gb, b_c], w=[bb])
            V(CALL("tensor_copy", out=gcs[:, 0:16], in_=pb[:, 0:16]), r=[bb], w=[b_gcs])
            yield
            A(CALL("activation", out=gcs[:, 16:24], in_=gcs[:, 0:8], func=AF.Exp), r=[b_gcs], w=[b_gcs])
            yield
            pn, _, bbn = nb()
            A(CALL("activation", out=sqs_, in_=kt_, func=AF.Square), r=[bkt], w=[b_sqs_])
            yield
            for h in range(8):
                T(CALL("matmul", out=pn[:, h:h + 1], lhsT=sqs_[:, h, :], rhs=onesb[:, 0:1], start=True, stop=True), r=[b_sqs_, b_c], w=[bbn])
            if main:
                V(CALL("tensor_tensor", out=sqs_, in0=qt_, in1=qt_, op=ALU.mult), r=[bqt], w=[b_sqs_])
                for h in range(8):
                    T(CALL("matmul", out=pn[:, 8 + h:9 + h], lhsT=sqs_[:, h, :], rhs=onesb[:, 0:1], start=True, stop=True), r=[b_sqs_, b_c], w=[bbn])
            nq = 16 if main else 8
            A(CALL("activation", out=nsc[:, 0:nq], in_=pn[:, 0:nq], func=AF.Ln, bias=epst[:, 0:1]), r=[bbn, b_c], w=[b_nsc])
            yield
            V(CALL("tensor_tensor", out=nsc[:, 16:24], in0=lbv, in1=nsc[:, 0:8], op=ALU.subtract), r=[b_gb, b_nsc], w=[b_nsc])
            yield
            A(CALL("activation", out=nsc[:, 24:32], in_=nsc[:, 16:24], func=AF.Exp), r=[b_nsc], w=[b_nsc])
            yield
            V(CALL("scalar_tensor_tensor", out=nsc[:, 56:64], in0=nsc[:, 0:8], scalar=-0.5, in1=lbv, op0=ALU.mult, op1=ALU.add),
              r=[b_nsc, b_gb], w=[b_nsc])
            yield
            A(CALL("activation", out=nsc[:, 32:40], in_=nsc[:, 56:64], func=AF.Exp), r=[b_nsc], w=[b_nsc])
            yield
            if main:
                A(CALL("activation", out=nsc[:, 40:48], in_=nsc[:, 8:16], func=AF.Exp, scale=-0.5), r=[b_nsc], w=[b_nsc])
                V(CALL("tensor_scalar_mul", out=nsc[:, 40:48], in0=nsc[:, 40:48], scalar1=128.0 ** -0.5), r=[b_nsc], w=[b_nsc])
                V(CALL("tensor_tensor", out=nsc[:, 48:56], in0=nsc[:, 40:48], in1=nsc[:, 40:48], op=ALU.mult), r=[b_nsc], w=[b_nsc])
            V(CALL("tensor_tensor", out=gcs[:, 24:32], in0=gcs[:, 16:24], in1=nsc[:, 24:32], op=ALU.mult), r=[b_gcs, b_nsc], w=[b_gcs])
            yield
            V(CALL("tensor_tensor", out=gcs[:, 32:40], in0=gcs[:, 8:16], in1=gcs[:, 0:8], op=ALU.subtract), r=[b_gcs], w=[b_gcs])
            yield
            A(CALL("activation", out=gcs[:, 32:40], in_=gcs[:, 32:40], func=AF.Exp), r=[b_gcs], w=[b_gcs])
            yield
            A(CALL("activation", out=gcs[:, 40:48], in_=gcs[:, 8:16], func=AF.Exp), r=[b_gcs], w=[b_gcs])
            yield
            V(CALL("tensor_tensor", out=GU, in0=utri[:].unsqueeze(1).to_broadcast([128, 8, 128]),
                   in1=gv.unsqueeze(2).to_broadcast([128, 8, 128]), op=ALU.mult), r=[b_c, b_gb], w=[b_GU])
            gens = [gdn_pre(c, main, hh, kt_, bkt, km_, bkm, vm_, bvm, qt_, bqt, gv, lbv, bev, vTc, b_vTc, gate_t, b_gate)
                    for hh in range(NSTR)]
            started = 1
            step = 0
            while gens:
                for g_ in list(gens[:started]):
                    try:
                        next(g_)
                    except StopIteration:
                        gens.remove(g_)
                        started -= 1
                step += 1
                if step % 2 == 0 and started < len(gens):
                    started += 1
                yield

        def gdn_seq_chunk(c, main, cur):
            kt_, bkt = kTc[cur], b_kTc[cur]
            km_, bkm = ktm[cur], b_ktm[cur]
            vm_, bvm = vtm[cur], b_vtm[cur]
            qt_, bqt = qTc[cur], b_qTc[cur]
            vTc, b_vTc = vTc_l[cur], b_vTc_l[cur]
            gate_t, b_gate = gate_l[cur], b_gate_l[cur]
            gv = gb[:, c, 0:8]
            lbv = gb[:, c, 8:16]
            bev = gb[:, c, 16:24]
            return [gdn_seq(c, main, hh, kt_, bkt, km_, bkm, vm_, bvm, qt_, bqt, gv, lbv, bev, vTc, b_vTc, gate_t, b_gate)
                    for hh in range(NSTR)]

        def gdn_run(c_lo, c_hi, main):
            gdn_loads(c_lo, main, c_lo % 2)
            interleave([gdn_pre_chunk(c_lo, main, c_lo % 2)])
            for c in range(c_lo, c_hi):
                gens = gdn_seq_chunk(c, main, c % 2)
                if c + 1 < c_hi:
                    gdn_loads(c + 1, main, (c + 1) % 2)
                    gens = gens + [gdn_pre_chunk(c + 1, main, (c + 1) % 2)]
                interleave(gens)


        def u_chunk(wt, bw, j, pc, blocks, first_zero_halo, si=0):
            Z = SETS[si]
            rw, brw = Z["uraw"], Z["b_uraw"]
            hk_, bhk = Z["hk"], Z["b_hk"]
            g = pc // 2
            win = 2 << g
            first = True
            for (c0, n, bk) in blocks:
                pb, pbb, bb = nb()
                tiles = list(range(c0 // 128, (c0 + n - 1) // 128 + 1))
                for kc in range(16):
                    T(CALL("matmul", out=pb[:, 0:n], lhsT=wt[:, kc * 256 + j * 128:kc * 256 + (j + 1) * 128],
                           rhs=hT[:, kc, c0:c0 + n], start=(kc == 0), stop=(kc == 15)),
                      r=[bw] + [b_hT[t] for t in tiles], w=[bb])
                yield
                if bk == "sample":
                    A(CALL("activation", out=rawtail[:, 24 + pc, 16:32], in_=pb[:, 0:16], func=AF.Copy), r=[bb], w=[b_rt])
                    yield
                    continue
                if bk == "halo":
                    A(CALL("activation", out=halo_u[:, pc, :], in_=pb[:, 1:16], func=AF.Copy), r=[bb], w=[b_hu])
                    yield
                    continue
                if first:
                    V(CALL("tensor_copy", out=rw[:, 0:15], in_=halo_u[:, pc, :]), r=[b_hu], w=[brw])
                else:
                    V(CALL("tensor_copy", out=rw[:, 0:15], in_=hk_[:, 0:15]), r=[bhk], w=[brw])
                first = False
                A(CALL("activation", out=rw[:, 15:15 + n], in_=pb[:, 0:n], func=AF.Copy), r=[bb], w=[brw])
                yield
                V(CALL("tensor_copy", out=hk_[:, 0:15], in_=rw[:, n:n + 15]), r=[brw], w=[bhk])
                if bk == "lastmain":
                    V(CALL("tensor_copy", out=rawtail[:, 24 + pc, 0:16], in_=rw[:, 15 + n - 16:15 + n]), r=[brw], w=[b_rt])
                L = 15 + n
                src, bsrc = rw, brw
                k = 1
                flip = 0
                while k < win:
                    dst, bdst = (Z["uA"], Z["b_uA"]) if flip == 0 else (Z["uB"], Z["b_uB"])
                    V(CALL("tensor_tensor", out=dst[:, 2 * k - 1:L], in0=src[:, 2 * k - 1:L], in1=src[:, k - 1:L - k], op=ALU.add),
                      r=[bsrc], w=[bdst])
                    yield
                    src, bsrc = dst, bdst
                    flip = 1 - flip
                    k *= 2
                V(CALL("scalar_tensor_tensor", out=poolT[:, pc, c0:c0 + n], in0=src[:, 15:15 + n], scalar=1.0 / win, in1=rw[:, 15:15 + n],
                       op0=ALU.mult, op1=ALU.subtract), r=[bsrc, brw], w=[b_poolT[pc]])
                yield
                if c0 == 0:
                    fx, bfx = Z["rs"], Z["b_rs"]
                    V(CALL("tensor_tensor", out=fx[:, 0:16], in0=src[:, 15:31], in1=rbc[:, RB_INVC + g * 16:RB_INVC + g * 16 + 16], op=ALU.mult),
                      r=[bsrc, b_rbc], w=[bfx])
                    V(CALL("tensor_tensor", out=poolT[:, pc, 0:16], in0=fx[:, 0:16], in1=rw[:, 15:31], op=ALU.subtract),
                      r=[bfx, brw], w=[b_poolT[pc]])
                    yield

        def pool_proj(g, blocks):
            for dch in range(2):
                for (c0, n) in blocks:
                    pb, pbb, bb = nb()
                    for c2 in range(2):
                        T(CALL("matmul",
                            out=pb[:, 0:n], lhsT=wpl[:, g, c2, dch * 128:(dch + 1) * 128],
                            rhs=poolT[:, 2 * g + c2, c0:c0 + n], start=(c2 == 0), stop=(c2 == 1)),
                          r=[b_wpl, b_poolT[2 * g], b_poolT[2 * g + 1]], w=[bb])
                    fi = nxt("fmo")
                    fo, bfo = fmo[fi], b_fmo[fi]
                    A(CALL("activation",
                        out=fo[:, 0:n], in_=pb[:, 0:n], func=AF.Copy,
                        scale=pvec[:, PV_PSCALE + 2 * g + dch:PV_PSCALE + 2 * g + dch + 1]), r=[bb, b_pv], w=[bfo])
                    key = (8 + 2 * g + dch, c0)
                    b_mixT[key] = Buf("mixb")
                    DQ(CALL("dma_start", out=sc_mixT.ap()[8 + 2 * g + dch, :, c0:c0 + n], in_=fo[:, 0:n]),
                       r=[bfo], w=[b_mixT[key]])

        def gate_unit(wt, bw, u4):
            for i in range(9):
                rows = 128 if i < 8 else TS
                pb, pbb, bb = nb()
                for kc in range(16):
                    T(CALL("matmul",
                        out=pb[0:rows, 0:256], lhsT=hT[:, kc, i * 128:i * 128 + rows], rhs=wt[:, kc * 256:(kc + 1) * 256],
                        start=(kc == 0), stop=(kc == 15)), r=[bw, b_hT[i]], w=[bb])
                gi = nxt("gto")
                go, bgo = gto[gi], b_gto[gi]
                A(CALL("activation", out=go[0:rows, 0:256], in_=pb[0:rows, 0:256], func=AF.Silu),
                  r=[bb], w=[bgo])
                DQ(CALL("dma_start", out=sc_gate.ap()[i * 128:i * 128 + rows, u4 * 256:(u4 + 1) * 256],
                                                              in_=go[0:rows, 0:256]), r=[bgo], w=[b_sc_g[i]])

        GQ(CALL("dma_start", out=wab[:], in_=w_in.ap()[:, 4096:4112].rearrange("(k p) n -> p k n", p=128)), w=[b_wab])
        GQ(CALL("dma_start", out=wpl[:], in_=w_pool.ap().rearrange("g (c p) d -> p g c d", p=128)), w=[b_wpl])
        V(CALL("memset", Sst[:], 0.0), w=hb_S)
        V(CALL("memset", Sbf[:], 0.0), w=hb_Sb)
        for i2 in range(4):
            gens = []
            for xi in range(2):
                i = i2 * 2 + xi
                DQ(CALL("dma_start", out=xt[xi][:], in_=xall.ap()[i * 128:(i + 1) * 128, :]), w=[b_xt[xi]])
                gens.append(norm_transpose(xt[xi][:], [b_xt[xi]], 128, i * 128, i, PV_PREMIX, xi))
            interleave(gens)
        ab_tiles([(i, i, 128) for i in range(8)])
        pre_blocks = [(0, 512, 0, "n"), (512, 512, 512, "lastpre")]
        for u in range(4):
            wt, bw = load_w(w_in_unit(u * 256))
            interleave([qkv_chunk(wt, bw, j, u * 2 + j, [(1008, 16, None, "halo")], True, j) for j in range(2)])
        qkv_stream([(kind, u) for kind in (1, 2) for u in range(4)], pre_blocks, True)
        for u in range(4):
            wt, bw = load_w(w_in_unit(4112 + u * 256))
            interleave([u_chunk(wt, bw, j, u * 2 + j, [(1008, 16, "halo")], True, j) for j in range(2)])
        inherit(ALT_BUFS, INPROJ_LOW)
        gdn_run(0, 8, False)
        inherit(INPROJ_LOW, ALT_BUFS)

        for i2 in range(5):
            gens = []
            for xi in range(2):
                i = i2 * 2 + xi
                if i > 8:
                    continue
                rows = 128 if i < 8 else TS
                DQ(CALL("dma_start", out=xt[xi][0:rows, :], in_=xall.ap()[TP + i * 128:TP + i * 128 + rows, :]), w=[b_xt[xi]])
                gens.append(norm_transpose(xt[xi][0:rows, :], [b_xt[xi]], rows, i * 128, i, PV_PREMIX, xi))
            interleave(gens)
        ab_tiles([(i, 8 + i, 128 if i < 8 else TS) for i in range(9)])
        main_blocks = [(0, 512, TP, "n"), (512, 512, TP + 512, "lastmain"), (1024, 16, None, "sample")]
        qkv_stream([(kind, u) for kind in (0, 1, 2) for u in range(4)], main_blocks, False)
        for u in range(4):
            wt, bw = load_w(w_in_unit(3072 + u * 256))
            gate_unit(wt, bw, u)
        for u in range(4):
            wt, bw = load_w(w_in_unit(4112 + u * 256))
            interleave([u_chunk(wt, bw, j, u * 2 + j, [(0, 512, "n"), (512, 512, "lastmain"), (1024, 16, "sample")], False, j)
                        for j in range(2)])
            pool_proj(u, [(0, 512), (512, 512)])

        tmT = arena2[:, 8 * NM:8 * NM + 8192].bitcast(F32) if False else None
        for half in range(2):
            for q4 in range(4):
                pb, pbb, bb = nb()
                for k in range(4):
                    chn = half * 16 + q4 * 4 + k
                    T(CALL("transpose", out=pb[0:32, k * 128:(k + 1) * 128], in_=rawtail[:, chn, :],
                                                                 identity=identf[:]), r=[b_rt, b_c], w=[bb])
                V(CALL("tensor_copy", out=xt[half][0:32, q4 * 512:(q4 + 1) * 512], in_=pb[0:32, :]),
                  r=[bb], w=[b_xt[half]])
        DQ(CALL("dma_start", out=o_conv_p.ap()[:, 0:2048], in_=xt[0][13:16, :]), r=[b_xt[0]], stream="out")
        DQ(CALL("dma_start", out=o_conv_p.ap()[:, 2048:3072], in_=xt[1][13:16, 0:1024]), r=[b_xt[1]], stream="out")
        DQ(CALL("dma_start", out=o_pool_p.ap(), in_=xt[1][1:16, 1024:2048]), r=[b_xt[1]], stream="out")
        DQ(CALL("dma_start", out=o_conv_s.ap()[:, 2, 0:2048], in_=xt[0][16:32, :]), r=[b_xt[0]], stream="out")
        DQ(CALL("dma_start", out=o_conv_s.ap()[:, 2, 2048:3072], in_=xt[1][16:32, 0:1024]), r=[b_xt[1]], stream="out")
        DQ(CALL("dma_start", out=o_pool_s.ap()[:, 14, :], in_=xt[1][16:32, 1024:2048]), r=[b_xt[1]], stream="out")
        DQ(CALL("dma_start", out=o_conv_s.ap()[:, 0:2, :], in_=st_conv.ap().rearrange("(t j) c -> t j c", j=3)[:, 1:3, :]), stream="out")
        DQ(CALL("dma_start", out=o_pool_s.ap()[:, 0:14, :], in_=st_pool.ap().rearrange("(t j) c -> t j c", j=15)[:, 1:15, :]), stream="out")

        a2f = arena2.bitcast(F32)
        stc = a2f[0:48, 0:3072]; b_stc = Buf("stc")
        inherit([b_stc], b_poolT + A2HI)
        DQ(CALL("dma_start", out=stc, in_=st_conv.ap()), w=[b_stc])
        stT = a2f[:, 5120:6272].rearrange("p (c t) -> p c t", c=24); b_stT = Buf("stT")
        inherit([b_stT], b_poolT + A2HI)
        for q6 in range(6):
            pb, pbb, bb = nb()
            for k in range(4):
                chn = q6 * 4 + k
                T(CALL("transpose", out=pb[:, k * 48:(k + 1) * 48], in_=stc[:, chn * 128:(chn + 1) * 128],
                                                             identity=identf[0:48, 0:48]), r=[b_stc, b_c], w=[bb])
            V(CALL("tensor_copy", out=stT[:, q6 * 4:q6 * 4 + 4, :], in_=pb[:, 0:192].rearrange("p (k t) -> p k t", k=4)),
              r=[bb], w=[b_stT])
        cwv = pvec[:, PV_CONV:PV_CONV + 96].rearrange("p (c j) -> p c j", j=4)
        stv = stT.rearrange("p c (t j) -> p c t j", j=3)
        sacc = sb("sacc", [128, 24, 16], F32); b_sacc = Buf("sacc")
        stmp = sb("stmp", [128, 24, 16], F32); b_stmp = Buf("stmp")
        V(CALL("tensor_tensor", out=sacc[:], in0=rawtail[:, 0:24, 16:32], in1=cwv[:, :, 3:4].to_broadcast([128, 24, 16]), op=ALU.mult),
          r=[b_rt, b_pv], w=[b_sacc])
        for jj in range(3):
            V(CALL("tensor_tensor", out=stmp[:], in0=stv[:, :, :, jj], in1=cwv[:, :, jj:jj + 1].to_broadcast([128, 24, 16]), op=ALU.mult),
              r=[b_stT, b_pv], w=[b_stmp])
            V(CALL("tensor_tensor", out=sacc[:], in0=sacc[:], in1=stmp[:], op=ALU.add), r=[b_sacc, b_stmp], w=[b_sacc])
        A(CALL("activation", out=stmp[:], in_=sacc[:], func=AF.Exp, scale=-1.0), r=[b_sacc], w=[b_stmp])
        A(CALL("activation", out=stmp[:], in_=stmp[:], func=AF.Ln, bias=1.0), r=[b_stmp], w=[b_stmp])
        A(CALL("activation", out=stmp[:], in_=stmp[:], func=AF.Exp, scale=-1.0), r=[b_stmp], w=[b_stmp])
        V(CALL("tensor_tensor", out=smp[:], in0=sacc[:], in1=stmp[:], op=ALU.mult), r=[b_sacc, b_stmp], w=[b_smp])
        sqs = sb("sqs", [128, 16, 16], BF16); b_sqs = Buf("sqs")
        V(CALL("tensor_tensor", out=sqs[:], in0=smp[:, 0:16, :], in1=smp[:, 0:16, :], op=ALU.mult), r=[b_smp], w=[b_sqs])
        pb, pbb, bb = nb()
        T(CALL("matmul", out=pb[:, 0:256], lhsT=onesb[:], rhs=sqs[:, :, :].rearrange("p c t -> p (c t)"), start=True, stop=True),
          r=[b_sqs, b_c], w=[bb])
        A(CALL("activation", out=stmp[:, 0:16, :].rearrange("p c t -> p (c t)"), in_=pb[:, 0:256], func=AF.Ln, bias=epst[:, 0:1]),
          r=[bb, b_c], w=[b_stmp])
        A(CALL("activation", out=stmp[:, 0:16, :], in_=stmp[:, 0:16, :], func=AF.Exp, scale=-0.5), r=[b_stmp], w=[b_stmp])
        V(CALL("tensor_tensor", out=smp[:, 0:16, :], in0=smp[:, 0:16, :], in1=stmp[:, 0:16, :], op=ALU.mult), r=[b_smp, b_stmp], w=[b_smp])
        V(CALL("tensor_scalar_mul", out=smp[:, 0:8, :], in0=smp[:, 0:8, :], scalar1=128.0 ** -0.5), r=[b_smp], w=[b_smp])
        spc = a2f[0:120, 3072:5120].rearrange("p (a c) -> p a c", a=2); b_spc = Buf("spc")
        inherit([b_spc], b_poolT + A2HI)
        DQ(CALL("dma_start", out=spc, in_=st_pool.ap().rearrange("(a r) c -> r a c", a=2)), w=[b_spc])
        spT = a2f[:, 6272:8192].rearrange("p (c t j) -> p c t j", c=8, j=15); b_spT = Buf("spT")
        inherit([b_spT], b_poolT + A2HI)
        for a_ in range(2):
            for q2 in range(2):
                pb, pbb, bb = nb()
                for k in range(4):
                    chn = q2 * 4 + k
                    T(CALL("transpose", out=pb[:, k * 120:(k + 1) * 120], in_=spc[:, a_, chn * 128:(chn + 1) * 128],
                                                                      identity=identf[0:120, 0:120]), r=[b_spc, b_c], w=[bb])
                V(CALL("tensor_copy",
                    out=spT[:, q2 * 4:q2 * 4 + 4, a_ * 8:a_ * 8 + 8, :],
                    in_=pb[:, 0:480].rearrange("p (k t j) -> p k t j", k=4, j=15)), r=[bb], w=[b_spT])
        spl = sb("spl", [128, 8, 16], F32); b_spl = Buf("spl")
        spb = sb("spb", [128, 8, 16], BF16); b_spb = Buf("spb")
        for g in range(4):
            win = 2 << g
            V(CALL("tensor_reduce", out=spl[:, 2 * g:2 * g + 2, :], in_=spT[:, 2 * g:2 * g + 2, :, 15 - (win - 1):15],
                                                      axis=mybir.AxisListType.X, op=ALU.add), r=[b_spT], w=[b_spl])
            V(CALL("tensor_tensor", out=spl[:, 2 * g:2 * g + 2, :], in0=spl[:, 2 * g:2 * g + 2, :],
                                             in1=rawtail[:, 24 + 2 * g:24 + 2 * g + 2, 16:32], op=ALU.add), r=[b_spl, b_rt], w=[b_spl])
            V(CALL("scalar_tensor_tensor", out=spb[:, 2 * g:2 * g + 2, :], in0=spl[:, 2 * g:2 * g + 2, :], scalar=1.0 / win,
                                                             in1=rawtail[:, 24 + 2 * g:24 + 2 * g + 2, 16:32], op0=ALU.mult, op1=ALU.subtract),
              r=[b_spl, b_rt], w=[b_spb])
        for g in range(4):
            for dch in range(2):
                pb, pbb, bb = nb()
                for c2 in range(2):
                    T(CALL("matmul", out=pb[:, 0:16], lhsT=wpl[:, g, c2, dch * 128:(dch + 1) * 128],
                                                                     rhs=spb[:, 2 * g + c2, :], start=(c2 == 0), stop=(c2 == 1)),
                      r=[b_wpl, b_spb], w=[bb])
                fi = nxt("fmo")
                fo, bfo = fmo[fi], b_fmo[fi]
                A(CALL("activation", out=fo[:, 0:16], in_=pb[:, 0:16], func=AF.Copy,
                                                                     scale=pvec[:, PV_PSCALE + 2 * g + dch:PV_PSCALE + 2 * g + dch + 1]),
                  r=[bb, b_pv], w=[bfo])
                key = (8 + 2 * g + dch, 1024)
                b_mixT[key] = Buf("mixs")
                DQ(CALL("dma_start", out=sc_mixT.ap()[8 + 2 * g + dch, :, 1024:1040], in_=fo[:, 0:16]),
                   r=[bfo], w=[b_mixT[key]])

        inherit(ALT_BUFS, INPROJ_LOW)
        gdn_run(8, 16, True)
        DQ(CALL("dma_start", out=o_gdn_p.ap().rearrange("h d e -> d h e"), in_=Sst[:]), r=hb_S, stream="out")

        A1ALL = [b_raw[0], b_raw[1], b_acc, b_sqb, b_rs, b_fmo[0], b_fmo[1], b_tmo[0], b_tmo[1], b_gto[0], b_uA, b_uB,
                 b_uraw[0], b_uraw[1], b_kTc[0], b_qTc[0], b_ktm[0], b_vtm[0], b_GU, b_bD, b_DT, b_Lm[0], b_Lm[1],
                 b_LTm[0], b_LTm[1], b_Tt[0], b_Tt[1], b_kbg, b_kd, b_nwT, b_vnew, b_vb, b_intra, b_otm, b_oab, b_oaT,
                 b_gcs, b_sqs_] + b_gate_l + b_vTc_l + b_kTc + b_qTc + list(GDN_BUFS) + hb_km + hb_vm + list(CTX0_BUFS) + list(CTX1_BUFS)
        gs = gb[0:16, 16, 0:8]
        bs = gb[0:16, 16, 16:24]
        gd = sb("gd", [16, 2, 8, 16], F32); b_gd = Buf("gd")
        i16 = identf[0:16, 0:16]
        V(CALL("tensor_tensor", out=gd[:, 0, :, :], in0=i16.unsqueeze(1).to_broadcast([16, 8, 16]),
                                    in1=gs.unsqueeze(2).to_broadcast([16, 8, 16]), op=ALU.mult), r=[b_gb, b_c], w=[b_gd])
        V(CALL("tensor_tensor", out=gd[:, 1, :, :], in0=i16.unsqueeze(1).to_broadcast([16, 8, 16]),
                                    in1=bs.unsqueeze(2).to_broadcast([16, 8, 16]), op=ALU.mult), r=[b_gb, b_c], w=[b_gd])
        pb, pbb, bb = nb()
        T(CALL("matmul", out=pb[:, 0:256], lhsT=onesf[0:16, :], rhs=gd[:, :, :, :].rearrange("t a h u -> t (a h u)"),
                                    start=True, stop=True), r=[b_gd, b_c], w=[bb])
        egb = sb("egb", [128, 2, 8, 16], F32); b_egb = Buf("egb")
        A(CALL("activation", out=egb[:, 0, :, :].rearrange("p h t -> p (h t)"), in_=pb[:, 0:128], func=AF.Exp), r=[bb], w=[b_egb])
        V(CALL("tensor_copy", out=egb[:, 1, :, :].rearrange("p h t -> p (h t)"), in_=pb[:, 128:256]), r=[bb], w=[b_egb])
        qk = arena1[:, 18304:18432].rearrange("p (h t) -> p h t", h=8); b_qk = Buf("qk")
        inherit([b_qk], A1ALL)
        V(CALL("tensor_tensor", out=qk, in0=smp[:, 0:8, :], in1=smp[:, 8:16, :], op=ALU.mult), r=[b_smp], w=[b_qk])
        pb2, _, bb2 = nb()
        T(CALL("matmul", out=pb2[:, 0:128], lhsT=onesf[:], rhs=qk.rearrange("p h t -> p (h t)"), start=True, stop=True),
          r=[b_qk, b_c], w=[bb2])
        V(CALL("tensor_copy", out=qk.rearrange("p h t -> p (h t)"), in_=pb2[:, 0:128]), r=[bb2], w=[b_qk])
        kq2 = arena1[:, 18432:18688].rearrange("p (h t a) -> p h t a", h=8, a=2); b_kq2 = Buf("kq2")
        inherit([b_kq2], A1ALL)
        V(CALL("tensor_copy", out=kq2[:, :, :, 0], in_=smp[:, 8:16, :]), r=[b_smp], w=[b_kq2])
        V(CALL("tensor_copy", out=kq2[:, :, :, 1], in_=smp[:, 0:8, :]), r=[b_smp], w=[b_kq2])
        SS = [arena1[:, 0:4096].rearrange("p (n e) -> p n e", e=128), arena1[:, 4096:8192].rearrange("p (n e) -> p n e", e=128)]
        b_SS = [Buf("SS0"), Buf("SS1")]
        inherit(b_SS, A1ALL)
        ksq = arena1[:, 18688:18944].rearrange("p (h t a) -> p h t a", h=8, a=2); b_ksq = Buf("ksq")
        inherit([b_ksq], A1ALL)
        for qtr in range(4):
            si = qtr % 2
            DQ(CALL("dma_start", out=SS[si].rearrange("p (t h) e -> p t h e", h=8),
                                                    in_=st_gdn.ap()[qtr * 4:(qtr + 1) * 4].rearrange("t h d e -> d t h e")),
               w=[b_SS[si]])
            pb, pbb, bb = nb()
            for tl in range(4):
                t = qtr * 4 + tl
                for h in range(8):
                    n_ = tl * 8 + h
                    T(CALL("matmul", out=pb[:, n_ * 2:n_ * 2 + 2], lhsT=SS[si][:, n_, :], rhs=kq2[:, h, t, :],
                                                                       start=True, stop=True), r=[b_SS[si], b_kq2], w=[bb])
            V(CALL("tensor_copy", out=ksq[:, :, qtr * 4:(qtr + 1) * 4, :],
                                                      in_=pb[:, 0:64].rearrange("p (t h a) -> p h t a", h=8, a=2)), r=[bb], w=[b_ksq])
            if qtr == 3:
                break
        dl = arena1[:, 17920:18048].rearrange("p (h t) -> p h t", h=8); b_dl = Buf("dl")
        osT = arena1[:, 18048:18176].rearrange("p (h t) -> p h t", h=8); b_osT = Buf("osT")
        inherit([b_dl, b_osT], A1ALL)
        V(CALL("tensor_tensor", out=dl, in0=ksq[:, :, :, 0], in1=egb[:, 0, :, :], op=ALU.mult), r=[b_ksq, b_egb], w=[b_dl])
        V(CALL("tensor_tensor", out=dl, in0=smp[:, 16:24, :], in1=dl, op=ALU.subtract), r=[b_smp, b_dl], w=[b_dl])
        V(CALL("tensor_tensor", out=dl, in0=dl, in1=egb[:, 1, :, :], op=ALU.mult), r=[b_dl, b_egb], w=[b_dl])
        V(CALL("tensor_tensor", out=osT, in0=ksq[:, :, :, 1], in1=egb[:, 0, :, :], op=ALU.mult), r=[b_ksq, b_egb], w=[b_osT])
        V(CALL("tensor_tensor", out=qk, in0=qk, in1=dl, op=ALU.mult), r=[b_qk, b_dl], w=[b_qk])
        V(CALL("tensor_tensor", out=osT, in0=osT, in1=qk, op=ALU.add), r=[b_osT, b_qk], w=[b_osT])
        V(CALL("tensor_tensor", out=sqs[:, 0:8, :], in0=osT, in1=osT, op=ALU.mult), r=[b_osT], w=[b_sqs])
        pb, pbb, bb = nb()
        T(CALL("matmul", out=pb[:, 0:128], lhsT=onesb[:], rhs=sqs[:, 0:8, :].rearrange("p h t -> p (h t)"), start=True, stop=True),
          r=[b_sqs, b_c], w=[bb])
        rso = arena1[:, 18176:18304].rearrange("p (h t) -> p h t", h=8); b_rso = Buf("rso")
        inherit([b_rso], A1ALL)
        V(CALL("tensor_scalar", out=rso.rearrange("p h t -> p (h t)"), in0=pb[:, 0:128], scalar1=1.0 / 128, scalar2=EPS,
                                           op0=ALU.mult, op1=ALU.add), r=[bb], w=[b_rso])
        A(CALL("activation", out=rso, in_=rso, func=AF.Ln), r=[b_rso], w=[b_rso])
        A(CALL("activation", out=rso, in_=rso, func=AF.Exp, scale=-0.5), r=[b_rso], w=[b_rso])
        V(CALL("tensor_tensor", out=osT, in0=osT, in1=rso, op=ALU.mult), r=[b_osT, b_rso], w=[b_osT])
        V(CALL("tensor_scalar_mul", out=osT, in0=osT, scalar1=pvec[:, PV_GDNOUT:PV_GDNOUT + 1]), r=[b_osT, b_pv], w=[b_osT])
        gsm = arena1b[0:16, 2 * 17408:2 * 17408 + 1024]; b_gsm = Buf("gsm")
        inherit([b_gsm], A1ALL)
        DQ(CALL("dma_start", out=gsm[:], in_=sc_gate.ap()[1024:1040, :]), r=[b_sc_g[8]], w=[b_gsm])
        pb, pbb, bb = nb()
        for h in range(8):
            T(CALL("transpose", out=pbb[:, h * 16:(h + 1) * 16], in_=gsm[:, h * 128:(h + 1) * 128], identity=identb[0:16, 0:16]),
              r=[b_gsm, b_c], w=[bb])
        osb = sb("osb", [128, 8, 16], BF16); b_osb = Buf("osb")
        V(CALL("tensor_tensor", out=osb[:], in0=osT, in1=pbb[:, 0:128].rearrange("p (h t) -> p h t", h=8), op=ALU.mult),
          r=[b_osT, bb], w=[b_osb])
        b_mixT[(0, 8)] = Buf("mixs0")
        DQ(CALL("dma_start", out=sc_mixT.ap()[0:8, :, 1024:1040].rearrange("h e t -> e h t"), in_=osb[:]), r=[b_osb], w=[b_mixT[(0, 8)]])
        dlb = sb("dlb", [128, 8, 16], BF16); b_dlb = Buf("dlb")
        V(CALL("tensor_copy", out=dlb[:], in_=dl), r=[b_dl], w=[b_dlb])
        ksb = sb("ksb", [128, 8, 16], BF16); b_ksb = Buf("ksb")
        V(CALL("tensor_copy", out=ksb[:], in_=smp[:, 8:16, :]), r=[b_smp], w=[b_ksb])
        ktmS = arena1b[0:16, 2 * 16384:2 * 16384 + 1024].rearrange("p (h d) -> p h d", h=8); b_ktmS = Buf("ktmS")
        dtmS = arena1b[0:16, 2 * 16896:2 * 16896 + 1024].rearrange("p (h d) -> p h d", h=8); b_dtmS = Buf("dtmS")
        inherit([b_ktmS, b_dtmS], A1ALL)
        for (srcT, dstT, bsrc, bdst) in ((ksb, ktmS, b_ksb, b_ktmS), (dlb, dtmS, b_dlb, b_dtmS)):
            pb, pbb, bb = nb()
            for h in range(8):
                T(CALL("transpose", out=pbb[0:16, h * 128:(h + 1) * 128], in_=srcT[:, h, :], identity=identb[:]),
                  r=[bsrc, b_c], w=[bb])
            V(CALL("tensor_copy", out=dstT[:], in_=pbb[0:16, 0:1024].rearrange("p (h d) -> p h d", h=8)), r=[bb], w=[bdst])
        dmask = arena1b[0:16, 2 * 8192:2 * 16384].rearrange("p (t h e) -> p t h e", t=16, h=8); b_dmask = Buf("dmask")
        inherit([b_dmask], A1ALL)
        for t in range(16):
            V(CALL("tensor_scalar_mul", out=dmask[:, t, :, :], in0=dtmS[:], scalar1=identf[0:16, t:t + 1]), r=[b_dtmS, b_c], w=[b_dmask])
        for qtr in range(4):
            si = qtr % 2
            DQ(CALL("dma_start", out=SS[si].rearrange("p (t h) e -> p t h e", h=8),
                                                    in_=st_gdn.ap()[qtr * 4:(qtr + 1) * 4].rearrange("t h d e -> d t h e")),
               w=[b_SS[si]])
            for tl in range(4):
                t = qtr * 4 + tl
                for hh in range(2):
                    pb, pbb, bb = nb()
                    for h4 in range(4):
                        h = hh * 4 + h4
                        T(CALL("matmul", out=pb[:, h4 * 128:(h4 + 1) * 128], lhsT=ktmS[:, h, :], rhs=dmask[:, t, h, :],
                                                                     start=True, stop=True), r=[b_ktmS, b_dmask], w=[bb])
                    sv = SS[si][:, tl * 8 + hh * 4:tl * 8 + hh * 4 + 4, :]
                    V(CALL("tensor_tensor", out=sv, in0=sv, in1=egb[:, 0, hh * 4:hh * 4 + 4, t:t + 1].to_broadcast([128, 4, 128]),
                                                                   op=ALU.mult), r=[b_SS[si], b_egb], w=[b_SS[si]])
                    V(CALL("tensor_tensor", out=sv, in0=sv, in1=pb[:, :].rearrange("p (h e) -> p h e", h=4), op=ALU.add),
                      r=[b_SS[si], bb], w=[b_SS[si]])
            DQ(CALL("dma_start", out=o_gdn_s.ap()[qtr * 4:(qtr + 1) * 4].rearrange("t h d e -> d t h e"),
                                                    in_=SS[si].rearrange("p (t h) e -> p t h e", h=8)), r=[b_SS[si]], stream="out")

        ffa = arena1[:, 0:9 * 2048].rearrange("p (i c) -> p i c", i=9)
        b_ffa = [Buf("ffa%d" % i) for i in range(9)]
        inherit(b_ffa, b_SS + A1ALL + [b_gsm, b_ktmS, b_dtmS, b_dmask, b_qk, b_kq2, b_ksq, b_dl, b_osT, b_rso])
        mxa = arena2[:, :].rearrange("p (k t) -> p k t", k=16)
        b_mxa = Buf("mxa")
        inherit([b_mxa], b_poolT + [b_stc, b_stT, b_spc, b_spT] + A2HI)
        DQ(CALL("dma_start", out=mxa, in_=sc_mixT.ap().rearrange("k e t -> e k t")), r=list(b_mixT.values()), w=[b_mxa])
        nrm = xt[1]; b_nrm = b_xt[1]
        DQ(CALL("dma_start", out=nrm[:], in_=nrm_post.ap()[0]), w=[b_nrm])

        def w_unit(src, c0):
            def f(t_):
                return t_[:, :].rearrange("p (k n) -> p k n", k=16)
            f.src = src.ap()[:, c0:c0 + 256].rearrange("(k p) n -> p k n", p=128)
            return f

        for n8 in range(8):
            wt, bw = load_w(w_unit(w_out, n8 * 256))
            for i in range(9):
                rows = 128 if i < 8 else TS
                pb, pbb, bb = nb()
                for kc in range(16):
                    T(CALL("matmul",
                        out=pb[0:rows, 0:256], lhsT=mxa[:, kc, i * 128:i * 128 + rows], rhs=wt[:, kc * 256:(kc + 1) * 256],
                        start=(kc == 0), stop=(kc == 15)), r=[bw, b_mxa], w=[bb])
                A(CALL("activation", out=ffa[0:rows, i, n8 * 256:(n8 + 1) * 256], in_=pb[0:rows, 0:256], func=AF.Copy),
                  r=[bb], w=[b_ffa[i]])

        xt2 = arena2[:, 0:4096].bitcast(F32); b_xt2 = Buf("xt2")
        RES = [(xt[0], b_xt[0]), (xt2, b_xt2)]
        b_y = [Buf("y%d" % i) for i in range(9)]

        def pnr_chain(i, si, phase):
            rows = 128 if i < 8 else TS
            xb_, bxb, stc_, bst = NT_SETS[si]
            rt, brt = RES[si]
            fv = ffa[0:rows, i, :]
            if phase == 4:
                DQ(CALL("dma_start", out=rt[0:rows, :], in_=xall.ap()[TP + i * 128:TP + i * 128 + rows, :]), w=[brt])
            else:
                DQ(CALL("dma_start", out=rt[0:rows, :], in_=y_main.ap()[i * 128:i * 128 + rows, :]), r=[b_y[i]], w=[brt])
            A(CALL("activation", out=xb_[0:rows, :], in_=fv, func=AF.Square, accum_out=stc_[0:rows, 0:1]), r=[b_ffa[i]], w=[bxb, bst])
            yield
            rstd_from_ss(stc_[0:rows, 0:1], rows, 1.0 / D, [bst], [bst])
            yield
            V(CALL("scalar_tensor_tensor", out=fv, in0=fv, scalar=stc_[0:rows, 0:1], in1=nrm[0:rows, :], op0=ALU.mult, op1=ALU.mult),
              r=[b_ffa[i], bst, b_nrm], w=[b_ffa[i]])
            yield
            V(CALL("tensor_tensor", out=fv, in0=fv, in1=rt[0:rows, :], op=ALU.add), r=[b_ffa[i], brt], w=[b_ffa[i]])
            yield
            if phase == 4:
                DQ(CALL("dma_start", out=y_main.ap()[i * 128:i * 128 + rows, :], in_=fv), r=[b_ffa[i]], w=[b_y[i]])
                for _ in norm_transpose(fv, [b_ffa[i]], rows, i * 128, i, PV_PREMLP, si):
                    yield
            else:
                DQ(CALL("dma_start", out=y_main.ap()[i * 128:i * 128 + rows, :], in_=fv), r=[b_ffa[i]], w=[b_y[i]], stream="out")

        inherit([b_xt2, b_xb2], [b_mxa])
        for i2 in range(5):
            interleave([pnr_chain(i, si, 4) for si, i in enumerate((2 * i2, 2 * i2 + 1)) if i < 9], lag=3)

        upT = [arena2[:, 0:8 * NM].rearrange("p (c t) -> p c t", c=8), arena2[:, 8 * NM:16 * NM].rearrange("p (c t) -> p c t", c=8)]
        b_upT = [Buf("upT0"), Buf("upT1")]
        inherit(b_upT, [b_mxa, b_xt2, b_xb2])
        DQ(CALL("dma_start", out=nrm[:], in_=nrm_post.ap()[1]), w=[b_nrm])
        blocks5 = [(0, 512), (512, 512), (1024, 16)]
        for G in range(8):
            ui = G % 2
            for u4 in range(4):
                wt, bw = load_w(w_unit(w_up, G * 1024 + u4 * 256))
                for j in range(2):
                    fc = u4 * 2 + j
                    for (c0, n) in blocks5:
                        pb, pbb, bb = nb()
                        tiles = list(range(c0 // 128, (c0 + n - 1) // 128 + 1))
                        for kc in range(16):
                            T(CALL("matmul",
                                out=pb[:, 0:n], lhsT=wt[:, kc * 256 + j * 128:kc * 256 + (j + 1) * 128], rhs=hT[:, kc, c0:c0 + n],
                                start=(kc == 0), stop=(kc == 15)), r=[bw] + [b_hT[t_] for t_ in tiles], w=[bb])
                        A(CALL("activation", out=upT[ui][:, fc, c0:c0 + n], in_=pb[:, 0:n], func=AF.Relu),
                          r=[bb], w=[b_upT[ui]])
                        V(CALL("tensor_tensor", out=upT[ui][:, fc, c0:c0 + n], in0=upT[ui][:, fc, c0:c0 + n],
                                                                             in1=upT[ui][:, fc, c0:c0 + n], op=ALU.mult),
                          r=[b_upT[ui]], w=[b_upT[ui]])
            for n8 in range(8):
                i_ = wrr[0]
                wrr[0] = (i_ + 1) % 2
                wt = wbuf[i_]
                bw = b_wbuf[i_]
                GQ(CALL("dma_start",
                    out=wt[:, 0:2048].rearrange("p (k n) -> p k n", k=8),
                    in_=w_down.ap()[G * 1024:(G + 1) * 1024, n8 * 256:(n8 + 1) * 256].rearrange("(k p) n -> p k n", p=128)), w=[bw])
                for i in range(9):
                    rows = 128 if i < 8 else TS
                    pb, pbb, bb = nb()
                    for fc in range(8):
                        T(CALL("matmul",
                            out=pb[0:rows, 0:256], lhsT=upT[ui][:, fc, i * 128:i * 128 + rows], rhs=wt[:, fc * 256:(fc + 1) * 256],
                            start=(fc == 0), stop=(fc == 7)), r=[bw, b_upT[ui]], w=[bb])
                    dv = ffa[0:rows, i, n8 * 256:(n8 + 1) * 256]
                    if G == 0:
                        V(CALL("tensor_copy", out=dv, in_=pb[0:rows, 0:256]), r=[bb], w=[b_ffa[i]])
                    else:
                        V(CALL("tensor_tensor", out=dv, in0=dv, in1=pb[0:rows, 0:256], op=ALU.add),
                          r=[bb, b_ffa[i]], w=[b_ffa[i]])
        inherit([b_xt2, b_xb2], b_upT)
        for i2 in range(5):
            interleave([pnr_chain(i, si, 5) for si, i in enumerate((2 * i2, 2 * i2 + 1)) if i < 9], lag=3)
        S.emit(st, final_streams=["out"])
    return nc


def _prep(inputs):
    f = lambda k: np.ascontiguousarray(np.asarray(inputs[k], dtype=np.float32))
    xp = f("x_prompt"); xs = f("x_sample")
    conv_w = f("conv_w")[0]
    pvec = np.zeros((128, PV), np.float32)
    pvec[:, PV_CONV:PV_CONV + 96] = conv_w.reshape(4, 24, 128).transpose(2, 1, 0).reshape(128, 96)
    pvec[:, PV_PREMIX:PV_PREMIX + 16] = f("norm_pre_mix")[0].reshape(16, 128).T
    pvec[:, PV_PREMLP:PV_PREMLP + 16] = f("norm_pre_mlp")[0].reshape(16, 128).T
    pvec[:, PV_PSCALE:PV_PSCALE + 8] = f("pool_scale")[0].reshape(8, 128).T
    pvec[:, PV_GDNOUT] = f("norm_gdn_out")[0]
    nrm_post = np.stack([np.broadcast_to(f("norm_post_mix")[0], (128, D)), np.broadcast_to(f("norm_post_mlp")[0], (128, D))]).copy()
    shared = dict(w_in=f("w_in")[0], w_pool=f("w_pool")[0], w_out=f("w_out")[0], w_up=f("w_up")[0], w_down=f("w_down")[0],
                  pvec=pvec, nrm_post=nrm_post)
    maps = []
    for c in range(8):
        b, s = c // 2, c % 2
        xall = np.zeros((NT, D), np.float32)
        if s == 1:
            xall[0:TP] = xp[b, 0:TP]
        xall[TP:2 * TP] = xp[b, s * TP:(s + 1) * TP]
        xall[2 * TP:] = xs[c * TS:(c + 1) * TS, 0]
        rows = np.zeros((RB,), np.float32)
        rows[RB_GDNOUT:RB_GDNOUT + 128] = f("norm_gdn_out")[0]
        rows[RB_ALOG:RB_ALOG + 8] = f("a_log")[0]
        rows[RB_DT:RB_DT + 8] = f("dt_bias")[0]
        for g in range(4):
            pos = s * TP + np.arange(16)
            rows[RB_INVC + g * 16:RB_INVC + (g + 1) * 16] = 1.0 / np.minimum(pos + 1, 2 << g)
        m = dict(shared)
        m.update(xall=xall, rows_bc=np.broadcast_to(rows, (128, RB)).copy(),
                 st_gdn=f("state_gdn")[0, c * TS:(c + 1) * TS], st_conv=f("state_conv")[0, c * TS:(c + 1) * TS].reshape(TS * 3, 3072),
                 st_pool=f("state_pool")[0, c * TS:(c + 1) * TS].reshape(TS * 15, 1024))
        maps.append(m)
    return maps


_NC = [None]


def kernel(**inputs):
    maps = _prep(inputs)
    if _NC[0] is None:
        _NC[0] = build_program()
    res = run_bass_kernel_spmd(_NC[0], maps, core_ids=list(range(8))).results
    yp = np.zeros((4, 2048, D), np.float32)
    ys = np.zeros((128, 1, D), np.float32)
    gp = np.zeros((1, 4, 8, 128, 128), np.float32)
    cp = np.zeros((1, 4, 3, 3072), np.float32)
    pp = np.zeros((1, 4, 15, 1024), np.float32)
    gs = np.zeros((1, 128, 8, 128, 128), np.float32)
    cs = np.zeros((1, 128, 3, 3072), np.float32)
    ps = np.zeros((1, 128, 15, 1024), np.float32)
    for c in range(8):
        b, s = c // 2, c % 2
        r = res[c]
        yp[b, s * TP:(s + 1) * TP] = r["y_main"][0:TP]
        ys[c * TS:(c + 1) * TS, 0] = r["y_main"][TP:]
        if s == 1:
            gp[0, b] = r["o_gdn_p"]
            cp[0, b] = r["o_conv_p"]
            pp[0, b] = r["o_pool_p"]
        gs[0, c * TS:(c + 1) * TS] = r["o_gdn_s"]
        cs[0, c * TS:(c + 1) * TS] = r["o_conv_s"]
        ps[0, c * TS:(c + 1) * TS] = r["o_pool_s"]
    return (yp, ys, gp, cp, pp, gs, cs, ps)
```
